# Optimizing a Trainium2 kernel written in Bass

```python
import jax, jax.numpy as jnp
from jax import lax
import numpy as np

D_MODEL = 1024
BATCH = 8
SEQ = 4096
DEPTH = 1

N_HEADS = 16
HEAD_DIM = 64
ATTN_WIDTH = N_HEADS * HEAD_DIM
CONV_WIDTH = D_MODEL
CONV_K = 31
D_FF = 4 * D_MODEL
Q_BLOCK = 128
N_ADA = 6
NORM_EPS = 1e-6

IN_COLS = (ATTN_WIDTH, ATTN_WIDTH, ATTN_WIDTH, N_HEADS, 2 * CONV_WIDTH, 2 * D_MODEL)
IN_SPLITS = tuple(int(s) for s in np.cumsum(IN_COLS)[:-1])
D_IN = int(sum(IN_COLS))

kernel_name = "hybrid_fox_conformer_gated_block"


def rms_norm(x, g):
    xf = x.astype(jnp.float32)
    y = xf * lax.rsqrt(jnp.mean(xf * xf, axis=-1, keepdims=True) + NORM_EPS)
    return (y * g.astype(jnp.float32)).astype(x.dtype)


def layer_norm(x, g, b):
    xf = x.astype(jnp.float32)
    mu = jnp.mean(xf, axis=-1, keepdims=True)
    var = jnp.mean(jnp.square(xf - mu), axis=-1, keepdims=True)
    y = (xf - mu) * lax.rsqrt(var + NORM_EPS)
    return (y * g.astype(jnp.float32) + b.astype(jnp.float32)).astype(x.dtype)


def forgetting_attention(q, k, v, log_f):
    B, S, H, Dh = q.shape
    nb = S // Q_BLOCK
    scale = Dh ** -0.5
    f_cum = jnp.cumsum(log_f, axis=1)
    f_key = jnp.transpose(f_cum, (0, 2, 1))
    q_blocks = jnp.transpose(q.reshape(B, nb, Q_BLOCK, H, Dh), (1, 0, 2, 3, 4))
    f_blocks = jnp.transpose(f_cum.reshape(B, nb, Q_BLOCK, H), (1, 0, 3, 2))
    k_pos = jnp.arange(S)

    def one_block(args):
        q_i, f_i, i = args
        s = jnp.einsum('bqhd,bkhd->bhqk', q_i, k).astype(jnp.float32) * scale
        s = s + f_i[..., :, None] - f_key[:, :, None, :]
        q_pos = i * Q_BLOCK + jnp.arange(Q_BLOCK)
        causal = k_pos[None, :] <= q_pos[:, None]
        s = jnp.where(causal[None, None], s, -jnp.inf)
        p = jax.nn.softmax(s, axis=-1)
        return jnp.einsum('bhqk,bkhd->bqhd', p.astype(v.dtype), v)

    out = lax.map(one_block, (q_blocks, f_blocks, jnp.arange(nb)))
    return jnp.transpose(out, (1, 0, 2, 3, 4)).reshape(B, S, H * Dh)


def causal_depthwise_conv(u, w, b):
    K, C = w.shape
    u_pad = jnp.pad(u, ((0, 0), (K - 1, 0), (0, 0)))
    y = lax.conv_general_dilated(
        u_pad, w[:, None, :].astype(u.dtype), window_strides=(1,), padding='VALID',
        dimension_numbers=('NWC', 'WIO', 'NWC'), feature_group_count=C)
    return y + b.astype(u.dtype)


def setup_inputs(seed: int = 0) -> dict:
    key = jax.random.key(seed)
    ks = jax.random.split(key, 20)
    n = jax.random.normal
    D, L = D_MODEL, DEPTH
    return {
        "x": n(ks[0], (BATCH, SEQ, D), jnp.float32),
        "c": n(ks[1], (BATCH, D), jnp.float32),
        "w_ada": n(ks[2], (L, D, N_ADA * D), jnp.float32) * (0.5 * D ** -0.5),
        "b_ada": n(ks[3], (L, N_ADA * D), jnp.float32) * 0.02,
        "norm1_g": 1.0 + 0.1 * n(ks[4], (L, D), jnp.float32),
        "w_in": n(ks[5], (L, D, D_IN), jnp.float32) * D ** -0.5,
        "b_forget": 3.0 + 0.5 * n(ks[6], (L, N_HEADS), jnp.float32),
        "q_norm_g": 1.0 + 0.1 * n(ks[7], (L, HEAD_DIM), jnp.float32),
        "k_norm_g": 1.0 + 0.1 * n(ks[8], (L, HEAD_DIM), jnp.float32),
        "w_attn_proj": n(ks[9], (L, ATTN_WIDTH, D), jnp.float32) * ATTN_WIDTH ** -0.5,
        "conv_w": n(ks[10], (L, CONV_K, CONV_WIDTH), jnp.float32) * CONV_K ** -0.5,
        "conv_b": 0.02 * n(ks[11], (L, CONV_WIDTH), jnp.float32),
        "conv_ln_g": 1.0 + 0.1 * n(ks[12], (L, CONV_WIDTH), jnp.float32),
        "conv_ln_b": 0.02 * n(ks[13], (L, CONV_WIDTH), jnp.float32),
        "w_conv_proj": n(ks[14], (L, CONV_WIDTH, D), jnp.float32) * CONV_WIDTH ** -0.5,
        "w_out": n(ks[15], (L, D, D), jnp.float32) * D ** -0.5,
        "norm2_g": 1.0 + 0.1 * n(ks[16], (L, D), jnp.float32),
        "w_mlp1": n(ks[17], (L, D, D_FF), jnp.float32) * D ** -0.5,
        "w_mlp2": n(ks[18], (L, D_FF, D), jnp.float32) * D_FF ** -0.5,
    }


def reference(x, c, w_ada, b_ada, norm1_g, w_in, b_forget, q_norm_g, k_norm_g,
              w_attn_proj, conv_w, conv_b, conv_ln_g, conv_ln_b, w_conv_proj,
              w_out, norm2_g, w_mlp1, w_mlp2):
    B, S, D = x.shape
    c_act = jax.nn.silu(c)
    for l in range(DEPTH):
        mod = c_act @ w_ada[l] + b_ada[l]
        sh1, sc1, g1, sh2, sc2, g2 = [m[:, None, :] for m in jnp.split(mod, N_ADA, axis=-1)]

        h = rms_norm(x, norm1_g[l]) * (1.0 + sc1) + sh1
        proj = h @ w_in[l]
        q, k, v, f_logit, glu_in, gate_logit = jnp.split(proj, IN_SPLITS, axis=-1)

        q = rms_norm(q.reshape(B, S, N_HEADS, HEAD_DIM), q_norm_g[l])
        k = rms_norm(k.reshape(B, S, N_HEADS, HEAD_DIM), k_norm_g[l])
        v = v.reshape(B, S, N_HEADS, HEAD_DIM)
        log_f = jax.nn.log_sigmoid(f_logit.astype(jnp.float32) + b_forget[l].astype(jnp.float32))
        branch_a = forgetting_attention(q, k, v, log_f) @ w_attn_proj[l]

        u = glu_in[..., :CONV_WIDTH] * jax.nn.sigmoid(glu_in[..., CONV_WIDTH:])
        u = causal_depthwise_conv(u, conv_w[l], conv_b[l])
        u = jax.nn.silu(layer_norm(u, conv_ln_g[l], conv_ln_b[l]))
        branch_b = u @ w_conv_proj[l]

        gate_a, gate_b = jnp.split(gate_logit, 2, axis=-1)
        merged = jax.nn.sigmoid(gate_a) * branch_a + jax.nn.sigmoid(gate_b) * branch_b
        x = x + g1 * (merged @ w_out[l])

        h2 = rms_norm(x, norm2_g[l]) * (1.0 + sc2) + sh2
        x = x + g2 * (jnp.square(jax.nn.relu(h2 @ w_mlp1[l])) @ w_mlp2[l])
    return x
```

```python
import os
from contextlib import ExitStack
import numpy as np
import concourse.bass as bass
import concourse.mybir as mybir
from concourse.bass_utils import run_bass_kernel_spmd

F32 = mybir.dt.float32
BF16 = mybir.dt.bfloat16
AF = mybir.ActivationFunctionType
ALU = mybir.AluOpType
AX = mybir.AxisListType

S = 4096
D = 1024
H = 16
DH = 64
KC = 8
DFF = 4096
FC = 32
CK = 31
EPS = 1e-6
DIN = 7184
OFF_Q, OFF_K, OFF_V, OFF_F, OFF_GLU, OFF_GATE = 0, 1024, 2048, 3072, 3088, 5136

ENGS = ("pe", "act", "dve", "pool", "sp")
RAW_ONLY = False


class Buf:
    __slots__ = ("name", "w", "rs", "dw")
    excl = False

    def __init__(self, name=""):
        self.name = name
        self.w = None
        self.rs = []
        self.dw = []


class PBuf(Buf):
    __slots__ = ()
    excl = True


class Op:
    __slots__ = ("eng", "fn", "deps", "sig", "val", "idx", "dma", "dsem", "dval")

    def __init__(self, eng, fn, dma):
        self.eng = eng
        self.fn = fn
        self.dma = dma
        self.deps = ([], [])
        self.sig = False
        self.val = 0
        self.dsem = None
        self.dval = 0


class DSem:
    __slots__ = ("name", "issued", "h", "bg")

    def __init__(self, name, bg=False):
        self.name = name
        self.issued = 0
        self.h = None
        self.bg = bg


class Prog:
    def __init__(self, same_engine_sync=True):
        self.ops = {e: [] for e in ENGS}
        self.dsems = []
        self.same_engine_sync = same_engine_sync
        self.final_waits = []

    def dsem(self, name, bg=False):
        s = DSem(name, bg)
        self.dsems.append(s)
        return s

    def op(self, eng, fn, reads=(), writes=(), dsem=None, dwrites=()):
        o = Op(eng, fn, dsem is not None)
        o.idx = len(self.ops[eng])
        deps = []
        raw = set()
        for b in reads:
            if b.w is not None:
                deps.append(b.w)
                raw.add(id(b.w))
            for d in b.dw:
                deps.append(d)
                raw.add(id(d))
            if b.excl:
                deps.extend(r for r in b.rs if r.eng != eng)
        for b in writes:
            if b.w is not None:
                deps.append(b.w)
            deps.extend(b.dw)
            deps.extend(b.rs)
        for b in dwrites:
            if b.w is not None:
                deps.append(b.w)
            deps.extend(b.rs)
        dd = {}
        cd = {}
        for d in deps:
            if d.dma:
                dd[id(d.dsem)] = (d.dsem, d.dsem.issued)
            else:
                if d.eng == eng and not o.dma:
                    if eng == "pe" or not self.same_engine_sync or (RAW_ONLY and id(d) not in raw):
                        continue
                p = cd.get(d.eng)
                if p is None or d.idx > p.idx:
                    cd[d.eng] = d
        o.deps = (list(cd.values()), list(dd.values()))
        for d in cd.values():
            d.sig = True
        if o.dma:
            dsem.issued += 16
            o.dsem = dsem
            o.dval = dsem.issued
        for b in reads:
            if o.dma:
                b.rs = [r for r in b.rs if not (r.dma and r.dsem is o.dsem)]
            else:
                b.rs = [r for r in b.rs if r.dma or r.eng != eng]
            b.rs.append(o)
        for b in writes:
            b.w = o
            b.rs = []
            b.dw = []
        for b in dwrites:
            b.dw = [r for r in b.dw if r.dma or r.eng != eng]
            b.dw.append(o)
        self.ops[eng].append(o)
        return o

    def barrier(self, exclude=()):
        lasts = []
        for e in ENGS:
            for o in reversed(self.ops[e]):
                if not o.dma and o.fn is not None:
                    lasts.append(o)
                    o.sig = True
                    break
        dds = [(s, s.issued) for s in self.dsems if s.issued > 0 and s not in exclude and not s.bg]
        for e in ENGS:
            o = Op(e, None, False)
            o.idx = len(self.ops[e])
            o.deps = ([d for d in lasts if d.eng != e], list(dds))
            self.ops[e].append(o)

    def emit(self, nc, stack):
        esem = {e: stack.enter_context(nc.semaphore("s_" + e)) for e in ENGS}
        for s in self.dsems:
            s.h = stack.enter_context(nc.semaphore("d_" + s.name))
        for e in ENGS:
            c = 0
            for o in self.ops[e]:
                if o.dma or o.fn is None:
                    continue
                if o.sig:
                    c += 1
                    o.val = c
        block = stack.enter_context(nc.Block())
        secs = {"pe": block.tensor, "act": block.scalar, "dve": block.vector,
                "pool": block.gpsimd, "sp": block.sync}
        for e in ENGS:
            ops = self.ops[e]
            final = self.final_waits if e == "sp" else []

            def section(eng, ops=ops, e=e, final=final):
                known = {}
                for o in ops:
                    cds, dds = o.deps
                    for d in cds:
                        key = ("e", d.eng)
                        if known.get(key, 0) >= d.val:
                            continue
                        known[key] = d.val
                        eng.wait_ge(esem[d.eng], d.val)
                    for (s, v) in dds:
                        key = ("d", id(s))
                        if known.get(key, 0) >= v:
                            continue
                        known[key] = v
                        eng.wait_ge(s.h, v)
                    if o.fn is None:
                        continue
                    ins = o.fn(eng)
                    if o.dma:
                        ins.then_inc(o.dsem.h, 16)
                    elif o.sig:
                        ins.then_inc(esem[e], 1)
                for s in final:
                    eng.wait_ge(s.h, s.issued)

            secs[e](section)


class Ring:
    def __init__(self, items):
        self.items = items
        self.i = 0

    def next(self):
        it = self.items[self.i % len(self.items)]
        self.i += 1
        return it


def build(upto=5, debug=False):
    nc = bass.Bass("TRN2", target_bir_lowering=False)
    P = Prog()

    def din(name, shape, dt=F32):
        return nc.dram_tensor(name, shape, dt, kind="ExternalInput").ap()

    x_d = din("x", [S, D])
    ccol_d = din("ccol", [128, KC])
    wada_d = din("w_ada", [D, 6 * D])
    bada_d = din("b_ada", [1, 6 * D])
    n1g_d = din("n1g", [128, KC])
    n2g_d = din("n2g", [128, KC])
    win_d = din("w_in", [D, DIN])
    bfor_d = din("bfor", [H, 1])
    qg_d = din("qg", [DH, 1])
    kg_d = din("kg", [DH, 1])
    wap_d = din("w_attn_proj", [D, D])
    cwT_d = din("cwT", [128, KC, CK])
    cb_d = din("cb", [128, KC])
    lng_d = din("lng", [128, KC])
    lnb_d = din("lnb", [128, KC])
    wcp_d = din("w_conv_proj", [D, D])
    wout_d = din("w_out", [D, D])
    w1_d = din("w_mlp1", [D, DFF])
    w2_d = din("w_mlp2", [DFF, D])
    ident_d = din("ident", [128, 128])
    mask_d = din("maskb", [128, 128])

    out_d = nc.dram_tensor("out", [S, D], F32, kind="ExternalOutput").ap()
    skind = "ExternalOutput" if debug else "Internal"
    modscr = nc.dram_tensor("modscr", [1, 6 * D], F32, kind=skind).ap()
    fscr = nc.dram_tensor("fscr", [6, H, S], BF16, kind=skind).ap()
    bgscr = nc.dram_tensor("bgscr", [128, KC, S], BF16, kind=skind).ap()
    mscr = nc.dram_tensor("mscr", [128, KC, S], BF16, kind=skind).ap()
    w1s = nc.dram_tensor("w1s", [D, DFF], BF16, kind="Internal").ap()
    w2s = nc.dram_tensor("w2s", [DFF, D], BF16, kind="Internal").ap()
    wos = nc.dram_tensor("wos", [D, D], BF16, kind="Internal").ap()
    wps = nc.dram_tensor("wps", [D, D], BF16, kind="Internal").ap()
    wgs = nc.dram_tensor("wgs", [D, D], BF16, kind="Internal").ap()
    b_wscr = Buf()
    b_modscr, b_fscr = Buf(), Buf()
    b_bg = [Buf() for _ in range(8)]
    b_ms = [Buf() for _ in range(8)]
    if debug:
        dbg_hT = nc.dram_tensor("dbg_hT", [128, KC, S], BF16, kind="ExternalOutput").ap()
        dbg_ao = nc.dram_tensor("dbg_ao", [128, KC, S], BF16, kind="ExternalOutput").ap()

    d_out = P.dsem("out")
    P.final_waits.append(d_out)
    d_misc = P.dsem("misc")

    with ExitStack() as top:
        def sb(st, name, shape, dt):
            return st.enter_context(nc.sbuf_tensor("s_" + name, shape, dt))

        def psum(st, name):
            return st.enter_context(nc.psum_tensor("p_" + name, [128, 512], F32))

        def DMA(q, out, in_, reads, writes, dsem):
            return P.op(q, lambda e: e.dma_start(out=out, in_=in_), reads=reads, writes=writes, dsem=dsem)

        def MM(out, lhsT, rhs, start, stop, reads, writes, skip=False):
            return P.op("pe", lambda e: e.matmul(out, lhsT=lhsT, rhs=rhs, start=start, stop=stop,
                                                 skip_group_check=skip), reads=reads, writes=writes)

        def TR(out, in_, ident, reads, writes):
            return P.op("pe", lambda e: e.transpose(out=out, in_=in_, identity=ident), reads=reads, writes=writes)

        def ACT(out, in_, func, reads, writes, bias=None, scale=None, accum=None, dwrites=()):
            def fn(e):
                kw = {}
                if bias is not None:
                    kw["bias"] = bias
                if scale is not None:
                    kw["scale"] = scale
                if accum is not None:
                    kw["accum_out"] = accum
                return e.activation(out=out, in_=in_, func=func, **kw)
            return P.op("act", fn, reads=reads, writes=writes, dwrites=dwrites)

        def TS(eng, out, in0, s1, s2, op0, op1, reads, writes, dwrites=()):
            def fn(e):
                if op1 is None:
                    return e.tensor_scalar(out=out, in0=in0, scalar1=s1, scalar2=None, op0=op0)
                return e.tensor_scalar(out=out, in0=in0, scalar1=s1, scalar2=s2, op0=op0, op1=op1)
            return P.op(eng, fn, reads=reads, writes=writes, dwrites=dwrites)

        def TT(eng, out, in0, in1, op, reads, writes):
            return P.op(eng, lambda e: e.tensor_tensor(out=out, in0=in0, in1=in1, op=op), reads=reads, writes=writes)

        def STT(out, in0, scalar, in1, op0, op1, reads, writes):
            return P.op("dve", lambda e: e.scalar_tensor_tensor(out=out, in0=in0, scalar=scalar, in1=in1, op0=op0, op1=op1),
                        reads=reads, writes=writes)

        def CP(eng, out, in_, reads, writes):
            return P.op(eng, lambda e: e.tensor_copy(out=out, in_=in_), reads=reads, writes=writes)

        def MSET(eng, ap, val, writes):
            return P.op(eng, lambda e: e.memset(ap, val), writes=writes)

        ident = sb(top, "ident", [128, 128], F32); b_ident = Buf()
        identb = sb(top, "identb", [128, 128], BF16); b_identb = Buf()
        maskb = sb(top, "maskb_s", [128, 128], BF16); b_maskb = Buf()
        onesf = sb(top, "onesf", [128, 512], F32); b_onesf = Buf()
        modcol = sb(top, "modcol", [128, 48], F32); b_modcol = Buf()
        gs1 = sb(top, "gs1", [128, KC], F32); b_gs1 = Buf()
        gs2 = sb(top, "gs2", [128, KC], F32); b_gs2 = Buf()
        cbc = sb(top, "cbc", [128, KC], F32); b_cbc = Buf()
        lngc = sb(top, "lngc", [128, KC], F32); b_lngc = Buf()
        lnbc = sb(top, "lnbc", [128, KC], F32); b_lnbc = Buf()
        gqk = sb(top, "gqk", [DH, 1], F32); b_gqk = Buf()
        epsc = sb(top, "epsc", [128, 1], F32); b_epsc = Buf()
        onec = sb(top, "onec", [128, 1], F32); b_onec = Buf()

        DMA("sp", ident[:], ident_d, [], [b_ident], d_misc)
        DMA("pool", identb[:], ident_d, [], [b_identb], d_misc)
        DMA("pool", maskb[:], mask_d, [], [b_maskb], d_misc)
        DMA("sp", cbc[:], cb_d, [], [b_cbc], d_misc)
        DMA("sp", lngc[:], lng_d, [], [b_lngc], d_misc)
        DMA("sp", lnbc[:], lnb_d, [], [b_lnbc], d_misc)
        MSET("dve", onesf[:], 1.0, [b_onesf])
        MSET("dve", epsc[:], EPS, [b_epsc])
        MSET("dve", onec[:], 1.0, [b_onec])

        stA = top.enter_context(ExitStack())
        hT = sb(stA, "hT", [128, KC, S], BF16)
        b_hT = [Buf() for _ in range(8)]

        with ExitStack() as st:
            pm = psum(st, "pm"); b_pm = PBuf()
            pc = psum(st, "pc"); b_pc = PBuf()
            ptr = [(psum(st, "ptrA%d" % i), PBuf(), psum(st, "ptrB%d" % i), PBuf()) for i in range(2)]
            ccol = sb(st, "ccol", [128, KC], F32); b_ccol = Buf()
            cact = sb(st, "cact", [128, KC], F32); b_cact = Buf()
            badar = sb(st, "badar", [1, 6 * D], F32); b_badar = Buf()
            modrow = sb(st, "modrow", [1, 6 * D], F32); b_modrow = Buf()
            n1gc = sb(st, "n1gc", [128, KC], F32); b_n1gc = Buf()
            n2gc = sb(st, "n2gc", [128, KC], F32); b_n2gc = Buf()
            qgc = sb(st, "qgc", [DH, 1], F32); b_qgc = Buf()
            kgc = sb(st, "kgc", [DH, 1], F32); b_kgc = Buf()
            wst = Ring([(sb(st, "wst%d" % i, [128, KC, 512], F32), Buf(), P.dsem("wst%d" % i)) for i in range(2)])
            xt = Ring([(sb(st, "xt%d" % i, [128, D], F32), Buf(), P.dsem("xt%d" % i)) for i in range(4)])
            xn = Ring([(sb(st, "xn%d" % i, [128, D], F32), Buf()) for i in range(3)])
            junk = sb(st, "junk", [128, D], BF16); b_junk = Buf()
            ss = Ring([(sb(st, "ss%d" % i, [128, 1], F32), Buf()) for i in range(4)])
            sd = Ring([(sb(st, "sd%d" % i, [128, 1], F32), Buf()) for i in range(4)])
            rs = Ring([(sb(st, "rs%d" % i, [128, 1], F32), Buf()) for i in range(4)])

            DMA("sp", ccol[:], ccol_d, [], [b_ccol], d_misc)
            DMA("sp", badar[:], bada_d, [], [b_badar], d_misc)
            DMA("sp", n1gc[:], n1g_d, [], [b_n1gc], d_misc)
            DMA("sp", n2gc[:], n2g_d, [], [b_n2gc], d_misc)
            DMA("sp", qgc[:], qg_d, [], [b_qgc], d_misc)
            DMA("sp", kgc[:], kg_d, [], [b_kgc], d_misc)
            ACT(cact[:], ccol[:], AF.Silu, [b_ccol], [b_cact])
            STT(gqk[:], qgc[:], 0.125, kgc[:], ALU.mult, ALU.mult, [b_qgc, b_kgc], [b_gqk])

            wada_v = wada_d.rearrange("(kc p) n -> p kc n", p=128)

            def mod_tile(n):
                w_t, w_b, w_s = wst.next()
                DMA("sp", w_t[:], wada_v[:, :, n * 512:(n + 1) * 512], [], [w_b], w_s)
                for kc in range(KC):
                    MM(pm[0:1, :], cact[:, kc:kc + 1], w_t[:, kc, :], kc == 0, kc == KC - 1, [b_cact, w_b], [b_pm])
                TT("dve", modrow[0:1, n * 512:(n + 1) * 512], pm[0:1, :], badar[0:1, n * 512:(n + 1) * 512], ALU.add,
                   [b_pm, b_badar], [b_modrow])

            def mod_cols(j0, j1):
                for j in range(j0, j1):
                    MM(pc[:, j:j + 1], modrow[0:1, j * 128:(j + 1) * 128], onec[0:1, 0:1], True, True,
                       [b_modrow, b_onec], [b_pc], skip=True)
                CP("dve", modcol[:, j0:j1], pc[:, j0:j1], [b_pc], [b_modcol])

            for n in range(4):
                mod_tile(n)
            mod_cols(0, 16)
            STT(gs1[:], modcol[:, 8:16], 1.0, n1gc[:], ALU.add, ALU.mult, [b_modcol, b_n1gc], [b_gs1])

            st1 = {}

            def stage1(i):
                x_t, x_b, x_s = xt.next()
                DMA("sp", x_t[:], x_d[i * 128:(i + 1) * 128, :], [], [x_b], x_s)
                ss_t, ss_b = ss.next()
                sd_t, sd_b = sd.next()
                rs_t, rs_b = rs.next()
                ACT(junk[:], x_t[:], AF.Square, [x_b], [b_junk, ss_b], accum=ss_t[:])
                ACT(sd_t[:], ss_t[:], AF.Sqrt, [ss_b, b_epsc], [sd_b], bias=epsc[:], scale=1.0 / D)
                P.op("dve", lambda e, o=rs_t, a=sd_t: e.reciprocal(out=o[:], in_=a[:]), reads=[sd_b], writes=[rs_b])
                xn_t, xn_b = xn.next()
                TS("pool", xn_t[:], x_t[:], rs_t[:], 1.0, ALU.mult, ALU.mult, [x_b, rs_b], [xn_b])
                st1[i] = (xn_t, xn_b)

            def stage2(i):
                xn_t, xn_b = st1.pop(i)
                pA, bA, pB, bB = ptr[i % 2]
                for kc in range(KC):
                    pp, bp = (pA, bA) if kc < 4 else (pB, bB)
                    TR(pp[:, (kc % 4) * 128:(kc % 4 + 1) * 128], xn_t[:, kc * 128:(kc + 1) * 128], ident[:],
                       [xn_b, b_ident], [bp])
                for kc in range(KC):
                    pp, bp = (pA, bA) if kc < 4 else (pB, bB)
                    src = pp[:, (kc % 4) * 128:(kc % 4 + 1) * 128]
                    dst = hT[:, kc, i * 128:(i + 1) * 128]
                    if kc < 4:
                        ACT(dst, src, AF.Identity, [bp, b_gs1, b_modcol], [], dwrites=[b_hT[i // 4]],
                            bias=modcol[:, kc:kc + 1], scale=gs1[:, kc:kc + 1])
                    else:
                        TS("dve", dst, src, gs1[:, kc:kc + 1], modcol[:, kc:kc + 1], ALU.mult, ALU.add,
                           [bp, b_gs1, b_modcol], [], dwrites=[b_hT[i // 4]])

            NTI = S // 128
            stage1(0)
            stage1(1)
            for i in range(NTI):
                if i + 2 < NTI:
                    stage1(i + 2)
                stage2(i)
                if i % 3 == 2 and 4 + i // 3 < 12:
                    mod_tile(4 + i // 3)
            mod_cols(16, 48)
            STT(gs2[:], modcol[:, 32:40], 1.0, n2gc[:], ALU.add, ALU.mult, [b_modcol, b_n2gc], [b_gs2])
            DMA("sp", modscr, modrow[:], [b_modrow], [b_modscr], d_misc)

            P.barrier()
        with ExitStack() as st:
            pf = psum(st, "pf"); b_pf = PBuf()
            wf = sb(st, "wf", [128, KC, H], BF16); b_wf = Buf()
            nbf = sb(st, "nbf", [H, 1], F32); b_nbf = Buf()
            bfc = sb(st, "bfc", [H, 1], F32); b_bfc = Buf()
            ef = sb(st, "ef", [H, 512], F32); b_ef = Buf()
            spf = sb(st, "spf", [H, S], F32); b_spf = Buf()
            ncum = sb(st, "ncum", [H, S], F32); b_ncum = Buf()
            res = sb(st, "res", [H, S], F32); b_res = Buf()
            fpos = sb(st, "fpos", [H, 3, S], BF16); b_fpos = Buf()
            fneg = sb(st, "fneg", [H, 3, S], BF16); b_fneg = Buf()
            DMA("pool", wf[:], win_d[:, OFF_F:OFF_F + H].rearrange("(kc p) n -> p kc n", p=128), [], [b_wf], d_misc)
            DMA("sp", bfc[:], bfor_d, [], [b_bfc], d_misc)
            TS("dve", nbf[:], bfc[:], -1.0, None, ALU.mult, None, [b_bfc], [b_nbf])
            for t in range(8):
                for kc in range(KC):
                    MM(pf[0:H, :], wf[:, kc, :], hT[:, kc, t * 512:(t + 1) * 512], kc == 0, kc == KC - 1,
                       [b_wf, b_hT[t]], [b_pf])
                ACT(ef[:], pf[0:H, :], AF.Exp, [b_pf, b_nbf], [b_ef], bias=nbf[:], scale=-1.0)
                ACT(spf[:, t * 512:(t + 1) * 512], ef[:], AF.Ln, [b_ef, b_onec], [b_spf], bias=onec[0:H, :])
            for t in range(8):
                sl = slice(t * 512, (t + 1) * 512)
                init = 0.0 if t == 0 else ncum[:, t * 512 - 1:t * 512]
                P.op("dve", lambda e, sl=sl, init=init: e.tensor_tensor_scan(out=ncum[:, sl], data0=onesf[0:H, :], data1=spf[:, sl],
                                                                        initial=init, op0=ALU.mult, op1=ALU.add),
                     reads=[b_onesf, b_spf, b_ncum], writes=[b_ncum])
            CP("dve", fpos[:, 0, :], ncum[:], [b_ncum], [b_fpos])
            TT("dve", res[:], ncum[:], fpos[:, 0, :], ALU.subtract, [b_ncum, b_fpos], [b_res])
            CP("dve", fpos[:, 1, :], res[:], [b_res], [b_fpos])
            TT("dve", res[:], res[:], fpos[:, 1, :], ALU.subtract, [b_res, b_fpos], [b_res])
            CP("dve", fpos[:, 2, :], res[:], [b_res], [b_fpos])
            TS("pool", fneg[:], fpos[:], -1.0, 1.0, ALU.mult, ALU.mult, [b_fpos], [b_fneg])
            DMA("sp", fscr[0:3].rearrange("r h s -> h r s"), fpos[:], [b_fpos], [b_fscr], d_misc)
            DMA("sp", fscr[3:6].rearrange("r h s -> h r s"), fneg[:], [b_fneg], [b_fscr], d_misc)
            if debug:
                DMA("sp", dbg_hT, hT[:], b_hT, [], d_out)
            P.barrier()

        if upto >= 2:
            stB = stA.enter_context(ExitStack())
            BIG = sb(stB, "BIG", [128, KC, S], BF16)
            b_big = [Buf() for _ in range(8)]

        if upto >= 2:
            with ExitStack() as st:
                pa = [(psum(st, "pa%d" % i), PBuf()) for i in range(2)]
                pb = [(psum(st, "pb%d" % i), PBuf()) for i in range(2)]
                py = [(psum(st, "py%d" % i), PBuf()) for i in range(2)]
                cw = sb(st, "cw", [128, KC, CK], F32); b_cw = Buf()
                DMA("sp", cw[:], cwT_d, [], [b_cw], d_misc)
                wgl = Ring([(sb(st, "wgl%d" % i, [128, KC, 2, 128], BF16), Buf(), P.dsem("wgl%d" % i)) for i in range(2)])
                dg = Ring([(sb(st, "dg%d" % i, [128, CK, 128], BF16), Buf()) for i in range(2)])
                ub = Ring([(sb(st, "ub%d" % i, [128, 30 + S], BF16), Buf()) for i in range(2)])
                sg = Ring([(sb(st, "sg%d" % i, [128, 512], F32), Buf()) for i in range(2)])
                for (u_t, u_b) in ub.items:
                    MSET("pool", u_t[:, 0:30], 0.0, [u_b])
                for c in range(KC):
                    g_t, g_b, g_s = wgl.next()
                    DMA("pool", g_t[:, :, 0, :], win_d[:, OFF_GLU + c * 128:OFF_GLU + (c + 1) * 128].rearrange("(kc p) n -> p kc n", p=128),
                        [], [g_b], g_s)
                    DMA("pool", g_t[:, :, 1, :], win_d[:, OFF_GLU + D + c * 128:OFF_GLU + D + (c + 1) * 128].rearrange("(kc p) n -> p kc n", p=128),
                        [], [g_b], g_s)
                    if upto >= 5 and c == 1:
                        d_wscr = P.dsem("wscr", bg=True)
                        DMA("pool", wps.rearrange("r (h n) -> r h n", h=1), wap_d.rearrange("r (h n) -> r h n", h=1), [], [b_wscr], d_wscr)
                        DMA("pool", wgs.rearrange("r (h n) -> r h n", h=1), win_d[:, OFF_GATE:OFF_GATE + D].rearrange("r (h n) -> r h n", h=1),
                            [], [b_wscr], d_wscr)
                        DMA("pool", wos.rearrange("r (h n) -> r h n", h=1), wout_d.rearrange("r (h n) -> r h n", h=1), [], [b_wscr], d_wscr)
                        for q in range(4):
                            DMA("pool", w1s[q * 256:(q + 1) * 256, :].rearrange("r (h n) -> r h n", h=2),
                                w1_d[q * 256:(q + 1) * 256, :].rearrange("r (h n) -> r h n", h=2), [], [b_wscr], d_wscr)
                        for q in range(4):
                            DMA("pool", w2s[q * 1024:(q + 1) * 1024, :].rearrange("r (h n) -> r h n", h=1),
                                w2_d[q * 1024:(q + 1) * 1024, :].rearrange("r (h n) -> r h n", h=1), [], [b_wscr], d_wscr)

                    d_t, d_b = dg.next()
                    for k in range(CK):
                        TS("dve", d_t[:, k, :], ident[:], cw[:, c, k:k + 1], None, ALU.mult, None, [b_ident, b_cw], [d_b])
                    u_t, u_b = ub.next()
                    for t in range(8):
                        (pa_t, pa_b), (pb_t, pb_b) = pa[t % 2], pb[t % 2]
                        for kc in range(KC):
                            MM(pa_t[:, :], g_t[:, kc, 0, :], hT[:, kc, t * 512:(t + 1) * 512], kc == 0, kc == KC - 1,
                               [g_b, b_hT[t]], [pa_b])
                        for kc in range(KC):
                            MM(pb_t[:, :], g_t[:, kc, 1, :], hT[:, kc, t * 512:(t + 1) * 512], kc == 0, kc == KC - 1,
                               [g_b, b_hT[t]], [pb_b])
                        s_t, s_b = sg.next()
                        ACT(s_t[:], pb_t[:, :], AF.Sigmoid, [pb_b], [s_b])
                        TT("dve", u_t[:, 30 + t * 512:30 + (t + 1) * 512], pa_t[:, :], s_t[:], ALU.mult, [pa_b, s_b], [u_b])
                    for t in range(8):
                        y_t, y_b = py[t % 2]
                        for k in range(CK):
                            MM(y_t[:, :], d_t[:, k, :], u_t[:, t * 512 + k:t * 512 + k + 512], k == 0, k == CK - 1,
                               [d_b, u_b], [y_b])
                        ACT(BIG[:, c, t * 512:(t + 1) * 512], y_t[:, :], AF.Identity, [y_b, b_cbc], [b_big[t]],
                            bias=cbc[:, c:c + 1])
                P.barrier()
            with ExitStack() as st:
                pa = [(psum(st, "pa2%d" % i), PBuf()) for i in range(2)]
                pb = [(psum(st, "pb2%d" % i), PBuf()) for i in range(2)]
                wcp = sb(st, "wcp", [128, KC, D], BF16); b_wcp = Buf(); d_wcp = P.dsem("wcp")
                wgb = sb(st, "wgb", [128, KC, D], BF16); b_wgb = Buf(); d_wgb = P.dsem("wgb")
                for kc in range(KC):
                    DMA("pool", wcp[:, kc, :], wcp_d[kc * 128:(kc + 1) * 128, :], [], [b_wcp], d_wcp)
                    DMA("pool", wgb[:, kc, :], win_d[kc * 128:(kc + 1) * 128, OFF_GATE + D:OFF_GATE + 2 * D], [], [b_wgb], d_wgb)
                pmean = psum(st, "pmean"); b_pmean = PBuf()
                pmsq = psum(st, "pmsq"); b_pmsq = PBuf()
                onesb = sb(st, "onesb", [128, 128], BF16); b_onesb = Buf()
                MSET("dve", onesb[:], 1.0 / D, [b_onesb])
                ysq = sb(st, "ysq", [128, KC, 512], BF16); b_ysq = Buf()
                mean_s = sb(st, "mean_s", [128, 512], F32); b_mean = Buf()
                var_s = sb(st, "var_s", [128, 512], F32); b_var = Buf()
                rstd_s = var_s; b_rstd = b_var
                zc = Ring([(sb(st, "zc%d" % i, [128, 512], F32), Buf()) for i in range(2)])
                zbr = Ring([(sb(st, "zb%d" % i, [128, KC, 512], BF16), Buf()) for i in range(2)])
                sgb = Ring([(sb(st, "sgb%d" % i, [128, 512], F32), Buf()) for i in range(1)])
                bgt = Ring([(sb(st, "bgt%d" % i, [128, KC, 512], BF16), Buf(), P.dsem("bgt%d" % i)) for i in range(1)])
                zs = Ring([(sb(st, "zs%d" % i, [128, 512], BF16), Buf()) for i in range(2)])
                zcur = {}

                def S1(t):
                    sl = slice(t * 512, (t + 1) * 512)
                    TT("pool", ysq[:], BIG[:, :, sl], BIG[:, :, sl], ALU.mult, [b_big[t]], [b_ysq])
                    for c in range(KC):
                        MM(pmean[:, :], onesb[:], BIG[:, c, sl], c == 0, c == KC - 1, [b_onesb, b_big[t]], [b_pmean])
                    for c in range(KC):
                        MM(pmsq[:, :], onesb[:], ysq[:, c, :], c == 0, c == KC - 1, [b_onesb, b_ysq], [b_pmsq])

                def S2_pieces(t):
                    sl = slice(t * 512, (t + 1) * 512)
                    zb_t, zb_b = zbr.next()
                    zcur[t] = (zb_t, zb_b)

                    def stats():
                        CP("dve", mean_s[:], pmean[:, :], [b_pmean], [b_mean])
                        TT("dve", var_s[:], mean_s[:], mean_s[:], ALU.mult, [b_mean], [b_var])
                        TT("dve", var_s[:], pmsq[:, :], var_s[:], ALU.subtract, [b_pmsq, b_var], [b_var])
                        ACT(var_s[:], var_s[:], AF.Sqrt, [b_var, b_epsc], [b_var], bias=epsc[:])
                        P.op("dve", lambda e: e.reciprocal(out=rstd_s[:], in_=var_s[:]), reads=[b_var], writes=[b_rstd])

                    def zchunk(c):
                        z_t, z_b = zc.next()
                        s_t, s_b = zs.next()
                        TT("pool", z_t[:], BIG[:, c, sl], mean_s[:], ALU.subtract, [b_big[t], b_mean], [z_b])
                        TT("pool", z_t[:], z_t[:], rstd_s[:], ALU.mult, [z_b, b_rstd], [z_b])
                        ACT(s_t[:], z_t[:], AF.Sigmoid, [z_b, b_lngc, b_lnbc], [s_b],
                            bias=lnbc[:, c:c + 1], scale=lngc[:, c:c + 1])
                        TS("dve", z_t[:], z_t[:], lngc[:, c:c + 1], lnbc[:, c:c + 1], ALU.mult, ALU.add, [z_b, b_lngc, b_lnbc], [z_b])
                        P.op("dve", lambda e, o=zb_t[:, c, :], a=z_t[:], b_=s_t[:]: e.tensor_tensor(out=o, in0=a, in1=b_, op=ALU.mult),
                             reads=[z_b, s_b], dwrites=[zb_b])
                    return [stats, lambda: (zchunk(0), zchunk(1)), lambda: (zchunk(2), zchunk(3)), lambda: zchunk(4),
                            lambda: zchunk(5), lambda: zchunk(6), lambda: zchunk(7), lambda: None]

                def Mo(t, o, o_t, o_b):
                    sl = slice(t * 512, (t + 1) * 512)
                    zb_t, zb_b = zcur[t]
                    (pa_t, pa_b), (pb_t, pb_b) = pa[o % 2], pb[o % 2]
                    for c in range(KC):
                        MM(pa_t[:, :], wcp[:, c, o * 128:(o + 1) * 128], zb_t[:, c, :], c == 0, c == KC - 1,
                           [b_wcp, zb_b], [pa_b])
                    for kc in range(KC):
                        MM(pb_t[:, :], wgb[:, kc, o * 128:(o + 1) * 128], hT[:, kc, sl], kc == 0, kc == KC - 1,
                           [b_wgb, b_hT[t]], [pb_b])
                    s_t, s_b = sgb.next()
                    ACT(s_t[:], pb_t[:, :], AF.Sigmoid, [pb_b], [s_b])
                    TT("dve", o_t[:, o, :], pa_t[:, :], s_t[:], ALU.mult, [pa_b, s_b], [o_b])

                S1(0)
                for p_ in S2_pieces(0):
                    p_()
                for t in range(8):
                    sl = slice(t * 512, (t + 1) * 512)
                    pieces = []
                    if t + 1 < 8:
                        S1(t + 1)
                        pieces = S2_pieces(t + 1)
                    o_t, o_b, o_s = bgt.next()
                    for o in range(KC):
                        Mo(t, o, o_t, o_b)
                        if pieces:
                            pieces.pop(0)()
                    DMA("sp", bgscr[:, :, sl], o_t[:], [o_b], [b_bg[t]], o_s)
                P.barrier()

        if upto >= 3:
            with ExitStack() as st:
                pS = Ring([(psum(st, "pS%d" % i), PBuf()) for i in range(3)])
                pO = Ring([(psum(st, "pO%d" % i), PBuf()) for i in range(2)])
                pBc = psum(st, "pBc"); b_pBc = PBuf()
                pT = pBc; b_pT = b_pBc
                pQ = Ring([(psum(st, "pQ%d" % i), PBuf()) for i in range(2)])
                qaug = [(sb(st, "qaug%d" % i, [70, S], BF16), Buf(), P.dsem("qaug%d" % i)) for i in range(2)]
                kaug = [(sb(st, "kaug%d" % i, [70, S], BF16), Buf(), P.dsem("kaug%d" % i)) for i in range(2)]
                vh = [(sb(st, "vh0", [128, S // 128, 65], BF16), Buf()), (sb(st, "vh1", [128, S // 128, 128], BF16), Buf())]
                wqkv = Ring([(sb(st, "wqkv%d" % i, [128, KC, 3, 128], BF16), Buf(), P.dsem("wqkv%d" % i)) for i in range(2)])
                qkraw = Ring([(sb(st, "qkraw%d" % i, [128, 8, 2, DH], F32), Buf()) for i in range(2)])
                sqs = sb(st, "sqs", [128, 8, 2, DH], F32); b_sqs = Buf()
                ssq = sb(st, "ssq", [128, 16], F32); b_ssq = Buf()
                rsq = sb(st, "rsq", [128, 16], F32); b_rsq = Buf()
                nhalf = sb(st, "nhalf", [128, 16], F32); b_nhalf = Buf()
                MSET("pool", nhalf[:], -0.5, [b_nhalf])
                PT = Ring([(sb(st, "PT%d" % i, [128, 512], BF16), Buf()) for i in range(4)])
                rden = Ring([(sb(st, "rden%d" % i, [128, 512], F32), Buf()) for i in range(1)])
                bcs = Ring([(sb(st, "bcs%d" % i, [128, 512], F32), Buf()) for i in range(1)])
                for i in range(2):
                    MSET("pool", qaug[i][0][64:70, :], 1.0, [qaug[i][1]])
                    MSET("pool", kaug[i][0][64:70, :], 1.0, [kaug[i][1]])
                MSET("pool", vh[0][0][:, :, 64:65], 1.0, [vh[0][1]])
                MSET("pool", vh[1][0][:, :, 0:64], 0.0, [vh[1][1]])
                MSET("pool", vh[1][0][:, :, 0:1], 1.0, [vh[1][1]])
                cur_w = [None]
                def inproj_gen(h):
                    e = h % 2
                    if e == 0:
                        w_t, w_b, w_s = wqkv.next()
                        c0 = (h // 2) * 128
                        for j, off in enumerate((OFF_Q, OFF_K, OFF_V)):
                            DMA("pool", w_t[:, :, j, :], win_d[:, off + c0:off + c0 + 128].rearrange("(kc p) n -> p kc n", p=128),
                                [], [w_b], w_s)
                        cur_w[0] = (w_t, w_b)
                    w_t, w_b = cur_w[0]
                    q_t, q_b, q_s = qaug[e]
                    k_t, k_b, k_s = kaug[e]
                    v_t, v_b = vh[e]
                    DMA("sp", q_t[64:67, :], fscr[3:6, h, :], [b_fscr], [q_b], q_s)
                    DMA("sp", k_t[67:70, :], fscr[0:3, h, :], [b_fscr], [k_b], k_s)
                    yield
                    voff = 0 if e == 0 else 64
                    pending_tr = []

                    def tr_steps(grp, qr_t, qr_b):
                        steps = []
                        for g2 in range(2):
                            g = grp * 2 + g2
                            for which in range(2):
                                def step(g=g, g2=g2, which=which):
                                    for ii in range(4):
                                        TR(pT[0:64, ii * 128:(ii + 1) * 128], qr_t[:, g2 * 4 + ii, which, :], ident[:],
                                           [qr_b, b_ident], [b_pT])
                                    if which == 0:
                                        CP("dve", q_t[0:64, g * 512:(g + 1) * 512], pT[0:64, :], [b_pT], [q_b])
                                    else:
                                        TS("dve", k_t[0:64, g * 512:(g + 1) * 512], pT[0:64, :], gqk[:], None, ALU.mult, None,
                                           [b_pT, b_gqk], [k_b])
                                steps.append(step)
                        return steps

                    for grp in range(4):
                        qr_t, qr_b = qkraw.next()
                        qk3 = qr_t[:].rearrange("p a b d -> p (a b) d")
                        for il in range(8):
                            i = grp * 8 + il
                            p_t, p_b = pQ.next()
                            for kc in range(KC):
                                MM(p_t[:, 0:192].rearrange("p (a b) -> p a b", a=3), hT[:, kc, i * 128:(i + 1) * 128],
                                   w_t[:, kc, :, e * 64:(e + 1) * 64], kc == 0, kc == KC - 1, [b_hT[i // 4], w_b], [p_b])
                            CP("dve", qr_t[:, il, :, :], p_t[:, 0:128].rearrange("p (a b) -> p a b", a=2), [p_b], [qr_b])
                            CP("dve", v_t[:, i, voff:voff + 64], p_t[:, 128:192], [p_b], [v_b])
                            if pending_tr and il % 2 == 1:
                                pending_tr.pop(0)()
                            yield
                        TT("pool", sqs[:], qr_t[:], qr_t[:], ALU.mult, [qr_b], [b_sqs])
                        P.op("dve", lambda e_: e_.tensor_reduce(out=ssq[:], in_=sqs[:].rearrange("p a b d -> p (a b) d"), axis=AX.X, op=ALU.add),
                             reads=[b_sqs], writes=[b_ssq])
                        TS("pool", ssq[:], ssq[:], 1.0 / DH, EPS, ALU.mult, ALU.add, [b_ssq], [b_ssq])
                        TT("pool", rsq[:], ssq[:], nhalf[:], ALU.pow, [b_ssq, b_nhalf], [b_rsq])
                        TT("pool", qk3, qk3, rsq[:].unsqueeze(2).to_broadcast([128, 16, DH]), ALU.mult, [qr_b, b_rsq], [qr_b])
                        yield
                        pending_tr = tr_steps(grp, qr_t, qr_b)
                    yield
                    yield
                    while pending_tr:
                        pending_tr.pop(0)()
                        yield

                LAG = 3
                pend = []
                defer = []
                otile = {}

                def emit_S(h, j, i):
                    e = h % 2
                    q_t, q_b, _ = qaug[e]
                    k_t, k_b, _ = kaug[e]
                    s_t, s_b = pS.next()
                    kl = k_t[0:70, i * 128:(i + 1) * 128]
                    if i < 4 * j:
                        c0 = 0
                        MM(s_t[:, :], kl, q_t[0:70, j * 512:(j + 1) * 512], True, True, [k_b, q_b], [s_b])
                    else:
                        c0 = (i - 4 * j) * 128
                        MM(s_t[:, c0:c0 + 128], kl, q_t[0:70, j * 512 + c0:j * 512 + c0 + 128], True, False,
                           [k_b, q_b], [s_b], skip=True)
                        MM(s_t[:, c0:c0 + 128], identb[:], maskb[:], False, True, [b_identb, b_maskb], [s_b], skip=True)
                        if c0 + 128 < 512:
                            MM(s_t[:, c0 + 128:512], kl, q_t[0:70, j * 512 + c0 + 128:(j + 1) * 512], True, True,
                               [k_b, q_b], [s_b], skip=True)
                    p_t, p_b = PT.next()
                    ACT(p_t[:, c0:512], s_t[:, c0:512], AF.Exp, [s_b], [p_b])
                    pend.append((h, j, i, c0, p_t, p_b))

                def emit_PV():
                    h, j, i, c0, p_t, p_b = pend.pop(0)
                    e = h % 2
                    c = h // 2
                    v_t, v_b = vh[e]
                    M = 65 if e == 0 else 128
                    p0 = 64 if e == 0 else 0
                    o0 = 0 if e == 0 else 64
                    nblk = 4 * (j + 1)
                    if i == 0:
                        otile[(h, j)] = pO.next()
                    o_t, o_b = otile[(h, j)]
                    MM(o_t[0:M, c0:512], v_t[:, i, 0:M], p_t[:, c0:512], i == 0, i == nblk - 1, [v_b, p_b], [o_b], skip=True)
                    if i == nblk - 1:
                        del otile[(h, j)]
                        r_t, r_b = rden.next()
                        ACT(r_t[p0:p0 + 1, :], o_t[p0:p0 + 1, :], AF.Ln, [o_b], [r_b])
                        ACT(r_t[p0:p0 + 1, :], r_t[p0:p0 + 1, :], AF.Exp, [r_b], [r_b], scale=-1.0)

                        def tail():
                            MM(pBc[:, :], onesf[p0:p0 + 1, 0:128], r_t[p0:p0 + 1, :], True, True, [b_onesf, r_b], [b_pBc])
                            b_t, b_b = bcs.next()
                            CP("dve", b_t[o0:o0 + 64, :], pBc[o0:o0 + 64, :], [b_pBc], [b_b])
                            TT("dve", BIG[o0:o0 + 64, c, j * 512:(j + 1) * 512], o_t[o0:o0 + 64, :], b_t[o0:o0 + 64, :], ALU.mult,
                               [o_b, b_b], [b_big[j]])
                        defer.append([3, tail])

                def tick_defer(force=False):
                    for d in list(defer):
                        d[0] -= 1
                        if d[0] <= 0 or force:
                            d[1]()
                            defer.remove(d)

                g0 = inproj_gen(0)
                for _ in g0:
                    pass
                for h in range(H):
                    nxt = inproj_gen(h + 1) if h + 1 < H else None
                    cnt = 0
                    for j in range(8):
                        for i in range(4 * (j + 1)):
                            emit_S(h, j, i)
                            if len(pend) > LAG:
                                emit_PV()
                            tick_defer()
                            cnt += 1
                            if nxt is not None and cnt % 3 == 0:
                                next(nxt, None)
                    if nxt is not None:
                        for _ in nxt:
                            pass
                while pend:
                    emit_PV()
                    tick_defer()
                tick_defer(force=True)
                tick_defer(force=True)
                if debug:
                    DMA("sp", dbg_ao, BIG[:], b_big, [], d_out)
                P.barrier()

        if upto >= 4:
            with ExitStack() as st:
                pa = [(psum(st, "p4a%d" % i), PBuf()) for i in range(2)]
                pb = [(psum(st, "p4b%d" % i), PBuf()) for i in range(2)]
                wap = sb(st, "wap", [128, KC, D], BF16); b_wap = Buf(); d_wap = P.dsem("wap")
                wga = sb(st, "wga", [128, KC, D], BF16); b_wga = Buf(); d_wga = P.dsem("wga")
                if upto >= 5:
                    DMA("sp", wap[:], wps.rearrange("(kc p) n -> p kc n", p=128), [b_wscr], [b_wap], d_wap)
                    DMA("sp", wga[:], wgs.rearrange("(kc p) n -> p kc n", p=128), [b_wscr], [b_wga], d_wga)
                else:
                    for kc in range(KC):
                        DMA("pool", wap[:, kc, :], wap_d[kc * 128:(kc + 1) * 128, :], [], [b_wap], d_wap)
                        DMA("pool", wga[:, kc, :], win_d[kc * 128:(kc + 1) * 128, OFF_GATE:OFF_GATE + D], [], [b_wga], d_wga)
                bgl = Ring([(sb(st, "bgl%d" % i, [128, KC, 512], BF16), Buf(), P.dsem("bgl%d" % i)) for i in range(2)])
                mgt = Ring([(sb(st, "mgt%d" % i, [128, KC, 512], BF16), Buf(), P.dsem("mgt%d" % i)) for i in range(2)])
                gb2 = sb(st, "gb2", [128, D], F32); b_gb2 = Buf()
                DMA("sp", gb2[:], modscr[0:1, 5 * D:6 * D].broadcast_to([128, D]), [b_modscr], [b_gb2], d_misc)

                def scale_w2(t_):
                    for k_ in range(4):
                        w2c_ = BIG[:, 2 * k_:2 * k_ + 2, t_ * 512:(t_ + 1) * 512]
                        TT("pool", w2c_, w2c_, gb2[:].rearrange("p (h n) -> p h n", h=2), ALU.mult, [b_big[t_], b_gb2], [b_big[t_]])
                sga = Ring([(sb(st, "sga%d" % i, [128, 512], F32), Buf()) for i in range(2)])
                tmp = Ring([(sb(st, "tmp4%d" % i, [128, 512], F32), Buf()) for i in range(2)])
                d_w1 = P.dsem("w1"); d_w2 = P.dsem("w2")
                b_w2 = [Buf() for _ in range(4)]
                w1v = hT
                w2v = BIG[:].rearrange("p c t -> p (c t)").rearrange("p (f n) -> p f n", n=D)
                nxt_l = bgl.next()
                DMA("sp", nxt_l[0][:], bgscr[:, :, 0:512], [b_bg[0]], [nxt_l[1]], nxt_l[2])
                for t in range(8):
                    sl = slice(t * 512, (t + 1) * 512)
                    l_t, l_b, l_s = nxt_l
                    m_t, m_b, m_s = mgt.next()
                    for o in range(KC):
                        (pa_t, pa_b), (pb_t, pb_b) = pa[o % 2], pb[o % 2]
                        for c in range(KC):
                            MM(pa_t[:, :], wap[:, c, o * 128:(o + 1) * 128], BIG[:, c, sl], c == 0, c == KC - 1,
                               [b_wap, b_big[t]], [pa_b])
                        for kc in range(KC):
                            MM(pb_t[:, :], wga[:, kc, o * 128:(o + 1) * 128], hT[:, kc, sl], kc == 0, kc == KC - 1,
                               [b_wga, b_hT[t]], [pb_b])
                        s_t, s_b = sga.next()
                        ACT(s_t[:], pb_t[:, :], AF.Sigmoid, [pb_b], [s_b])
                        t_t, t_b = tmp.next()
                        TT("dve", t_t[:], pa_t[:, :], s_t[:], ALU.mult, [pa_b, s_b], [t_b])
                        TT("dve", m_t[:, o, :], t_t[:], l_t[:, o, :], ALU.add, [t_b, l_b], [m_b])
                    DMA("sp", mscr[:, :, sl], m_t[:], [m_b], [b_ms[t]], m_s)
                    if t + 1 < 8:
                        nxt_l = bgl.next()
                        DMA("sp", nxt_l[0][:], bgscr[:, :, (t + 1) * 512:(t + 2) * 512], [b_bg[t + 1]], [nxt_l[1]], nxt_l[2])
                    if upto >= 5:
                        DMA("sp", hT[:, :, sl], w1s[:, sl].rearrange("(kc p) n -> p kc n", p=128), [b_wscr], [b_hT[t]], d_w1)
                        for f in range(4 * t, 4 * t + 4):
                            DMA("sp", BIG[:, (f % 4) * 2:(f % 4) * 2 + 2, sl],
                                w2s[f * 128:(f + 1) * 128, :].rearrange("p (h n) -> p h n", h=2), [b_wscr], [b_big[t]], d_w2)
                    if upto >= 5 and t >= 1:
                        scale_w2(t - 1)
                if upto >= 5:
                    scale_w2(7)
                P.barrier(exclude=(d_w1,))

        if upto >= 5:
            with ExitStack() as st:
                TT5 = 256
                NT5 = S // TT5
                wo = sb(st, "wo", [128, KC, D], BF16); b_wo = Buf(); d_wo = P.dsem("wo")
                w1 = w1v
                w2 = w2v
                identf = ident
                pw = Ring([(psum(st, "pw%d" % i), PBuf()) for i in range(2)])
                ph = Ring([(psum(st, "ph%d" % i), PBuf()) for i in range(2)])
                po = [(psum(st, "po%d" % i), PBuf()) for i in range(4)]
                mgl = Ring([(sb(st, "mgl%d" % i, [128, KC, TT5], BF16), Buf(), P.dsem("mgl%d" % i)) for i in range(2)])
                xl = Ring([(sb(st, "xl%d" % i, [128, D], F32), Buf(), P.dsem("xl%d" % i)) for i in range(1)])
                x1 = Ring([(sb(st, "x1_%d" % i, [128, D], F32), Buf()) for i in range(4)])
                xn2 = Ring([(sb(st, "xn2_%d" % i, [128, D], F32), Buf()) for i in range(2)])
                junk5 = sb(st, "junk5", [128, D], BF16); b_junk5 = Buf()
                ss5 = Ring([(sb(st, "ss5_%d" % i, [128, 1], F32), Buf()) for i in range(4)])
                sd5 = Ring([(sb(st, "sd5_%d" % i, [128, 1], F32), Buf()) for i in range(4)])
                rs5 = Ring([(sb(st, "rs5_%d" % i, [128, 1], F32), Buf()) for i in range(4)])
                h2T = Ring([(sb(st, "h2T%d" % i, [128, KC, TT5], BF16), Buf()) for i in range(2)])
                rl = Ring([(sb(st, "rl%d" % i, [128, TT5], F32), Buf()) for i in range(2)])
                aT = Ring([(sb(st, "aT%d" % i, [128, TT5], BF16), Buf()) for i in range(3)])
                ot = Ring([(sb(st, "ot%d" % i, [128, D], F32), Buf(), P.dsem("ot%d" % i)) for i in range(2)])

                gb_t, gb_b, _ = ot.items[0]
                DMA("sp", wo[:], wos.rearrange("(kc p) n -> p kc n", p=128), [b_wscr], [b_wo], d_wo)
                DMA("sp", gb_t[:], modscr[0:1, 2 * D:3 * D].broadcast_to([128, D]), [b_modscr], [gb_b], d_misc)
                for kc in range(KC):
                    TT("dve", wo[:, kc, :], wo[:, kc, :], gb_t[:], ALU.mult, [b_wo, gb_b], [b_wo])
                stA5 = {}

                def A1(tt):
                    m_t, m_b, m_s = mgl.next()
                    DMA("sp", m_t[:], mscr[:, :, tt * TT5:(tt + 1) * TT5], [b_ms[tt // 2]], [m_b], m_s)
                    x1s, ns = [], []
                    for sub in range(2):
                        tok0 = tt * TT5 + sub * 128
                        x_t, x_b, x_s = xl.next()
                        DMA("sp", x_t[:], x_d[tok0:tok0 + 128, :], [], [x_b], x_s)
                        x1_t, x1_b = x1.next()
                        x1s.append((x1_t, x1_b))
                        for half in range(2):
                            hs = slice(half * 512, (half + 1) * 512)
                            p_t, p_b = pw.next()
                            for kc in range(KC):
                                MM(p_t[:, :], m_t[:, kc, sub * 128:(sub + 1) * 128], wo[:, kc, hs],
                                   kc == 0, kc == KC - 1, [m_b, b_wo], [p_b])
                            TT("dve", x1_t[:, hs], p_t[:, :], x_t[:, hs], ALU.add, [p_b, x_b], [], ) if False else \
                                P.op("dve", lambda e, o=x1_t[:, hs], a=p_t[:, :], b_=x_t[:, hs]: e.tensor_tensor(out=o, in0=a, in1=b_, op=ALU.add),
                                     reads=[p_b, x_b], dwrites=[x1_b])
                        ss_t, ss_b = ss5.next()
                        sd_t, sd_b = sd5.next()
                        rs_t, rs_b = rs5.next()
                        ACT(junk5[:], x1_t[:], AF.Square, [x1_b], [b_junk5, ss_b], accum=ss_t[:])
                        ACT(sd_t[:], ss_t[:], AF.Sqrt, [ss_b, b_epsc], [sd_b], bias=epsc[:], scale=1.0 / D)
                        P.op("dve", lambda e, o=rs_t, a=sd_t: e.reciprocal(out=o[:], in_=a[:]), reads=[sd_b], writes=[rs_b])
                        n_t, n_b = xn2.next()
                        TS("pool", n_t[:], x1_t[:], rs_t[:], 1.0, ALU.mult, ALU.mult, [x1_b, rs_b], [n_b])
                        ns.append((n_t, n_b))
                    stA5[tt] = (x1s, ns)

                def A2(tt):
                    x1s, ns = stA5[tt]
                    h_t, h_b = h2T.next()
                    for sub in range(2):
                        n_t, n_b = ns[sub]
                        for half in range(2):
                            p_t, p_b = pw.next()
                            for k4 in range(4):
                                kc = half * 4 + k4
                                TR(p_t[:, k4 * 128:(k4 + 1) * 128], n_t[:, kc * 128:(kc + 1) * 128], ident[:], [n_b, b_ident], [p_b])
                            for k4 in range(4):
                                kc = half * 4 + k4
                                src_ = p_t[:, k4 * 128:(k4 + 1) * 128]
                                dst = h_t[:, kc, sub * 128:(sub + 1) * 128]
                                if half == 0:
                                    ACT(dst, src_, AF.Identity, [p_b, b_gs2, b_modcol], [], dwrites=[h_b],
                                        bias=modcol[:, 24 + kc:25 + kc], scale=gs2[:, kc:kc + 1])
                                else:
                                    TS("dve", dst, src_, gs2[:, kc:kc + 1], modcol[:, 24 + kc:25 + kc], ALU.mult, ALU.add,
                                       [p_b, b_gs2, b_modcol], [], dwrites=[h_b])
                    stA5[tt] = (x1s, ns, h_t, h_b)

                def Bst(tt):
                    x1s, ns, h_t, h_b = stA5.pop(tt)
                    pend = None
                    for f in range(FC + 1):
                        if f < FC:
                            p_t, p_b = ph.next()
                            for kc in range(KC):
                                MM(p_t[:, 0:TT5], w1[:, kc, f * 128:(f + 1) * 128], h_t[:, kc, :], kc == 0, kc == KC - 1,
                                   [b_hT[f // 4], h_b], [p_b])
                            r_t, r_b = rl.next()
                            ACT(r_t[:], p_t[:, 0:TT5], AF.Relu, [p_b], [r_b])
                            a_t, a_b = aT.next()
                            TT("dve", a_t[:], r_t[:], r_t[:], ALU.mult, [r_b], [a_b])
                        if pend is not None:
                            pf_, pa_t, pa_b = pend
                            w2c = BIG[:, (pf_ % 4) * 2:(pf_ % 4) * 2 + 2, (pf_ // 4) * 512:(pf_ // 4 + 1) * 512]
                            for sub in range(2):
                                for half in range(2):
                                    o_t, o_b = po[sub * 2 + half]
                                    MM(o_t[:, :], pa_t[:, sub * 128:(sub + 1) * 128], w2c[:, half, :],
                                       pf_ == 0, pf_ == FC - 1, [pa_b, b_big[pf_ // 4]], [o_b])
                        pend = (f, a_t, a_b) if f < FC else None
                        if f == 3 and tt + 1 < NT5:
                            A1(tt + 1)
                        if f == 18 and tt + 1 < NT5:
                            A2(tt + 1)
                    for sub in range(2):
                        tok0 = tt * TT5 + sub * 128
                        out_t, out_b, out_s = ot.next()
                        for half in range(2):
                            hs = slice(half * 512, (half + 1) * 512)
                            o_t, o_b = po[sub * 2 + half]
                            x1_t, x1_b = x1s[sub]
                            P.op("dve", lambda e, o=out_t[:, hs], a=o_t[:, :], b_=x1_t[:, hs]: e.tensor_tensor(out=o, in0=a, in1=b_, op=ALU.add),
                                 reads=[o_b, x1_b], dwrites=[out_b])
                        DMA("sp", out_d[tok0:tok0 + 128, :], out_t[:], [out_b], [], out_s)
                        if out_s not in P.final_waits:
                            P.final_waits.append(out_s)

                A1(0)
                A2(0)
                for tt in range(NT5):
                    Bst(tt)
        P.emit(nc, top)
    return nc


def _col(v):
    return np.ascontiguousarray(np.asarray(v, np.float32).reshape(KC, 128).T)


def make_in_maps(inputs, cores):
    x = np.asarray(inputs["x"], np.float32)
    c = np.asarray(inputs["c"], np.float32)
    shared = {
        "w_ada": np.ascontiguousarray(inputs["w_ada"][0], dtype=np.float32),
        "b_ada": np.ascontiguousarray(inputs["b_ada"][0].reshape(1, -1), dtype=np.float32),
        "n1g": _col(inputs["norm1_g"][0]),
        "n2g": _col(inputs["norm2_g"][0]),
        "w_in": np.ascontiguousarray(inputs["w_in"][0], dtype=np.float32),
        "bfor": np.ascontiguousarray(np.asarray(inputs["b_forget"][0], np.float32).reshape(H, 1)),
        "qg": np.ascontiguousarray(np.asarray(inputs["q_norm_g"][0], np.float32).reshape(DH, 1)),
        "kg": np.ascontiguousarray(np.asarray(inputs["k_norm_g"][0], np.float32).reshape(DH, 1)),
        "w_attn_proj": np.ascontiguousarray(inputs["w_attn_proj"][0], dtype=np.float32),
        "cwT": np.ascontiguousarray(np.asarray(inputs["conv_w"][0], np.float32).T.reshape(KC, 128, CK).transpose(1, 0, 2)),
        "cb": _col(inputs["conv_b"][0]),
        "lng": _col(inputs["conv_ln_g"][0]),
        "lnb": _col(inputs["conv_ln_b"][0]),
        "w_conv_proj": np.ascontiguousarray(inputs["w_conv_proj"][0], dtype=np.float32),
        "w_out": np.ascontiguousarray(inputs["w_out"][0], dtype=np.float32),
        "w_mlp1": np.ascontiguousarray(inputs["w_mlp1"][0], dtype=np.float32),
        "w_mlp2": np.ascontiguousarray(inputs["w_mlp2"][0], dtype=np.float32),
        "ident": np.eye(128, dtype=np.float32),
        "maskb": np.where(np.arange(128)[None, :] >= np.arange(128)[:, None], 0.0, -30000.0).astype(np.float32),
    }
    maps = []
    for b in cores:
        m = dict(shared)
        m["x"] = np.ascontiguousarray(x[b])
        m["ccol"] = _col(c[b])
        maps.append(m)
    return maps


def kernel(**inputs):
    nc = build(upto=5, debug=False)
    cores = list(range(8))
    in_maps = make_in_maps(inputs, cores)
    res = run_bass_kernel_spmd(nc, in_maps, core_ids=cores)
    out = np.stack([np.asarray(r["out"], dtype=np.float32) for r in res.results], axis=0)
    return out
```

```python
import os
from contextlib import ExitStack
import numpy as np
import concourse.bass as bass
import concourse.mybir as mybir
from concourse.bass_utils import run_bass_kernel_spmd

F32 = mybir.dt.float32
BF16 = mybir.dt.bfloat16
AF = mybir.ActivationFunctionType
ALU = mybir.AluOpType
AX = mybir.AxisListType

S = 4096
D = 1024
H = 16
DH = 64
KC = 8
DFF = 4096
FC = 32
CK = 31
EPS = 1e-6
DIN = 7184
OFF_Q, OFF_K, OFF_V, OFF_F, OFF_GLU, OFF_GATE = 0, 1024, 2048, 3072, 3088, 5136

ENGS = ("pe", "act", "dve", "pool", "sp")
RAW_ONLY = False


class Buf:
    __slots__ = ("name", "w", "rs", "dw")
    excl = False

    def __init__(self, name=""):
        self.name = name
        self.w = None
        self.rs = []
        self.dw = []


class PBuf(Buf):
    __slots__ = ()
    excl = True


class Op:
    __slots__ = ("eng", "fn", "deps", "sig", "val", "idx", "dma", "dsem", "dval")

    def __init__(self, eng, fn, dma):
        self.eng = eng
        self.fn = fn
        self.dma = dma
        self.deps = ([], [])
        self.sig = False
        self.val = 0
        self.dsem = None
        self.dval = 0


class DSem:
    __slots__ = ("name", "issued", "h", "bg")

    def __init__(self, name, bg=False):
        self.name = name
        self.issued = 0
        self.h = None
        self.bg = bg


class Prog:
    def __init__(self, same_engine_sync=True):
        self.ops = {e: [] for e in ENGS}
        self.dsems = []
        self.same_engine_sync = same_engine_sync
        self.final_waits = []

    def dsem(self, name, bg=False):
        s = DSem(name, bg)
        self.dsems.append(s)
        return s

    def op(self, eng, fn, reads=(), writes=(), dsem=None, dwrites=()):
        o = Op(eng, fn, dsem is not None)
        o.idx = len(self.ops[eng])
        deps = []
        raw = set()
        for b in reads:
            if b.w is not None:
                deps.append(b.w)
                raw.add(id(b.w))
            for d in b.dw:
                deps.append(d)
                raw.add(id(d))
            if b.excl:
                deps.extend(r for r in b.rs if r.eng != eng)
        for b in writes:
            if b.w is not None:
                deps.append(b.w)
            deps.extend(b.dw)
            deps.extend(b.rs)
        for b in dwrites:
            if b.w is not None:
                deps.append(b.w)
            deps.extend(b.rs)
        dd = {}
        cd = {}
        for d in deps:
            if d.dma:
                dd[id(d.dsem)] = (d.dsem, d.dsem.issued)
            else:
                if d.eng == eng and not o.dma:
                    if eng == "pe" or not self.same_engine_sync or (RAW_ONLY and id(d) not in raw):
                        continue
                p = cd.get(d.eng)
                if p is None or d.idx > p.idx:
                    cd[d.eng] = d
        o.deps = (list(cd.values()), list(dd.values()))
        for d in cd.values():
            d.sig = True
        if o.dma:
            dsem.issued += 16
            o.dsem = dsem
            o.dval = dsem.issued
        for b in reads:
            if o.dma:
                b.rs = [r for r in b.rs if not (r.dma and r.dsem is o.dsem)]
            else:
                b.rs = [r for r in b.rs if r.dma or r.eng != eng]
            b.rs.append(o)
        for b in writes:
            b.w = o
            b.rs = []
            b.dw = []
        for b in dwrites:
            b.dw = [r for r in b.dw if r.dma or r.eng != eng]
            b.dw.append(o)
        self.ops[eng].append(o)
        return o

    def barrier(self, exclude=()):
        lasts = []
        for e in ENGS:
            for o in reversed(self.ops[e]):
                if not o.dma and o.fn is not None:
                    lasts.append(o)
                    o.sig = True
                    break
        dds = [(s, s.issued) for s in self.dsems if s.issued > 0 and s not in exclude and not s.bg]
        for e in ENGS:
            o = Op(e, None, False)
            o.idx = len(self.ops[e])
            o.deps = ([d for d in lasts if d.eng != e], list(dds))
            self.ops[e].append(o)

    def emit(self, nc, stack):
        esem = {e: stack.enter_context(nc.semaphore("s_" + e)) for e in ENGS}
        for s in self.dsems:
            s.h = stack.enter_context(nc.semaphore("d_" + s.name))
        for e in ENGS:
            c = 0
            for o in self.ops[e]:
                if o.dma or o.fn is None:
                    continue
                if o.sig:
                    c += 1
                    o.val = c
        block = stack.enter_context(nc.Block())
        secs = {"pe": block.tensor, "act": block.scalar, "dve": block.vector,
                "pool": block.gpsimd, "sp": block.sync}
        for e in ENGS:
            ops = self.ops[e]
            final = self.final_waits if e == "sp" else []

            def section(eng, ops=ops, e=e, final=final):
                known = {}
                for o in ops:
                    cds, dds = o.deps
                    for d in cds:
                        key = ("e", d.eng)
                        if known.get(key, 0) >= d.val:
                            continue
                        known[key] = d.val
                        eng.wait_ge(esem[d.eng], d.val)
                    for (s, v) in dds:
                        key = ("d", id(s))
                        if known.get(key, 0) >= v:
                            continue
                        known[key] = v
                        eng.wait_ge(s.h, v)
                    if o.fn is None:
                        continue
                    ins = o.fn(eng)
                    if o.dma:
                        ins.then_inc(o.dsem.h, 16)
                    elif o.sig:
                        ins.then_inc(esem[e], 1)
                for s in final:
                    eng.wait_ge(s.h, s.issued)

            secs[e](section)


class Ring:
    def __init__(self, items):
        self.items = items
        self.i = 0

    def next(self):
        it = self.items[self.i % len(self.items)]
        self.i += 1
        return it


def build(upto=5, debug=False):
    nc = bass.Bass("TRN2", target_bir_lowering=False)
    P = Prog()

    def din(name, shape, dt=F32):
        return nc.dram_tensor(name, shape, dt, kind="ExternalInput").ap()

    x_d = din("x", [S, D])
    ccol_d = din("ccol", [128, KC])
    wada_d = din("w_ada", [D, 6 * D])
    bada_d = din("b_ada", [1, 6 * D])
    n1g_d = din("n1g", [128, KC])
    n2g_d = din("n2g", [128, KC])
    win_d = din("w_in", [D, DIN])
    bfor_d = din("bfor", [H, 1])
    qg_d = din("qg", [DH, 1])
    kg_d = din("kg", [DH, 1])
    wap_d = din("w_attn_proj", [D, D])
    cwT_d = din("cwT", [128, KC, CK])
    cb_d = din("cb", [128, KC])
    lng_d = din("lng", [128, KC])
    lnb_d = din("lnb", [128, KC])
    wcp_d = din("w_conv_proj", [D, D])
    wout_d = din("w_out", [D, D])
    w1_d = din("w_mlp1", [D, DFF])
    w2_d = din("w_mlp2", [DFF, D])
    ident_d = din("ident", [128, 128])
    mask_d = din("maskb", [128, 128])

    out_d = nc.dram_tensor("out", [S, D], F32, kind="ExternalOutput").ap()
    skind = "ExternalOutput" if debug else "Internal"
    modscr = nc.dram_tensor("modscr", [1, 6 * D], F32, kind=skind).ap()
    fscr = nc.dram_tensor("fscr", [6, H, S], BF16, kind=skind).ap()
    bgscr = nc.dram_tensor("bgscr", [128, KC, S], BF16, kind=skind).ap()
    mscr = nc.dram_tensor("mscr", [128, KC, S], BF16, kind=skind).ap()
    w1s = nc.dram_tensor("w1s", [D, DFF], BF16, kind="Internal").ap()
    w2s = nc.dram_tensor("w2s", [DFF, D], BF16, kind="Internal").ap()
    wos = nc.dram_tensor("wos", [D, D], BF16, kind="Internal").ap()
    wps = nc.dram_tensor("wps", [D, D], BF16, kind="Internal").ap()
    wgs = nc.dram_tensor("wgs", [D, D], BF16, kind="Internal").ap()
    b_wscr = Buf()
    b_modscr, b_fscr = Buf(), Buf()
    b_bg = [Buf() for _ in range(8)]
    b_ms = [Buf() for _ in range(8)]
    if debug:
        dbg_hT = nc.dram_tensor("dbg_hT", [128, KC, S], BF16, kind="ExternalOutput").ap()
        dbg_ao = nc.dram_tensor("dbg_ao", [128, KC, S], BF16, kind="ExternalOutput").ap()

    d_out = P.dsem("out")
    P.final_waits.append(d_out)
    d_misc = P.dsem("misc")

    with ExitStack() as top:
        def sb(st, name, shape, dt):
            return st.enter_context(nc.sbuf_tensor("s_" + name, shape, dt))

        def psum(st, name):
            return st.enter_context(nc.psum_tensor("p_" + name, [128, 512], F32))

        def DMA(q, out, in_, reads, writes, dsem):
            return P.op(q, lambda e: e.dma_start(out=out, in_=in_), reads=reads, writes=writes, dsem=dsem)

        def MM(out, lhsT, rhs, start, stop, reads, writes, skip=False):
            return P.op("pe", lambda e: e.matmul(out, lhsT=lhsT, rhs=rhs, start=start, stop=stop,
                                                 skip_group_check=skip), reads=reads, writes=writes)

        def TR(out, in_, ident, reads, writes):
            return P.op("pe", lambda e: e.transpose(out=out, in_=in_, identity=ident), reads=reads, writes=writes)

        def ACT(out, in_, func, reads, writes, bias=None, scale=None, accum=None, dwrites=()):
            def fn(e):
                kw = {}
                if bias is not None:
                    kw["bias"] = bias
                if scale is not None:
                    kw["scale"] = scale
                if accum is not None:
                    kw["accum_out"] = accum
                return e.activation(out=out, in_=in_, func=func, **kw)
            return P.op("act", fn, reads=reads, writes=writes, dwrites=dwrites)

        def TS(eng, out, in0, s1, s2, op0, op1, reads, writes, dwrites=()):
            def fn(e):
                if op1 is None:
                    return e.tensor_scalar(out=out, in0=in0, scalar1=s1, scalar2=None, op0=op0)
                return e.tensor_scalar(out=out, in0=in0, scalar1=s1, scalar2=s2, op0=op0, op1=op1)
            return P.op(eng, fn, reads=reads, writes=writes, dwrites=dwrites)

        def TT(eng, out, in0, in1, op, reads, writes):
            return P.op(eng, lambda e: e.tensor_tensor(out=out, in0=in0, in1=in1, op=op), reads=reads, writes=writes)

        def STT(out, in0, scalar, in1, op0, op1, reads, writes):
            return P.op("dve", lambda e: e.scalar_tensor_tensor(out=out, in0=in0, scalar=scalar, in1=in1, op0=op0, op1=op1),
                        reads=reads, writes=writes)

        def CP(eng, out, in_, reads, writes):
            return P.op(eng, lambda e: e.tensor_copy(out=out, in_=in_), reads=reads, writes=writes)

        def MSET(eng, ap, val, writes):
            return P.op(eng, lambda e: e.memset(ap, val), writes=writes)

        ident = sb(top, "ident", [128, 128], F32); b_ident = Buf()
        identb = sb(top, "identb", [128, 128], BF16); b_identb = Buf()
        maskb = sb(top, "maskb_s", [128, 128], BF16); b_maskb = Buf()
        onesf = sb(top, "onesf", [128, 512], F32); b_onesf = Buf()
        modcol = sb(top, "modcol", [128, 48], F32); b_modcol = Buf()
        gs1 = sb(top, "gs1", [128, KC], F32); b_gs1 = Buf()
        gs2 = sb(top, "gs2", [128, KC], F32); b_gs2 = Buf()
        cbc = sb(top, "cbc", [128, KC], F32); b_cbc = Buf()
        lngc = sb(top, "lngc", [128, KC], F32); b_lngc = Buf()
        lnbc = sb(top, "lnbc", [128, KC], F32); b_lnbc = Buf()
        gqk = sb(top, "gqk", [DH, 1], F32); b_gqk = Buf()
        epsc = sb(top, "epsc", [128, 1], F32); b_epsc = Buf()
        onec = sb(top, "onec", [128, 1], F32); b_onec = Buf()

        DMA("sp", ident[:], ident_d, [], [b_ident], d_misc)
        DMA("pool", identb[:], ident_d, [], [b_identb], d_misc)
        DMA("pool", maskb[:], mask_d, [], [b_maskb], d_misc)
        DMA("sp", cbc[:], cb_d, [], [b_cbc], d_misc)
        DMA("sp", lngc[:], lng_d, [], [b_lngc], d_misc)
        DMA("sp", lnbc[:], lnb_d, [], [b_lnbc], d_misc)
        MSET("dve", onesf[:], 1.0, [b_onesf])
        MSET("dve", epsc[:], EPS, [b_epsc])
        MSET("dve", onec[:], 1.0, [b_onec])

        stA = top.enter_context(ExitStack())
        hT = sb(stA, "hT", [128, KC, S], BF16)
        b_hT = [Buf() for _ in range(8)]

        with ExitStack() as st:
            pm = psum(st, "pm"); b_pm = PBuf()
            pc = psum(st, "pc"); b_pc = PBuf()
            ptr = [(psum(st, "ptrA%d" % i), PBuf(), psum(st, "ptrB%d" % i), PBuf()) for i in range(2)]
            ccol = sb(st, "ccol", [128, KC], F32); b_ccol = Buf()
            cact = sb(st, "cact", [128, KC], F32); b_cact = Buf()
            badar = sb(st, "badar", [1, 6 * D], F32); b_badar = Buf()
            modrow = sb(st, "modrow", [1, 6 * D], F32); b_modrow = Buf()
            n1gc = sb(st, "n1gc", [128, KC], F32); b_n1gc = Buf()
            n2gc = sb(st, "n2gc", [128, KC], F32); b_n2gc = Buf()
            qgc = sb(st, "qgc", [DH, 1], F32); b_qgc = Buf()
            kgc = sb(st, "kgc", [DH, 1], F32); b_kgc = Buf()
            wst = Ring([(sb(st, "wst%d" % i, [128, KC, 512], F32), Buf(), P.dsem("wst%d" % i)) for i in range(2)])
            xt = Ring([(sb(st, "xt%d" % i, [128, D], F32), Buf(), P.dsem("xt%d" % i)) for i in range(4)])
            xn = Ring([(sb(st, "xn%d" % i, [128, D], F32), Buf()) for i in range(3)])
            junk = sb(st, "junk", [128, D], BF16); b_junk = Buf()
            ss = Ring([(sb(st, "ss%d" % i, [128, 1], F32), Buf()) for i in range(4)])
            sd = Ring([(sb(st, "sd%d" % i, [128, 1], F32), Buf()) for i in range(4)])
            rs = Ring([(sb(st, "rs%d" % i, [128, 1], F32), Buf()) for i in range(4)])

            DMA("sp", ccol[:], ccol_d, [], [b_ccol], d_misc)
            DMA("sp", badar[:], bada_d, [], [b_badar], d_misc)
            DMA("sp", n1gc[:], n1g_d, [], [b_n1gc], d_misc)
            DMA("sp", n2gc[:], n2g_d, [], [b_n2gc], d_misc)
            DMA("sp", qgc[:], qg_d, [], [b_qgc], d_misc)
            DMA("sp", kgc[:], kg_d, [], [b_kgc], d_misc)
            ACT(cact[:], ccol[:], AF.Silu, [b_ccol], [b_cact])
            STT(gqk[:], qgc[:], 0.125, kgc[:], ALU.mult, ALU.mult, [b_qgc, b_kgc], [b_gqk])

            wada_v = wada_d.rearrange("(kc p) n -> p kc n", p=128)

            def mod_tile(n):
                w_t, w_b, w_s = wst.next()
                DMA("sp", w_t[:], wada_v[:, :, n * 512:(n + 1) * 512], [], [w_b], w_s)
                for kc in range(KC):
                    MM(pm[0:1, :], cact[:, kc:kc + 1], w_t[:, kc, :], kc == 0, kc == KC - 1, [b_cact, w_b], [b_pm])
                TT("dve", modrow[0:1, n * 512:(n + 1) * 512], pm[0:1, :], badar[0:1, n * 512:(n + 1) * 512], ALU.add,
                   [b_pm, b_badar], [b_modrow])

            def mod_cols(j0, j1):
                for j in range(j0, j1):
                    MM(pc[:, j:j + 1], modrow[0:1, j * 128:(j + 1) * 128], onec[0:1, 0:1], True, True,
                       [b_modrow, b_onec], [b_pc], skip=True)
                CP("dve", modcol[:, j0:j1], pc[:, j0:j1], [b_pc], [b_modcol])

            for n in range(4):
                mod_tile(n)
            mod_cols(0, 16)
            STT(gs1[:], modcol[:, 8:16], 1.0, n1gc[:], ALU.add, ALU.mult, [b_modcol, b_n1gc], [b_gs1])

            st1 = {}

            def stage1(i):
                x_t, x_b, x_s = xt.next()
                DMA("sp", x_t[:], x_d[i * 128:(i + 1) * 128, :], [], [x_b], x_s)
                ss_t, ss_b = ss.next()
                sd_t, sd_b = sd.next()
                rs_t, rs_b = rs.next()
                ACT(junk[:], x_t[:], AF.Square, [x_b], [b_junk, ss_b], accum=ss_t[:])
                ACT(sd_t[:], ss_t[:], AF.Sqrt, [ss_b, b_epsc], [sd_b], bias=epsc[:], scale=1.0 / D)
                P.op("dve", lambda e, o=rs_t, a=sd_t: e.reciprocal(out=o[:], in_=a[:]), reads=[sd_b], writes=[rs_b])
                xn_t, xn_b = xn.next()
                TS("pool", xn_t[:], x_t[:], rs_t[:], 1.0, ALU.mult, ALU.mult, [x_b, rs_b], [xn_b])
                st1[i] = (xn_t, xn_b)

            def stage2(i):
                xn_t, xn_b = st1.pop(i)
                pA, bA, pB, bB = ptr[i % 2]
                for kc in range(KC):
                    pp, bp = (pA, bA) if kc < 4 else (pB, bB)
                    TR(pp[:, (kc % 4) * 128:(kc % 4 + 1) * 128], xn_t[:, kc * 128:(kc + 1) * 128], ident[:],
                       [xn_b, b_ident], [bp])
                for kc in range(KC):
                    pp, bp = (pA, bA) if kc < 4 else (pB, bB)
                    src = pp[:, (kc % 4) * 128:(kc % 4 + 1) * 128]
                    dst = hT[:, kc, i * 128:(i + 1) * 128]
                    if kc < 4:
                        ACT(dst, src, AF.Identity, [bp, b_gs1, b_modcol], [], dwrites=[b_hT[i // 4]],
                            bias=modcol[:, kc:kc + 1], scale=gs1[:, kc:kc + 1])
                    else:
                        TS("dve", dst, src, gs1[:, kc:kc + 1], modcol[:, kc:kc + 1], ALU.mult, ALU.add,
                           [bp, b_gs1, b_modcol], [], dwrites=[b_hT[i // 4]])

            NTI = S // 128
            stage1(0)
            stage1(1)
            for i in range(NTI):
                if i + 2 < NTI:
                    stage1(i + 2)
                stage2(i)
                if i % 3 == 2 and 4 + i // 3 < 12:
                    mod_tile(4 + i // 3)
            mod_cols(16, 48)
            STT(gs2[:], modcol[:, 32:40], 1.0, n2gc[:], ALU.add, ALU.mult, [b_modcol, b_n2gc], [b_gs2])
            DMA("sp", modscr, modrow[:], [b_modrow], [b_modscr], d_misc)

            P.barrier()
        with ExitStack() as st:
            pf = psum(st, "pf"); b_pf = PBuf()
            wf = sb(st, "wf", [128, KC, H], BF16); b_wf = Buf()
            nbf = sb(st, "nbf", [H, 1], F32); b_nbf = Buf()
            bfc = sb(st, "bfc", [H, 1], F32); b_bfc = Buf()
            ef = sb(st, "ef", [H, 512], F32); b_ef = Buf()
            spf = sb(st, "spf", [H, S], F32); b_spf = Buf()
            ncum = sb(st, "ncum", [H, S], F32); b_ncum = Buf()
            res = sb(st, "res", [H, S], F32); b_res = Buf()
            fpos = sb(st, "fpos", [H, 3, S], BF16); b_fpos = Buf()
            fneg = sb(st, "fneg", [H, 3, S], BF16); b_fneg = Buf()
            DMA("pool", wf[:], win_d[:, OFF_F:OFF_F + H].rearrange("(kc p) n -> p kc n", p=128), [], [b_wf], d_misc)
            DMA("sp", bfc[:], bfor_d, [], [b_bfc], d_misc)
            TS("dve", nbf[:], bfc[:], -1.0, None, ALU.mult, None, [b_bfc], [b_nbf])
            for t in range(8):
                for kc in range(KC):
                    MM(pf[0:H, :], wf[:, kc, :], hT[:, kc, t * 512:(t + 1) * 512], kc == 0, kc == KC - 1,
                       [b_wf, b_hT[t]], [b_pf])
                ACT(ef[:], pf[0:H, :], AF.Exp, [b_pf, b_nbf], [b_ef], bias=nbf[:], scale=-1.0)
                ACT(spf[:, t * 512:(t + 1) * 512], ef[:], AF.Ln, [b_ef, b_onec], [b_spf], bias=onec[0:H, :])
            for t in range(8):
                sl = slice(t * 512, (t + 1) * 512)
                init = 0.0 if t == 0 else ncum[:, t * 512 - 1:t * 512]
                P.op("dve", lambda e, sl=sl, init=init: e.tensor_tensor_scan(out=ncum[:, sl], data0=onesf[0:H, :], data1=spf[:, sl],
                                                                        initial=init, op0=ALU.mult, op1=ALU.add),
                     reads=[b_onesf, b_spf, b_ncum], writes=[b_ncum])
            CP("dve", fpos[:, 0, :], ncum[:], [b_ncum], [b_fpos])
            TT("dve", res[:], ncum[:], fpos[:, 0, :], ALU.subtract, [b_ncum, b_fpos], [b_res])
            CP("dve", fpos[:, 1, :], res[:], [b_res], [b_fpos])
            TT("dve", res[:], res[:], fpos[:, 1, :], ALU.subtract, [b_res, b_fpos], [b_res])
            CP("dve", fpos[:, 2, :], res[:], [b_res], [b_fpos])
            TS("pool", fneg[:], fpos[:], -1.0, 1.0, ALU.mult, ALU.mult, [b_fpos], [b_fneg])
            DMA("sp", fscr[0:3].rearrange("r h s -> h r s"), fpos[:], [b_fpos], [b_fscr], d_misc)
            DMA("sp", fscr[3:6].rearrange("r h s -> h r s"), fneg[:], [b_fneg], [b_fscr], d_misc)
            if debug:
                DMA("sp", dbg_hT, hT[:], b_hT, [], d_out)
            P.barrier()

        if upto >= 2:
            stB = stA.enter_context(ExitStack())
            BIG = sb(stB, "BIG", [128, KC, S], BF16)
            b_big = [Buf() for _ in range(8)]

        if upto >= 2:
            with ExitStack() as st:
                pa = [(psum(st, "pa%d" % i), PBuf()) for i in range(2)]
                pb = [(psum(st, "pb%d" % i), PBuf()) for i in range(2)]
                py = [(psum(st, "py%d" % i), PBuf()) for i in range(2)]
                cw = sb(st, "cw", [128, KC, CK], F32); b_cw = Buf()
                DMA("sp", cw[:], cwT_d, [], [b_cw], d_misc)
                wgl = Ring([(sb(st, "wgl%d" % i, [128, KC, 2, 128], BF16), Buf(), P.dsem("wgl%d" % i)) for i in range(2)])
                dg = Ring([(sb(st, "dg%d" % i, [128, CK, 128], BF16), Buf()) for i in range(2)])
                ub = Ring([(sb(st, "ub%d" % i, [128, 30 + S], BF16), Buf()) for i in range(2)])
                sg = Ring([(sb(st, "sg%d" % i, [128, 512], F32), Buf()) for i in range(2)])
                for (u_t, u_b) in ub.items:
                    MSET("pool", u_t[:, 0:30], 0.0, [u_b])
                for c in range(KC):
                    g_t, g_b, g_s = wgl.next()
                    DMA("pool", g_t[:, :, 0, :], win_d[:, OFF_GLU + c * 128:OFF_GLU + (c + 1) * 128].rearrange("(kc p) n -> p kc n", p=128),
                        [], [g_b], g_s)
                    DMA("pool", g_t[:, :, 1, :], win_d[:, OFF_GLU + D + c * 128:OFF_GLU + D + (c + 1) * 128].rearrange("(kc p) n -> p kc n", p=128),
                        [], [g_b], g_s)
                    if upto >= 5 and c == 1:
                        d_wscr = P.dsem("wscr", bg=True)
                        DMA("pool", wps.rearrange("r (h n) -> r h n", h=1), wap_d.rearrange("r (h n) -> r h n", h=1), [], [b_wscr], d_wscr)
                        DMA("pool", wgs.rearrange("r (h n) -> r h n", h=1), win_d[:, OFF_GATE:OFF_GATE + D].rearrange("r (h n) -> r h n", h=1),
                            [], [b_wscr], d_wscr)
                        DMA("pool", wos.rearrange("r (h n) -> r h n", h=1), wout_d.rearrange("r (h n) -> r h n", h=1), [], [b_wscr], d_wscr)
                        for q in range(4):
                            DMA("pool", w1s[q * 256:(q + 1) * 256, :].rearrange("r (h n) -> r h n", h=2),
                                w1_d[q * 256:(q + 1) * 256, :].rearrange("r (h n) -> r h n", h=2), [], [b_wscr], d_wscr)
                        for q in range(4):
                            DMA("pool", w2s[q * 1024:(q + 1) * 1024, :].rearrange("r (h n) -> r h n", h=1),
                                w2_d[q * 1024:(q + 1) * 1024, :].rearrange("r (h n) -> r h n", h=1), [], [b_wscr], d_wscr)

                    d_t, d_b = dg.next()
                    for k in range(CK):
                        TS("dve", d_t[:, k, :], ident[:], cw[:, c, k:k + 1], None, ALU.mult, None, [b_ident, b_cw], [d_b])
                    u_t, u_b = ub.next()
                    for t in range(8):
                        (pa_t, pa_b), (pb_t, pb_b) = pa[t % 2], pb[t % 2]
                        for kc in range(KC):
                            MM(pa_t[:, :], g_t[:, kc, 0, :], hT[:, kc, t * 512:(t + 1) * 512], kc == 0, kc == KC - 1,
                               [g_b, b_hT[t]], [pa_b])
                        for kc in range(KC):
                            MM(pb_t[:, :], g_t[:, kc, 1, :], hT[:, kc, t * 512:(t + 1) * 512], kc == 0, kc == KC - 1,
                               [g_b, b_hT[t]], [pb_b])
                        s_t, s_b = sg.next()
                        ACT(s_t[:], pb_t[:, :], AF.Sigmoid, [pb_b], [s_b])
                        TT("dve", u_t[:, 30 + t * 512:30 + (t + 1) * 512], pa_t[:, :], s_t[:], ALU.mult, [pa_b, s_b], [u_b])
                    for t in range(8):
                        y_t, y_b = py[t % 2]
                        for k in range(CK):
                            MM(y_t[:, :], d_t[:, k, :], u_t[:, t * 512 + k:t * 512 + k + 512], k == 0, k == CK - 1,
                               [d_b, u_b], [y_b])
                        ACT(BIG[:, c, t * 512:(t + 1) * 512], y_t[:, :], AF.Identity, [y_b, b_cbc], [b_big[t]],
                            bias=cbc[:, c:c + 1])
                P.barrier()
            with ExitStack() as st:
                pa = [(psum(st, "pa2%d" % i), PBuf()) for i in range(2)]
                pb = [(psum(st, "pb2%d" % i), PBuf()) for i in range(2)]
                wcp = sb(st, "wcp", [128, KC, D], BF16); b_wcp = Buf(); d_wcp = P.dsem("wcp")
                wgb = sb(st, "wgb", [128, KC, D], BF16); b_wgb = Buf(); d_wgb = P.dsem("wgb")
                for kc in range(KC):
                    DMA("pool", wcp[:, kc, :], wcp_d[kc * 128:(kc + 1) * 128, :], [], [b_wcp], d_wcp)
                    DMA("pool", wgb[:, kc, :], win_d[kc * 128:(kc + 1) * 128, OFF_GATE + D:OFF_GATE + 2 * D], [], [b_wgb], d_wgb)
                pmean = psum(st, "pmean"); b_pmean = PBuf()
                pmsq = psum(st, "pmsq"); b_pmsq = PBuf()
                onesb = sb(st, "onesb", [128, 128], BF16); b_onesb = Buf()
                MSET("dve", onesb[:], 1.0 / D, [b_onesb])
                ysq = sb(st, "ysq", [128, KC, 512], BF16); b_ysq = Buf()
                mean_s = sb(st, "mean_s", [128, 512], F32); b_mean = Buf()
                var_s = sb(st, "var_s", [128, 512], F32); b_var = Buf()
                rstd_s = var_s; b_rstd = b_var
                zc = Ring([(sb(st, "zc%d" % i, [128, 512], F32), Buf()) for i in range(2)])
                zbr = Ring([(sb(st, "zb%d" % i, [128, KC, 512], BF16), Buf()) for i in range(2)])
                sgb = Ring([(sb(st, "sgb%d" % i, [128, 512], F32), Buf()) for i in range(1)])
                bgt = Ring([(sb(st, "bgt%d" % i, [128, KC, 512], BF16), Buf(), P.dsem("bgt%d" % i)) for i in range(1)])
                zs = Ring([(sb(st, "zs%d" % i, [128, 512], BF16), Buf()) for i in range(2)])
                zcur = {}

                def S1(t):
                    sl = slice(t * 512, (t + 1) * 512)
                    TT("pool", ysq[:], BIG[:, :, sl], BIG[:, :, sl], ALU.mult, [b_big[t]], [b_ysq])
                    for c in range(KC):
                        MM(pmean[:, :], onesb[:], BIG[:, c, sl], c == 0, c == KC - 1, [b_onesb, b_big[t]], [b_pmean])
                    for c in range(KC):
                        MM(pmsq[:, :], onesb[:], ysq[:, c, :], c == 0, c == KC - 1, [b_onesb, b_ysq], [b_pmsq])

                def S2_pieces(t):
                    sl = slice(t * 512, (t + 1) * 512)
                    zb_t, zb_b = zbr.next()
                    zcur[t] = (zb_t, zb_b)

                    def stats():
                        CP("dve", mean_s[:], pmean[:, :], [b_pmean], [b_mean])
                        TT("dve", var_s[:], mean_s[:], mean_s[:], ALU.mult, [b_mean], [b_var])
                        TT("dve", var_s[:], pmsq[:, :], var_s[:], ALU.subtract, [b_pmsq, b_var], [b_var])
                        ACT(var_s[:], var_s[:], AF.Sqrt, [b_var, b_epsc], [b_var], bias=epsc[:])
                        P.op("dve", lambda e: e.reciprocal(out=rstd_s[:], in_=var_s[:]), reads=[b_var], writes=[b_rstd])

                    def zchunk(c):
                        z_t, z_b = zc.next()
                        s_t, s_b = zs.next()
                        TT("pool", z_t[:], BIG[:, c, sl], mean_s[:], ALU.subtract, [b_big[t], b_mean], [z_b])
                        TT("pool", z_t[:], z_t[:], rstd_s[:], ALU.mult, [z_b, b_rstd], [z_b])
                        ACT(s_t[:], z_t[:], AF.Sigmoid, [z_b, b_lngc, b_lnbc], [s_b],
                            bias=lnbc[:, c:c + 1], scale=lngc[:, c:c + 1])
                        TS("dve", z_t[:], z_t[:], lngc[:, c:c + 1], lnbc[:, c:c + 1], ALU.mult, ALU.add, [z_b, b_lngc, b_lnbc], [z_b])
                        P.op("dve", lambda e, o=zb_t[:, c, :], a=z_t[:], b_=s_t[:]: e.tensor_tensor(out=o, in0=a, in1=b_, op=ALU.mult),
                             reads=[z_b, s_b], dwrites=[zb_b])
                    return [stats, lambda: (zchunk(0), zchunk(1)), lambda: (zchunk(2), zchunk(3)), lambda: zchunk(4),
                            lambda: zchunk(5), lambda: zchunk(6), lambda: zchunk(7), lambda: None]

                def Mo(t, o, o_t, o_b):
                    sl = slice(t * 512, (t + 1) * 512)
                    zb_t, zb_b = zcur[t]
                    (pa_t, pa_b), (pb_t, pb_b) = pa[o % 2], pb[o % 2]
                    for c in range(KC):
                        MM(pa_t[:, :], wcp[:, c, o * 128:(o + 1) * 128], zb_t[:, c, :], c == 0, c == KC - 1,
                           [b_wcp, zb_b], [pa_b])
                    for kc in range(KC):
                        MM(pb_t[:, :], wgb[:, kc, o * 128:(o + 1) * 128], hT[:, kc, sl], kc == 0, kc == KC - 1,
                           [b_wgb, b_hT[t]], [pb_b])
                    s_t, s_b = sgb.next()
                    ACT(s_t[:], pb_t[:, :], AF.Sigmoid, [pb_b], [s_b])
                    TT("dve", o_t[:, o, :], pa_t[:, :], s_t[:], ALU.mult, [pa_b, s_b], [o_b])

                S1(0)
                for p_ in S2_pieces(0):
                    p_()
                for t in range(8):
                    sl = slice(t * 512, (t + 1) * 512)
                    pieces = []
                    if t + 1 < 8:
                        S1(t + 1)
                        pieces = S2_pieces(t + 1)
                    o_t, o_b, o_s = bgt.next()
                    for o in range(KC):
                        Mo(t, o, o_t, o_b)
                        if pieces:
                            pieces.pop(0)()
                    DMA("sp", bgscr[:, :, sl], o_t[:], [o_b], [b_bg[t]], o_s)
                P.barrier()

        if upto >= 3:
            with ExitStack() as st:
                pS = Ring([(psum(st, "pS%d" % i), PBuf()) for i in range(3)])
                pO = Ring([(psum(st, "pO%d" % i), PBuf()) for i in range(2)])
                pBc = psum(st, "pBc"); b_pBc = PBuf()
                pT = pBc; b_pT = b_pBc
                pQ = Ring([(psum(st, "pQ%d" % i), PBuf()) for i in range(2)])
                qaug = [(sb(st, "qaug%d" % i, [70, S], BF16), Buf(), P.dsem("qaug%d" % i)) for i in range(2)]
                kaug = [(sb(st, "kaug%d" % i, [70, S], BF16), Buf(), P.dsem("kaug%d" % i)) for i in range(2)]
                vh = [(sb(st, "vh0", [128, S // 128, 65], BF16), Buf()), (sb(st, "vh1", [128, S // 128, 128], BF16), Buf())]
                wqkv = Ring([(sb(st, "wqkv%d" % i, [128, KC, 3, 128], BF16), Buf(), P.dsem("wqkv%d" % i)) for i in range(2)])
                qkraw = Ring([(sb(st, "qkraw%d" % i, [128, 8, 2, DH], F32), Buf()) for i in range(2)])
                sqs = sb(st, "sqs", [128, 8, 2, DH], F32); b_sqs = Buf()
                ssq = sb(st, "ssq", [128, 16], F32); b_ssq = Buf()
                rsq = sb(st, "rsq", [128, 16], F32); b_rsq = Buf()
                nhalf = sb(st, "nhalf", [128, 16], F32); b_nhalf = Buf()
                MSET("pool", nhalf[:], -0.5, [b_nhalf])
                PT = Ring([(sb(st, "PT%d" % i, [128, 512], BF16), Buf()) for i in range(4)])
                rden = Ring([(sb(st, "rden%d" % i, [128, 512], F32), Buf()) for i in range(1)])
                bcs = Ring([(sb(st, "bcs%d" % i, [128, 512], F32), Buf()) for i in range(1)])
                for i in range(2):
                    MSET("pool", qaug[i][0][64:70, :], 1.0, [qaug[i][1]])
                    MSET("pool", kaug[i][0][64:70, :], 1.0, [kaug[i][1]])
                MSET("pool", vh[0][0][:, :, 64:65], 1.0, [vh[0][1]])
                MSET("pool", vh[1][0][:, :, 0:64], 0.0, [vh[1][1]])
                MSET("pool", vh[1][0][:, :, 0:1], 1.0, [vh[1][1]])
                cur_w = [None]
                def inproj_gen(h):
                    e = h % 2
                    if e == 0:
                        w_t, w_b, w_s = wqkv.next()
                        c0 = (h // 2) * 128
                        for j, off in enumerate((OFF_Q, OFF_K, OFF_V)):
                            DMA("pool", w_t[:, :, j, :], win_d[:, off + c0:off + c0 + 128].rearrange("(kc p) n -> p kc n", p=128),
                                [], [w_b], w_s)
                        cur_w[0] = (w_t, w_b)
                    w_t, w_b = cur_w[0]
                    q_t, q_b, q_s = qaug[e]
                    k_t, k_b, k_s = kaug[e]
                    v_t, v_b = vh[e]
                    DMA("sp", q_t[64:67, :], fscr[3:6, h, :], [b_fscr], [q_b], q_s)
                    DMA("sp", k_t[67:70, :], fscr[0:3, h, :], [b_fscr], [k_b], k_s)
                    yield
                    voff = 0 if e == 0 else 64
                    pending_tr = []

                    def tr_steps(grp, qr_t, qr_b):
                        steps = []
                        for g2 in range(2):
                            g = grp * 2 + g2
                            for which in range(2):
                                def step(g=g, g2=g2, which=which):
                                    for ii in range(4):
                                        TR(pT[0:64, ii * 128:(ii + 1) * 128], qr_t[:, g2 * 4 + ii, which, :], ident[:],
                                           [qr_b, b_ident], [b_pT])
                                    if which == 0:
                                        CP("dve", q_t[0:64, g * 512:(g + 1) * 512], pT[0:64, :], [b_pT], [q_b])
                                    else:
                                        TS("dve", k_t[0:64, g * 512:(g + 1) * 512], pT[0:64, :], gqk[:], None, ALU.mult, None,
                                           [b_pT, b_gqk], [k_b])
                                steps.append(step)
                        return steps

                    for grp in range(4):
                        qr_t, qr_b = qkraw.next()
                        qk3 = qr_t[:].rearrange("p a b d -> p (a b) d")
                        for il in range(8):
                            i = grp * 8 + il
                            p_t, p_b = pQ.next()
                            for kc in range(KC):
                                MM(p_t[:, 0:192].rearrange("p (a b) -> p a b", a=3), hT[:, kc, i * 128:(i + 1) * 128],
                                   w_t[:, kc, :, e * 64:(e + 1) * 64], kc == 0, kc == KC - 1, [b_hT[i // 4], w_b], [p_b])
                            CP("dve", qr_t[:, il, :, :], p_t[:, 0:128].rearrange("p (a b) -> p a b", a=2), [p_b], [qr_b])
                            CP("dve", v_t[:, i, voff:voff + 64], p_t[:, 128:192], [p_b], [v_b])
                            if pending_tr and il % 2 == 1:
                                pending_tr.pop(0)()
                            yield
                        TT("pool", sqs[:], qr_t[:], qr_t[:], ALU.mult, [qr_b], [b_sqs])
                        P.op("dve", lambda e_: e_.tensor_reduce(out=ssq[:], in_=sqs[:].rearrange("p a b d -> p (a b) d"), axis=AX.X, op=ALU.add),
                             reads=[b_sqs], writes=[b_ssq])
                        TS("pool", ssq[:], ssq[:], 1.0 / DH, EPS, ALU.mult, ALU.add, [b_ssq], [b_ssq])
                        TT("pool", rsq[:], ssq[:], nhalf[:], ALU.pow, [b_ssq, b_nhalf], [b_rsq])
                        TT("pool", qk3, qk3, rsq[:].unsqueeze(2).to_broadcast([128, 16, DH]), ALU.mult, [qr_b, b_rsq], [qr_b])
                        yield
                        pending_tr = tr_steps(grp, qr_t, qr_b)
                    yield
                    yield
                    while pending_tr:
                        pending_tr.pop(0)()
                        yield

                LAG = 3
                pend = []
                defer = []
                otile = {}

                def emit_S(h, j, i):
                    e = h % 2
                    q_t, q_b, _ = qaug[e]
                    k_t, k_b, _ = kaug[e]
                    s_t, s_b = pS.next()
                    kl = k_t[0:70, i * 128:(i + 1) * 128]
                    if i < 4 * j:
                        c0 = 0
                        MM(s_t[:, :], kl, q_t[0:70, j * 512:(j + 1) * 512], True, True, [k_b, q_b], [s_b])
                    else:
                        c0 = (i - 4 * j) * 128
                        MM(s_t[:, c0:c0 + 128], kl, q_t[0:70, j * 512 + c0:j * 512 + c0 + 128], True, False,
                           [k_b, q_b], [s_b], skip=True)
                        MM(s_t[:, c0:c0 + 128], identb[:], maskb[:], False, True, [b_identb, b_maskb], [s_b], skip=True)
                        if c0 + 128 < 512:
                            MM(s_t[:, c0 + 128:512], kl, q_t[0:70, j * 512 + c0 + 128:(j + 1) * 512], True, True,
                               [k_b, q_b], [s_b], skip=True)
                    p_t, p_b = PT.next()
                    ACT(p_t[:, c0:512], s_t[:, c0:512], AF.Exp, [s_b], [p_b])
                    pend.append((h, j, i, c0, p_t, p_b))

                def emit_PV():
                    h, j, i, c0, p_t, p_b = pend.pop(0)
                    e = h % 2
                    c = h // 2
                    v_t, v_b = vh[e]
                    M = 65 if e == 0 else 128
                    p0 = 64 if e == 0 else 0
                    o0 = 0 if e == 0 else 64
                    nblk = 4 * (j + 1)
                    if i == 0:
                        otile[(h, j)] = pO.next()
                    o_t, o_b = otile[(h, j)]
                    MM(o_t[0:M, c0:512], v_t[:, i, 0:M], p_t[:, c0:512], i == 0, i == nblk - 1, [v_b, p_b], [o_b], skip=True)
                    if i == nblk - 1:
                        del otile[(h, j)]
                        r_t, r_b = rden.next()
                        ACT(r_t[p0:p0 + 1, :], o_t[p0:p0 + 1, :], AF.Ln, [o_b], [r_b])
                        ACT(r_t[p0:p0 + 1, :], r_t[p0:p0 + 1, :], AF.Exp, [r_b], [r_b], scale=-1.0)

                        def tail():
                            MM(pBc[:, :], onesf[p0:p0 + 1, 0:128], r_t[p0:p0 + 1, :], True, True, [b_onesf, r_b], [b_pBc])
                            b_t, b_b = bcs.next()
                            CP("dve", b_t[o0:o0 + 64, :], pBc[o0:o0 + 64, :], [b_pBc], [b_b])
                            TT("dve", BIG[o0:o0 + 64, c, j * 512:(j + 1) * 512], o_t[o0:o0 + 64, :], b_t[o0:o0 + 64, :], ALU.mult,
                               [o_b, b_b], [b_big[j]])
                        defer.append([3, tail])

                def tick_defer(force=False):
                    for d in list(defer):
                        d[0] -= 1
                        if d[0] <= 0 or force:
                            d[1]()
                            defer.remove(d)

                g0 = inproj_gen(0)
                for _ in g0:
                    pass
                for h in range(H):
                    nxt = inproj_gen(h + 1) if h + 1 < H else None
                    cnt = 0
                    for j in range(8):
                        for i in range(4 * (j + 1)):
                            emit_S(h, j, i)
                            if len(pend) > LAG:
                                emit_PV()
                            tick_defer()
                            cnt += 1
                            if nxt is not None and cnt % 3 == 0:
                                next(nxt, None)
                    if nxt is not None:
                        for _ in nxt:
                            pass
                while pend:
                    emit_PV()
                    tick_defer()
                tick_defer(force=True)
                tick_defer(force=True)
                if debug:
                    DMA("sp", dbg_ao, BIG[:], b_big, [], d_out)
                P.barrier()

        if upto >= 4:
            with ExitStack() as st:
                pa = [(psum(st, "p4a%d" % i), PBuf()) for i in range(2)]
                pb = [(psum(st, "p4b%d" % i), PBuf()) for i in range(2)]
                wap = sb(st, "wap", [128, KC, D], BF16); b_wap = Buf(); d_wap = P.dsem("wap")
                wga = sb(st, "wga", [128, KC, D], BF16); b_wga = Buf(); d_wga = P.dsem("wga")
                if upto >= 5:
                    DMA("sp", wap[:], wps.rearrange("(kc p) n -> p kc n", p=128), [b_wscr], [b_wap], d_wap)
                    DMA("sp", wga[:], wgs.rearrange("(kc p) n -> p kc n", p=128), [b_wscr], [b_wga], d_wga)
                else:
                    for kc in range(KC):
                        DMA("pool", wap[:, kc, :], wap_d[kc * 128:(kc + 1) * 128, :], [], [b_wap], d_wap)
                        DMA("pool", wga[:, kc, :], win_d[kc * 128:(kc + 1) * 128, OFF_GATE:OFF_GATE + D], [], [b_wga], d_wga)
                bgl = Ring([(sb(st, "bgl%d" % i, [128, KC, 512], BF16), Buf(), P.dsem("bgl%d" % i)) for i in range(2)])
                mgt = Ring([(sb(st, "mgt%d" % i, [128, KC, 512], BF16), Buf(), P.dsem("mgt%d" % i)) for i in range(2)])
                gb2 = sb(st, "gb2", [128, D], F32); b_gb2 = Buf()
                DMA("sp", gb2[:], modscr[0:1, 5 * D:6 * D].broadcast_to([128, D]), [b_modscr], [b_gb2], d_misc)

                def scale_w2(t_):
                    for k_ in range(4):
                        w2c_ = BIG[:, 2 * k_:2 * k_ + 2, t_ * 512:(t_ + 1) * 512]
                        TT("pool", w2c_, w2c_, gb2[:].rearrange("p (h n) -> p h n", h=2), ALU.mult, [b_big[t_], b_gb2], [b_big[t_]])
                sga = Ring([(sb(st, "sga%d" % i, [128, 512], F32), Buf()) for i in range(2)])
                tmp = Ring([(sb(st, "tmp4%d" % i, [128, 512], F32), Buf()) for i in range(2)])
                d_w1 = P.dsem("w1"); d_w2 = P.dsem("w2")
                b_w2 = [Buf() for _ in range(4)]
                w1v = hT
                w2v = BIG[:].rearrange("p c t -> p (c t)").rearrange("p (f n) -> p f n", n=D)
                nxt_l = bgl.next()
                DMA("sp", nxt_l[0][:], bgscr[:, :, 0:512], [b_bg[0]], [nxt_l[1]], nxt_l[2])
                for t in range(8):
                    sl = slice(t * 512, (t + 1) * 512)
                    l_t, l_b, l_s = nxt_l
                    m_t, m_b, m_s = mgt.next()
                    for o in range(KC):
                        (pa_t, pa_b), (pb_t, pb_b) = pa[o % 2], pb[o % 2]
                        for c in range(KC):
                            MM(pa_t[:, :], wap[:, c, o * 128:(o + 1) * 128], BIG[:, c, sl], c == 0, c == KC - 1,
                               [b_wap, b_big[t]], [pa_b])
                        for kc in range(KC):
                            MM(pb_t[:, :], wga[:, kc, o * 128:(o + 1) * 128], hT[:, kc, sl], kc == 0, kc == KC - 1,
                               [b_wga, b_hT[t]], [pb_b])
                        s_t, s_b = sga.next()
                        ACT(s_t[:], pb_t[:, :], AF.Sigmoid, [pb_b], [s_b])
                        t_t, t_b = tmp.next()
                        TT("dve", t_t[:], pa_t[:, :], s_t[:], ALU.mult, [pa_b, s_b], [t_b])
                        TT("dve", m_t[:, o, :], t_t[:], l_t[:, o, :], ALU.add, [t_b, l_b], [m_b])
                    DMA("sp", mscr[:, :, sl], m_t[:], [m_b], [b_ms[t]], m_s)
                    if t + 1 < 8:
                        nxt_l = bgl.next()
                        DMA("sp", nxt_l[0][:], bgscr[:, :, (t + 1) * 512:(t + 2) * 512], [b_bg[t + 1]], [nxt_l[1]], nxt_l[2])
                    if upto >= 5:
                        DMA("sp", hT[:, :, sl], w1s[:, sl].rearrange("(kc p) n -> p kc n", p=128), [b_wscr], [b_hT[t]], d_w1)
                        for f in range(4 * t, 4 * t + 4):
                            DMA("sp", BIG[:, (f % 4) * 2:(f % 4) * 2 + 2, sl],
                                w2s[f * 128:(f + 1) * 128, :].rearrange("p (h n) -> p h n", h=2), [b_wscr], [b_big[t]], d_w2)
                    if upto >= 5 and t >= 1:
                        scale_w2(t - 1)
                if upto >= 5:
                    scale_w2(7)
                P.barrier(exclude=(d_w1,))

        if upto >= 5:
            with ExitStack() as st:
                TT5 = 256
                NT5 = S // TT5
                wo = sb(st, "wo", [128, KC, D], BF16); b_wo = Buf(); d_wo = P.dsem("wo")
                w1 = w1v
                w2 = w2v
                identf = ident
                pw = Ring([(psum(st, "pw%d" % i), PBuf()) for i in range(2)])
                ph = Ring([(psum(st, "ph%d" % i), PBuf()) for i in range(2)])
                po = [(psum(st, "po%d" % i), PBuf()) for i in range(4)]
                mgl = Ring([(sb(st, "mgl%d" % i, [128, KC, TT5], BF16), Buf(), P.dsem("mgl%d" % i)) for i in range(2)])
                xl = Ring([(sb(st, "xl%d" % i, [128, D], F32), Buf(), P.dsem("xl%d" % i)) for i in range(2)])
                x1 = Ring([(sb(st, "x1_%d" % i, [128, D], F32), Buf()) for i in range(4)])
                xn2 = Ring([(sb(st, "xn2_%d" % i, [128, D], F32), Buf()) for i in range(2)])
                junk5 = sb(st, "junk5", [128, D], BF16); b_junk5 = Buf()
                ss5 = Ring([(sb(st, "ss5_%d" % i, [128, 1], F32), Buf()) for i in range(4)])
                sd5 = Ring([(sb(st, "sd5_%d" % i, [128, 1], F32), Buf()) for i in range(4)])
                rs5 = Ring([(sb(st, "rs5_%d" % i, [128, 1], F32), Buf()) for i in range(4)])
                h2T = Ring([(sb(st, "h2T%d" % i, [128, KC, TT5], BF16), Buf()) for i in range(2)])
                rl = Ring([(sb(st, "rl%d" % i, [128, TT5], F32), Buf()) for i in range(2)])
                aT = Ring([(sb(st, "aT%d" % i, [128, TT5], BF16), Buf()) for i in range(3)])
                ot = Ring([(sb(st, "ot%d" % i, [128, D], F32), Buf(), P.dsem("ot%d" % i)) for i in range(1)])

                gb_t, gb_b, _ = ot.items[0]
                DMA("sp", wo[:], wos.rearrange("(kc p) n -> p kc n", p=128), [b_wscr], [b_wo], d_wo)
                DMA("sp", gb_t[:], modscr[0:1, 2 * D:3 * D].broadcast_to([128, D]), [b_modscr], [gb_b], d_misc)
                for kc in range(KC):
                    TT("dve", wo[:, kc, :], wo[:, kc, :], gb_t[:], ALU.mult, [b_wo, gb_b], [b_wo])
                stA5 = {}

                def A1(tt):
                    m_t, m_b, m_s = mgl.next()
                    DMA("sp", m_t[:], mscr[:, :, tt * TT5:(tt + 1) * TT5], [b_ms[tt // 2]], [m_b], m_s)
                    x1s, ns = [], []
                    for sub in range(2):
                        tok0 = tt * TT5 + sub * 128
                        x_t, x_b, x_s = xl.next()
                        DMA("sp", x_t[:], x_d[tok0:tok0 + 128, :], [], [x_b], x_s)
                        x1_t, x1_b = x1.next()
                        x1s.append((x1_t, x1_b))
                        for half in range(2):
                            hs = slice(half * 512, (half + 1) * 512)
                            p_t, p_b = pw.next()
                            for kc in range(KC):
                                MM(p_t[:, :], m_t[:, kc, sub * 128:(sub + 1) * 128], wo[:, kc, hs],
                                   kc == 0, kc == KC - 1, [m_b, b_wo], [p_b])
                            TT("dve", x1_t[:, hs], p_t[:, :], x_t[:, hs], ALU.add, [p_b, x_b], [], ) if False else \
                                P.op("dve", lambda e, o=x1_t[:, hs], a=p_t[:, :], b_=x_t[:, hs]: e.tensor_tensor(out=o, in0=a, in1=b_, op=ALU.add),
                                     reads=[p_b, x_b], dwrites=[x1_b])
                        ss_t, ss_b = ss5.next()
                        sd_t, sd_b = sd5.next()
                        rs_t, rs_b = rs5.next()
                        ACT(junk5[:], x1_t[:], AF.Square, [x1_b], [b_junk5, ss_b], accum=ss_t[:])
                        ACT(sd_t[:], ss_t[:], AF.Sqrt, [ss_b, b_epsc], [sd_b], bias=epsc[:], scale=1.0 / D)
                        P.op("dve", lambda e, o=rs_t, a=sd_t: e.reciprocal(out=o[:], in_=a[:]), reads=[sd_b], writes=[rs_b])
                        n_t, n_b = xn2.next()
                        TS("pool", n_t[:], x1_t[:], rs_t[:], 1.0, ALU.mult, ALU.mult, [x1_b, rs_b], [n_b])
                        ns.append((n_t, n_b))
                    stA5[tt] = (x1s, ns)

                def A2(tt):
                    x1s, ns = stA5[tt]
                    h_t, h_b = h2T.next()
                    for sub in range(2):
                        n_t, n_b = ns[sub]
                        for half in range(2):
                            p_t, p_b = pw.next()
                            for k4 in range(4):
                                kc = half * 4 + k4
                                TR(p_t[:, k4 * 128:(k4 + 1) * 128], n_t[:, kc * 128:(kc + 1) * 128], ident[:], [n_b, b_ident], [p_b])
                            for k4 in range(4):
                                kc = half * 4 + k4
                                src_ = p_t[:, k4 * 128:(k4 + 1) * 128]
                                dst = h_t[:, kc, sub * 128:(sub + 1) * 128]
                                if half == 0:
                                    ACT(dst, src_, AF.Identity, [p_b, b_gs2, b_modcol], [], dwrites=[h_b],
                                        bias=modcol[:, 24 + kc:25 + kc], scale=gs2[:, kc:kc + 1])
                                else:
                                    TS("dve", dst, src_, gs2[:, kc:kc + 1], modcol[:, 24 + kc:25 + kc], ALU.mult, ALU.add,
                                       [p_b, b_gs2, b_modcol], [], dwrites=[h_b])
                    stA5[tt] = (x1s, ns, h_t, h_b)

                def Bst(tt):
                    x1s, ns, h_t, h_b = stA5.pop(tt)
                    pend = None
                    for f in range(FC + 1):
                        if f < FC:
                            p_t, p_b = ph.next()
                            for kc in range(KC):
                                MM(p_t[:, 0:TT5], w1[:, kc, f * 128:(f + 1) * 128], h_t[:, kc, :], kc == 0, kc == KC - 1,
                                   [b_hT[f // 4], h_b], [p_b])
                            r_t, r_b = rl.next()
                            ACT(r_t[:], p_t[:, 0:TT5], AF.Relu, [p_b], [r_b])
                            a_t, a_b = aT.next()
                            TT("dve", a_t[:], r_t[:], r_t[:], ALU.mult, [r_b], [a_b])
                        if pend is not None:
                            pf_, pa_t, pa_b = pend
                            w2c = BIG[:, (pf_ % 4) * 2:(pf_ % 4) * 2 + 2, (pf_ // 4) * 512:(pf_ // 4 + 1) * 512]
                            for sub in range(2):
                                for half in range(2):
                                    o_t, o_b = po[sub * 2 + half]
                                    MM(o_t[:, :], pa_t[:, sub * 128:(sub + 1) * 128], w2c[:, half, :],
                                       pf_ == 0, pf_ == FC - 1, [pa_b, b_big[pf_ // 4]], [o_b])
                        pend = (f, a_t, a_b) if f < FC else None
                        if f == 3 and tt + 1 < NT5:
                            A1(tt + 1)
                        if f == 18 and tt + 1 < NT5:
                            A2(tt + 1)
                    for sub in range(2):
                        tok0 = tt * TT5 + sub * 128
                        out_t, out_b, out_s = ot.next()
                        for half in range(2):
                            hs = slice(half * 512, (half + 1) * 512)
                            o_t, o_b = po[sub * 2 + half]
                            x1_t, x1_b = x1s[sub]
                            P.op("dve", lambda e, o=out_t[:, hs], a=o_t[:, :], b_=x1_t[:, hs]: e.tensor_tensor(out=o, in0=a, in1=b_, op=ALU.add),
                                 reads=[o_b, x1_b], dwrites=[out_b])
                        DMA("sp", out_d[tok0:tok0 + 128, :], out_t[:], [out_b], [], out_s)
                        if out_s not in P.final_waits:
                            P.final_waits.append(out_s)

                A1(0)
                A2(0)
                for tt in range(NT5):
                    Bst(tt)
        P.emit(nc, top)
    return nc


def _col(v):
    return np.ascontiguousarray(np.asarray(v, np.float32).reshape(KC, 128).T)


def make_in_maps(inputs, cores):
    x = np.asarray(inputs["x"], np.float32)
    c = np.asarray(inputs["c"], np.float32)
    shared = {
        "w_ada": np.ascontiguousarray(inputs["w_ada"][0], dtype=np.float32),
        "b_ada": np.ascontiguousarray(inputs["b_ada"][0].reshape(1, -1), dtype=np.float32),
        "n1g": _col(inputs["norm1_g"][0]),
        "n2g": _col(inputs["norm2_g"][0]),
        "w_in": np.ascontiguousarray(inputs["w_in"][0], dtype=np.float32),
        "bfor": np.ascontiguousarray(np.asarray(inputs["b_forget"][0], np.float32).reshape(H, 1)),
        "qg": np.ascontiguousarray(np.asarray(inputs["q_norm_g"][0], np.float32).reshape(DH, 1)),
        "kg": np.ascontiguousarray(np.asarray(inputs["k_norm_g"][0], np.float32).reshape(DH, 1)),
        "w_attn_proj": np.ascontiguousarray(inputs["w_attn_proj"][0], dtype=np.float32),
        "cwT": np.ascontiguousarray(np.asarray(inputs["conv_w"][0], np.float32).T.reshape(KC, 128, CK).transpose(1, 0, 2)),
        "cb": _col(inputs["conv_b"][0]),
        "lng": _col(inputs["conv_ln_g"][0]),
        "lnb": _col(inputs["conv_ln_b"][0]),
        "w_conv_proj": np.ascontiguousarray(inputs["w_conv_proj"][0], dtype=np.float32),
        "w_out": np.ascontiguousarray(inputs["w_out"][0], dtype=np.float32),
        "w_mlp1": np.ascontiguousarray(inputs["w_mlp1"][0], dtype=np.float32),
        "w_mlp2": np.ascontiguousarray(inputs["w_mlp2"][0], dtype=np.float32),
        "ident": np.eye(128, dtype=np.float32),
        "maskb": np.where(np.arange(128)[None, :] >= np.arange(128)[:, None], 0.0, -30000.0).astype(np.float32),
    }
    maps = []
    for b in cores:
        m = dict(shared)
        m["x"] = np.ascontiguousarray(x[b])
        m["ccol"] = _col(c[b])
        maps.append(m)
    return maps


def kernel(**inputs):
    nc = build(upto=5, debug=False)
    cores = list(range(8))
    in_maps = make_in_maps(inputs, cores)
    res = run_bass_kernel_spmd(nc, in_maps, core_ids=cores)
    out = np.stack([np.asarray(r["out"], dtype=np.float32) for r in res.results], axis=0)
    return out
```

```python
import os
from contextlib import ExitStack
import numpy as np
import concourse.bass as bass
import concourse.mybir as mybir
from concourse.bass_utils import run_bass_kernel_spmd

F32 = mybir.dt.float32
BF16 = mybir.dt.bfloat16
AF = mybir.ActivationFunctionType
ALU = mybir.AluOpType
AX = mybir.AxisListType

S = 4096
D = 1024
H = 16
DH = 64
KC = 8
DFF = 4096
FC = 32
CK = 31
EPS = 1e-6
DIN = 7184
OFF_Q, OFF_K, OFF_V, OFF_F, OFF_GLU, OFF_GATE = 0, 1024, 2048, 3072, 3088, 5136

ENGS = ("pe", "act", "dve", "pool", "sp")
RAW_ONLY = False


class Buf:
    __slots__ = ("name", "w", "rs", "dw")
    excl = False

    def __init__(self, name=""):
        self.name = name
        self.w = None
        self.rs = []
        self.dw = []


class PBuf(Buf):
    __slots__ = ()
    excl = True


class Op:
    __slots__ = ("eng", "fn", "deps", "sig", "val", "idx", "dma", "dsem", "dval")

    def __init__(self, eng, fn, dma):
        self.eng = eng
        self.fn = fn
        self.dma = dma
        self.deps = ([], [])
        self.sig = False
        self.val = 0
        self.dsem = None
        self.dval = 0


class DSem:
    __slots__ = ("name", "issued", "h", "bg")

    def __init__(self, name, bg=False):
        self.name = name
        self.issued = 0
        self.h = None
        self.bg = bg


class Prog:
    def __init__(self, same_engine_sync=True):
        self.ops = {e: [] for e in ENGS}
        self.dsems = []
        self.same_engine_sync = same_engine_sync
        self.final_waits = []

    def dsem(self, name, bg=False):
        s = DSem(name, bg)
        self.dsems.append(s)
        return s

    def op(self, eng, fn, reads=(), writes=(), dsem=None, dwrites=()):
        o = Op(eng, fn, dsem is not None)
        o.idx = len(self.ops[eng])
        deps = []
        raw = set()
        for b in reads:
            if b.w is not None:
                deps.append(b.w)
                raw.add(id(b.w))
            for d in b.dw:
                deps.append(d)
                raw.add(id(d))
            if b.excl:
                deps.extend(r for r in b.rs if r.eng != eng)
        for b in writes:
            if b.w is not None:
                deps.append(b.w)
            deps.extend(b.dw)
            deps.extend(b.rs)
        for b in dwrites:
            if b.w is not None:
                deps.append(b.w)
            deps.extend(b.rs)
        dd = {}
        cd = {}
        for d in deps:
            if d.dma:
                dd[id(d.dsem)] = (d.dsem, d.dsem.issued)
            else:
                if d.eng == eng and not o.dma:
                    if eng == "pe" or not self.same_engine_sync or (RAW_ONLY and id(d) not in raw):
                        continue
                p = cd.get(d.eng)
                if p is None or d.idx > p.idx:
                    cd[d.eng] = d
        o.deps = (list(cd.values()), list(dd.values()))
        for d in cd.values():
            d.sig = True
        if o.dma:
            dsem.issued += 16
            o.dsem = dsem
            o.dval = dsem.issued
        for b in reads:
            if o.dma:
                b.rs = [r for r in b.rs if not (r.dma and r.dsem is o.dsem)]
            else:
                b.rs = [r for r in b.rs if r.dma or r.eng != eng]
            b.rs.append(o)
        for b in writes:
            b.w = o
            b.rs = []
            b.dw = []
        for b in dwrites:
            b.dw = [r for r in b.dw if r.dma or r.eng != eng]
            b.dw.append(o)
        self.ops[eng].append(o)
        return o

    def barrier(self, exclude=()):
        lasts = []
        for e in ENGS:
            for o in reversed(self.ops[e]):
                if not o.dma and o.fn is not None:
                    lasts.append(o)
                    o.sig = True
                    break
        dds = [(s, s.issued) for s in self.dsems if s.issued > 0 and s not in exclude and not s.bg]
        for e in ENGS:
            o = Op(e, None, False)
            o.idx = len(self.ops[e])
            o.deps = ([d for d in lasts if d.eng != e], list(dds))
            self.ops[e].append(o)

    def emit(self, nc, stack):
        esem = {e: stack.enter_context(nc.semaphore("s_" + e)) for e in ENGS}
        for s in self.dsems:
            s.h = stack.enter_context(nc.semaphore("d_" + s.name))
        for e in ENGS:
            c = 0
            for o in self.ops[e]:
                if o.dma or o.fn is None:
                    continue
                if o.sig:
                    c += 1
                    o.val = c
        block = stack.enter_context(nc.Block())
        secs = {"pe": block.tensor, "act": block.scalar, "dve": block.vector,
                "pool": block.gpsimd, "sp": block.sync}
        for e in ENGS:
            ops = self.ops[e]
            final = self.final_waits if e == "sp" else []

            def section(eng, ops=ops, e=e, final=final):
                known = {}
                for o in ops:
                    cds, dds = o.deps
                    for d in cds:
                        key = ("e", d.eng)
                        if known.get(key, 0) >= d.val:
                            continue
                        known[key] = d.val
                        eng.wait_ge(esem[d.eng], d.val)
                    for (s, v) in dds:
                        key = ("d", id(s))
                        if known.get(key, 0) >= v:
                            continue
                        known[key] = v
                        eng.wait_ge(s.h, v)
                    if o.fn is None:
                        continue
                    ins = o.fn(eng)
                    if o.dma:
                        ins.then_inc(o.dsem.h, 16)
                    elif o.sig:
                        ins.then_inc(esem[e], 1)
                for s in final:
                    eng.wait_ge(s.h, s.issued)

            secs[e](section)


class Ring:
    def __init__(self, items):
        self.items = items
        self.i = 0

    def next(self):
        it = self.items[self.i % len(self.items)]
        self.i += 1
        return it


def build(upto=5, debug=False):
    nc = bass.Bass("TRN2", target_bir_lowering=False)
    P = Prog()

    def din(name, shape, dt=F32):
        return nc.dram_tensor(name, shape, dt, kind="ExternalInput").ap()

    x_d = din("x", [S, D])
    ccol_d = din("ccol", [128, KC])
    wada_d = din("w_ada", [D, 6 * D])
    bada_d = din("b_ada", [1, 6 * D])
    n1g_d = din("n1g", [128, KC])
    n2g_d = din("n2g", [128, KC])
    win_d = din("w_in", [D, DIN])
    bfor_d = din("bfor", [H, 1])
    qg_d = din("qg", [DH, 1])
    kg_d = din("kg", [DH, 1])
    wap_d = din("w_attn_proj", [D, D])
    cwT_d = din("cwT", [128, KC, CK])
    cb_d = din("cb", [128, KC])
    lng_d = din("lng", [128, KC])
    lnb_d = din("lnb", [128, KC])
    wcp_d = din("w_conv_proj", [D, D])
    wout_d = din("w_out", [D, D])
    w1_d = din("w_mlp1", [D, DFF])
    w2_d = din("w_mlp2", [DFF, D])
    ident_d = din("ident", [128, 128])
    mask_d = din("maskb", [128, 128])

    out_d = nc.dram_tensor("out", [S, D], F32, kind="ExternalOutput").ap()
    skind = "ExternalOutput" if debug else "Internal"
    modscr = nc.dram_tensor("modscr", [1, 6 * D], F32, kind=skind).ap()
    fscr = nc.dram_tensor("fscr", [6, H, S], BF16, kind=skind).ap()
    bgscr = nc.dram_tensor("bgscr", [128, KC, S], BF16, kind=skind).ap()
    mscr = nc.dram_tensor("mscr", [128, KC, S], BF16, kind=skind).ap()
    w1s = nc.dram_tensor("w1s", [D, DFF], BF16, kind="Internal").ap()
    w2s = nc.dram_tensor("w2s", [DFF, D], BF16, kind="Internal").ap()
    wos = nc.dram_tensor("wos", [D, D], BF16, kind="Internal").ap()
    wps = nc.dram_tensor("wps", [D, D], BF16, kind="Internal").ap()
    wgs = nc.dram_tensor("wgs", [D, D], BF16, kind="Internal").ap()
    wcs = nc.dram_tensor("wcs", [D, D], BF16, kind="Internal").ap()
    wbs = nc.dram_tensor("wbs", [D, D], BF16, kind="Internal").ap()
    b_wscr = Buf()
    b_wscr2 = Buf()
    b_modscr, b_fscr = Buf(), Buf()
    b_bg = [Buf() for _ in range(8)]
    b_ms = [Buf() for _ in range(8)]
    if debug:
        dbg_hT = nc.dram_tensor("dbg_hT", [128, KC, S], BF16, kind="ExternalOutput").ap()
        dbg_ao = nc.dram_tensor("dbg_ao", [128, KC, S], BF16, kind="ExternalOutput").ap()

    d_out = P.dsem("out")
    P.final_waits.append(d_out)
    d_misc = P.dsem("misc")

    with ExitStack() as top:
        def sb(st, name, shape, dt):
            return st.enter_context(nc.sbuf_tensor("s_" + name, shape, dt))

        def psum(st, name):
            return st.enter_context(nc.psum_tensor("p_" + name, [128, 512], F32))

        def DMA(q, out, in_, reads, writes, dsem):
            return P.op(q, lambda e: e.dma_start(out=out, in_=in_), reads=reads, writes=writes, dsem=dsem)

        def MM(out, lhsT, rhs, start, stop, reads, writes, skip=False):
            return P.op("pe", lambda e: e.matmul(out, lhsT=lhsT, rhs=rhs, start=start, stop=stop,
                                                 skip_group_check=skip), reads=reads, writes=writes)

        def TR(out, in_, ident, reads, writes):
            return P.op("pe", lambda e: e.transpose(out=out, in_=in_, identity=ident), reads=reads, writes=writes)

        def ACT(out, in_, func, reads, writes, bias=None, scale=None, accum=None, dwrites=()):
            def fn(e):
                kw = {}
                if bias is not None:
                    kw["bias"] = bias
                if scale is not None:
                    kw["scale"] = scale
                if accum is not None:
                    kw["accum_out"] = accum
                return e.activation(out=out, in_=in_, func=func, **kw)
            return P.op("act", fn, reads=reads, writes=writes, dwrites=dwrites)

        def TS(eng, out, in0, s1, s2, op0, op1, reads, writes, dwrites=()):
            def fn(e):
                if op1 is None:
                    return e.tensor_scalar(out=out, in0=in0, scalar1=s1, scalar2=None, op0=op0)
                return e.tensor_scalar(out=out, in0=in0, scalar1=s1, scalar2=s2, op0=op0, op1=op1)
            return P.op(eng, fn, reads=reads, writes=writes, dwrites=dwrites)

        def TT(eng, out, in0, in1, op, reads, writes):
            return P.op(eng, lambda e: e.tensor_tensor(out=out, in0=in0, in1=in1, op=op), reads=reads, writes=writes)

        def STT(out, in0, scalar, in1, op0, op1, reads, writes):
            return P.op("dve", lambda e: e.scalar_tensor_tensor(out=out, in0=in0, scalar=scalar, in1=in1, op0=op0, op1=op1),
                        reads=reads, writes=writes)

        def CP(eng, out, in_, reads, writes):
            return P.op(eng, lambda e: e.tensor_copy(out=out, in_=in_), reads=reads, writes=writes)

        def MSET(eng, ap, val, writes):
            return P.op(eng, lambda e: e.memset(ap, val), writes=writes)

        ident = sb(top, "ident", [128, 128], F32); b_ident = Buf()
        identb = sb(top, "identb", [128, 128], BF16); b_identb = Buf()
        maskb = sb(top, "maskb_s", [128, 128], BF16); b_maskb = Buf()
        onesf = sb(top, "onesf", [128, 512], F32); b_onesf = Buf()
        modcol = sb(top, "modcol", [128, 48], F32); b_modcol = Buf()
        gs1 = sb(top, "gs1", [128, KC], F32); b_gs1 = Buf()
        gs2 = sb(top, "gs2", [128, KC], F32); b_gs2 = Buf()
        cbc = sb(top, "cbc", [128, KC], F32); b_cbc = Buf()
        lngc = sb(top, "lngc", [128, KC], F32); b_lngc = Buf()
        lnbc = sb(top, "lnbc", [128, KC], F32); b_lnbc = Buf()
        gqk = sb(top, "gqk", [DH, 1], F32); b_gqk = Buf()
        epsc = sb(top, "epsc", [128, 1], F32); b_epsc = Buf()
        onec = sb(top, "onec", [128, 1], F32); b_onec = Buf()

        DMA("sp", ident[:], ident_d, [], [b_ident], d_misc)
        DMA("pool", identb[:], ident_d, [], [b_identb], d_misc)
        DMA("pool", maskb[:], mask_d, [], [b_maskb], d_misc)
        DMA("sp", cbc[:], cb_d, [], [b_cbc], d_misc)
        DMA("sp", lngc[:], lng_d, [], [b_lngc], d_misc)
        DMA("sp", lnbc[:], lnb_d, [], [b_lnbc], d_misc)
        MSET("dve", onesf[:], 1.0, [b_onesf])
        MSET("dve", epsc[:], EPS, [b_epsc])
        MSET("dve", onec[:], 1.0, [b_onec])

        stA = top.enter_context(ExitStack())
        hT = sb(stA, "hT", [128, KC, S], BF16)
        b_hT = [Buf() for _ in range(8)]

        with ExitStack() as st:
            pm = psum(st, "pm"); b_pm = PBuf()
            pc = psum(st, "pc"); b_pc = PBuf()
            ptr = [(psum(st, "ptrA%d" % i), PBuf(), psum(st, "ptrB%d" % i), PBuf()) for i in range(2)]
            ccol = sb(st, "ccol", [128, KC], F32); b_ccol = Buf()
            cact = sb(st, "cact", [128, KC], F32); b_cact = Buf()
            badar = sb(st, "badar", [1, 6 * D], F32); b_badar = Buf()
            modrow = sb(st, "modrow", [1, 6 * D], F32); b_modrow = Buf()
            n1gc = sb(st, "n1gc", [128, KC], F32); b_n1gc = Buf()
            n2gc = sb(st, "n2gc", [128, KC], F32); b_n2gc = Buf()
            qgc = sb(st, "qgc", [DH, 1], F32); b_qgc = Buf()
            kgc = sb(st, "kgc", [DH, 1], F32); b_kgc = Buf()
            wst = Ring([(sb(st, "wst%d" % i, [128, KC, 512], F32), Buf(), P.dsem("wst%d" % i)) for i in range(2)])
            xt = Ring([(sb(st, "xt%d" % i, [128, D], F32), Buf(), P.dsem("xt%d" % i)) for i in range(4)])
            xn = Ring([(sb(st, "xn%d" % i, [128, D], F32), Buf()) for i in range(3)])
            junk = sb(st, "junk", [128, D], BF16); b_junk = Buf()
            ss = Ring([(sb(st, "ss%d" % i, [128, 1], F32), Buf()) for i in range(4)])
            sd = Ring([(sb(st, "sd%d" % i, [128, 1], F32), Buf()) for i in range(4)])
            rs = Ring([(sb(st, "rs%d" % i, [128, 1], F32), Buf()) for i in range(4)])

            DMA("sp", ccol[:], ccol_d, [], [b_ccol], d_misc)
            DMA("sp", badar[:], bada_d, [], [b_badar], d_misc)
            DMA("sp", n1gc[:], n1g_d, [], [b_n1gc], d_misc)
            DMA("sp", n2gc[:], n2g_d, [], [b_n2gc], d_misc)
            DMA("sp", qgc[:], qg_d, [], [b_qgc], d_misc)
            DMA("sp", kgc[:], kg_d, [], [b_kgc], d_misc)
            ACT(cact[:], ccol[:], AF.Silu, [b_ccol], [b_cact])
            STT(gqk[:], qgc[:], 0.125, kgc[:], ALU.mult, ALU.mult, [b_qgc, b_kgc], [b_gqk])

            wada_v = wada_d.rearrange("(kc p) n -> p kc n", p=128)

            def mod_tile(n):
                w_t, w_b, w_s = wst.next()
                DMA("sp", w_t[:], wada_v[:, :, n * 512:(n + 1) * 512], [], [w_b], w_s)
                for kc in range(KC):
                    MM(pm[0:1, :], cact[:, kc:kc + 1], w_t[:, kc, :], kc == 0, kc == KC - 1, [b_cact, w_b], [b_pm])
                TT("dve", modrow[0:1, n * 512:(n + 1) * 512], pm[0:1, :], badar[0:1, n * 512:(n + 1) * 512], ALU.add,
                   [b_pm, b_badar], [b_modrow])

            def mod_cols(j0, j1):
                for j in range(j0, j1):
                    MM(pc[:, j:j + 1], modrow[0:1, j * 128:(j + 1) * 128], onec[0:1, 0:1], True, True,
                       [b_modrow, b_onec], [b_pc], skip=True)
                CP("dve", modcol[:, j0:j1], pc[:, j0:j1], [b_pc], [b_modcol])

            for n in range(4):
                mod_tile(n)
            mod_cols(0, 16)
            STT(gs1[:], modcol[:, 8:16], 1.0, n1gc[:], ALU.add, ALU.mult, [b_modcol, b_n1gc], [b_gs1])

            st1 = {}

            def stage1(i):
                x_t, x_b, x_s = xt.next()
                DMA("sp", x_t[:], x_d[i * 128:(i + 1) * 128, :], [], [x_b], x_s)
                ss_t, ss_b = ss.next()
                sd_t, sd_b = sd.next()
                rs_t, rs_b = rs.next()
                ACT(junk[:], x_t[:], AF.Square, [x_b], [b_junk, ss_b], accum=ss_t[:])
                ACT(sd_t[:], ss_t[:], AF.Sqrt, [ss_b, b_epsc], [sd_b], bias=epsc[:], scale=1.0 / D)
                P.op("dve", lambda e, o=rs_t, a=sd_t: e.reciprocal(out=o[:], in_=a[:]), reads=[sd_b], writes=[rs_b])
                xn_t, xn_b = xn.next()
                TS("pool", xn_t[:], x_t[:], rs_t[:], 1.0, ALU.mult, ALU.mult, [x_b, rs_b], [xn_b])
                st1[i] = (xn_t, xn_b)

            def stage2(i):
                xn_t, xn_b = st1.pop(i)
                pA, bA, pB, bB = ptr[i % 2]
                for kc in range(KC):
                    pp, bp = (pA, bA) if kc < 4 else (pB, bB)
                    TR(pp[:, (kc % 4) * 128:(kc % 4 + 1) * 128], xn_t[:, kc * 128:(kc + 1) * 128], ident[:],
                       [xn_b, b_ident], [bp])
                for kc in range(KC):
                    pp, bp = (pA, bA) if kc < 4 else (pB, bB)
                    src = pp[:, (kc % 4) * 128:(kc % 4 + 1) * 128]
                    dst = hT[:, kc, i * 128:(i + 1) * 128]
                    if kc < 4:
                        ACT(dst, src, AF.Identity, [bp, b_gs1, b_modcol], [], dwrites=[b_hT[i // 4]],
                            bias=modcol[:, kc:kc + 1], scale=gs1[:, kc:kc + 1])
                    else:
                        TS("dve", dst, src, gs1[:, kc:kc + 1], modcol[:, kc:kc + 1], ALU.mult, ALU.add,
                           [bp, b_gs1, b_modcol], [], dwrites=[b_hT[i // 4]])

            NTI = S // 128
            stage1(0)
            stage1(1)
            for i in range(NTI):
                if i + 2 < NTI:
                    stage1(i + 2)
                stage2(i)
                if i % 3 == 2 and 4 + i // 3 < 12:
                    mod_tile(4 + i // 3)
            mod_cols(16, 48)
            STT(gs2[:], modcol[:, 32:40], 1.0, n2gc[:], ALU.add, ALU.mult, [b_modcol, b_n2gc], [b_gs2])
            DMA("sp", modscr, modrow[:], [b_modrow], [b_modscr], d_misc)

            P.barrier()
        with ExitStack() as st:
            pf = psum(st, "pf"); b_pf = PBuf()
            wf = sb(st, "wf", [128, KC, H], BF16); b_wf = Buf()
            nbf = sb(st, "nbf", [H, 1], F32); b_nbf = Buf()
            bfc = sb(st, "bfc", [H, 1], F32); b_bfc = Buf()
            ef = sb(st, "ef", [H, 512], F32); b_ef = Buf()
            spf = sb(st, "spf", [H, S], F32); b_spf = Buf()
            ncum = sb(st, "ncum", [H, S], F32); b_ncum = Buf()
            res = sb(st, "res", [H, S], F32); b_res = Buf()
            fpos = sb(st, "fpos", [H, 3, S], BF16); b_fpos = Buf()
            fneg = sb(st, "fneg", [H, 3, S], BF16); b_fneg = Buf()
            DMA("pool", wf[:], win_d[:, OFF_F:OFF_F + H].rearrange("(kc p) n -> p kc n", p=128), [], [b_wf], d_misc)
            DMA("sp", bfc[:], bfor_d, [], [b_bfc], d_misc)
            TS("dve", nbf[:], bfc[:], -1.0, None, ALU.mult, None, [b_bfc], [b_nbf])
            for t in range(8):
                for kc in range(KC):
                    MM(pf[0:H, :], wf[:, kc, :], hT[:, kc, t * 512:(t + 1) * 512], kc == 0, kc == KC - 1,
                       [b_wf, b_hT[t]], [b_pf])
                ACT(ef[:], pf[0:H, :], AF.Exp, [b_pf, b_nbf], [b_ef], bias=nbf[:], scale=-1.0)
                ACT(spf[:, t * 512:(t + 1) * 512], ef[:], AF.Ln, [b_ef, b_onec], [b_spf], bias=onec[0:H, :])
            for t in range(8):
                sl = slice(t * 512, (t + 1) * 512)
                init = 0.0 if t == 0 else ncum[:, t * 512 - 1:t * 512]
                P.op("dve", lambda e, sl=sl, init=init: e.tensor_tensor_scan(out=ncum[:, sl], data0=onesf[0:H, :], data1=spf[:, sl],
                                                                        initial=init, op0=ALU.mult, op1=ALU.add),
                     reads=[b_onesf, b_spf, b_ncum], writes=[b_ncum])
            CP("dve", fpos[:, 0, :], ncum[:], [b_ncum], [b_fpos])
            TT("dve", res[:], ncum[:], fpos[:, 0, :], ALU.subtract, [b_ncum, b_fpos], [b_res])
            CP("dve", fpos[:, 1, :], res[:], [b_res], [b_fpos])
            TT("dve", res[:], res[:], fpos[:, 1, :], ALU.subtract, [b_res, b_fpos], [b_res])
            CP("dve", fpos[:, 2, :], res[:], [b_res], [b_fpos])
            TS("pool", fneg[:], fpos[:], -1.0, 1.0, ALU.mult, ALU.mult, [b_fpos], [b_fneg])
            DMA("sp", fscr[0:3].rearrange("r h s -> h r s"), fpos[:], [b_fpos], [b_fscr], d_misc)
            DMA("sp", fscr[3:6].rearrange("r h s -> h r s"), fneg[:], [b_fneg], [b_fscr], d_misc)
            if debug:
                DMA("sp", dbg_hT, hT[:], b_hT, [], d_out)
            P.barrier()

        if upto >= 2:
            stB = stA.enter_context(ExitStack())
            BIG = sb(stB, "BIG", [128, KC, S], BF16)
            b_big = [Buf() for _ in range(8)]

        if upto >= 2:
            with ExitStack() as st:
                pa = [(psum(st, "pa%d" % i), PBuf()) for i in range(2)]
                pb = [(psum(st, "pb%d" % i), PBuf()) for i in range(2)]
                py = [(psum(st, "py%d" % i), PBuf()) for i in range(2)]
                cw = sb(st, "cw", [128, KC, CK], F32); b_cw = Buf()
                DMA("sp", cw[:], cwT_d, [], [b_cw], d_misc)
                wgl = Ring([(sb(st, "wgl%d" % i, [128, KC, 2, 128], BF16), Buf(), P.dsem("wgl%d" % i)) for i in range(2)])
                dg = Ring([(sb(st, "dg%d" % i, [128, CK, 128], BF16), Buf()) for i in range(2)])
                ub = Ring([(sb(st, "ub%d" % i, [128, 30 + S], BF16), Buf()) for i in range(2)])
                sg = Ring([(sb(st, "sg%d" % i, [128, 512], F32), Buf()) for i in range(2)])
                for (u_t, u_b) in ub.items:
                    MSET("pool", u_t[:, 0:30], 0.0, [u_b])
                for c in range(KC):
                    g_t, g_b, g_s = wgl.next()
                    DMA("pool", g_t[:, :, 0, :], win_d[:, OFF_GLU + c * 128:OFF_GLU + (c + 1) * 128].rearrange("(kc p) n -> p kc n", p=128),
                        [], [g_b], g_s)
                    DMA("pool", g_t[:, :, 1, :], win_d[:, OFF_GLU + D + c * 128:OFF_GLU + D + (c + 1) * 128].rearrange("(kc p) n -> p kc n", p=128),
                        [], [g_b], g_s)
                    if upto >= 5 and c == 1:
                        d_wscr = P.dsem("wscr", bg=True)
                        d_wscr2 = P.dsem("wscr2", bg=True)
                        DMA("pool", wcs.rearrange("r (h n) -> r h n", h=1), wcp_d.rearrange("r (h n) -> r h n", h=1), [], [b_wscr2], d_wscr2)
                        DMA("pool", wbs.rearrange("r (h n) -> r h n", h=1), win_d[:, OFF_GATE + D:OFF_GATE + 2 * D].rearrange("r (h n) -> r h n", h=1),
                            [], [b_wscr2], d_wscr2)
                        DMA("pool", wps.rearrange("r (h n) -> r h n", h=1), wap_d.rearrange("r (h n) -> r h n", h=1), [], [b_wscr], d_wscr)
                        DMA("pool", wgs.rearrange("r (h n) -> r h n", h=1), win_d[:, OFF_GATE:OFF_GATE + D].rearrange("r (h n) -> r h n", h=1),
                            [], [b_wscr], d_wscr)
                        DMA("pool", wos.rearrange("r (h n) -> r h n", h=1), wout_d.rearrange("r (h n) -> r h n", h=1), [], [b_wscr], d_wscr)
                        for q in range(4):
                            DMA("pool", w1s[q * 256:(q + 1) * 256, :].rearrange("r (h n) -> r h n", h=2),
                                w1_d[q * 256:(q + 1) * 256, :].rearrange("r (h n) -> r h n", h=2), [], [b_wscr], d_wscr)
                        for q in range(4):
                            DMA("pool", w2s[q * 1024:(q + 1) * 1024, :].rearrange("r (h n) -> r h n", h=1),
                                w2_d[q * 1024:(q + 1) * 1024, :].rearrange("r (h n) -> r h n", h=1), [], [b_wscr], d_wscr)

                    d_t, d_b = dg.next()
                    for k in range(CK):
                        TS("dve", d_t[:, k, :], ident[:], cw[:, c, k:k + 1], None, ALU.mult, None, [b_ident, b_cw], [d_b])
                    u_t, u_b = ub.next()
                    for t in range(8):
                        (pa_t, pa_b), (pb_t, pb_b) = pa[t % 2], pb[t % 2]
                        for kc in range(KC):
                            MM(pa_t[:, :], g_t[:, kc, 0, :], hT[:, kc, t * 512:(t + 1) * 512], kc == 0, kc == KC - 1,
                               [g_b, b_hT[t]], [pa_b])
                        for kc in range(KC):
                            MM(pb_t[:, :], g_t[:, kc, 1, :], hT[:, kc, t * 512:(t + 1) * 512], kc == 0, kc == KC - 1,
                               [g_b, b_hT[t]], [pb_b])
                        s_t, s_b = sg.next()
                        ACT(s_t[:], pb_t[:, :], AF.Sigmoid, [pb_b], [s_b])
                        TT("dve", u_t[:, 30 + t * 512:30 + (t + 1) * 512], pa_t[:, :], s_t[:], ALU.mult, [pa_b, s_b], [u_b])
                    for t in range(8):
                        y_t, y_b = py[t % 2]
                        for k in range(CK):
                            MM(y_t[:, :], d_t[:, k, :], u_t[:, t * 512 + k:t * 512 + k + 512], k == 0, k == CK - 1,
                               [d_b, u_b], [y_b])
                        ACT(BIG[:, c, t * 512:(t + 1) * 512], y_t[:, :], AF.Identity, [y_b, b_cbc], [b_big[t]],
                            bias=cbc[:, c:c + 1])
                P.barrier()
            with ExitStack() as st:
                pa = [(psum(st, "pa2%d" % i), PBuf()) for i in range(2)]
                pb = [(psum(st, "pb2%d" % i), PBuf()) for i in range(2)]
                wcp = sb(st, "wcp", [128, KC, D], BF16); b_wcp = Buf(); d_wcp = P.dsem("wcp")
                wgb = sb(st, "wgb", [128, KC, D], BF16); b_wgb = Buf(); d_wgb = P.dsem("wgb")
                if upto >= 5:
                    DMA("sp", wcp[:], wcs.rearrange("(kc p) n -> p kc n", p=128), [b_wscr2], [b_wcp], d_wcp)
                    DMA("sp", wgb[:], wbs.rearrange("(kc p) n -> p kc n", p=128), [b_wscr2], [b_wgb], d_wgb)
                else:
                    for kc in range(KC):
                        DMA("pool", wcp[:, kc, :], wcp_d[kc * 128:(kc + 1) * 128, :], [], [b_wcp], d_wcp)
                        DMA("pool", wgb[:, kc, :], win_d[kc * 128:(kc + 1) * 128, OFF_GATE + D:OFF_GATE + 2 * D], [], [b_wgb], d_wgb)
                pmean = psum(st, "pmean"); b_pmean = PBuf()
                pmsq = psum(st, "pmsq"); b_pmsq = PBuf()
                onesb = sb(st, "onesb", [128, 128], BF16); b_onesb = Buf()
                MSET("dve", onesb[:], 1.0 / D, [b_onesb])
                ysq = sb(st, "ysq", [128, KC, 512], BF16); b_ysq = Buf()
                mean_s = sb(st, "mean_s", [128, 512], F32); b_mean = Buf()
                var_s = sb(st, "var_s", [128, 512], F32); b_var = Buf()
                rstd_s = var_s; b_rstd = b_var
                zc = Ring([(sb(st, "zc%d" % i, [128, 512], F32), Buf()) for i in range(2)])
                zbr = Ring([(sb(st, "zb%d" % i, [128, KC, 512], BF16), Buf()) for i in range(2)])
                sgb = Ring([(sb(st, "sgb%d" % i, [128, 512], F32), Buf()) for i in range(1)])
                bgt = Ring([(sb(st, "bgt%d" % i, [128, KC, 512], BF16), Buf(), P.dsem("bgt%d" % i)) for i in range(1)])
                zs = Ring([(sb(st, "zs%d" % i, [128, 512], BF16), Buf()) for i in range(2)])
                zcur = {}

                def S1(t):
                    sl = slice(t * 512, (t + 1) * 512)
                    TT("pool", ysq[:], BIG[:, :, sl], BIG[:, :, sl], ALU.mult, [b_big[t]], [b_ysq])
                    for c in range(KC):
                        MM(pmean[:, :], onesb[:], BIG[:, c, sl], c == 0, c == KC - 1, [b_onesb, b_big[t]], [b_pmean])
                    for c in range(KC):
                        MM(pmsq[:, :], onesb[:], ysq[:, c, :], c == 0, c == KC - 1, [b_onesb, b_ysq], [b_pmsq])

                def S2_pieces(t):
                    sl = slice(t * 512, (t + 1) * 512)
                    zb_t, zb_b = zbr.next()
                    zcur[t] = (zb_t, zb_b)

                    def stats():
                        CP("dve", mean_s[:], pmean[:, :], [b_pmean], [b_mean])
                        TT("dve", var_s[:], mean_s[:], mean_s[:], ALU.mult, [b_mean], [b_var])
                        TT("dve", var_s[:], pmsq[:, :], var_s[:], ALU.subtract, [b_pmsq, b_var], [b_var])
                        ACT(var_s[:], var_s[:], AF.Sqrt, [b_var, b_epsc], [b_var], bias=epsc[:])
                        P.op("dve", lambda e: e.reciprocal(out=rstd_s[:], in_=var_s[:]), reads=[b_var], writes=[b_rstd])

                    def zchunk(c):
                        z_t, z_b = zc.next()
                        s_t, s_b = zs.next()
                        TT("pool", z_t[:], BIG[:, c, sl], mean_s[:], ALU.subtract, [b_big[t], b_mean], [z_b])
                        TT("pool", z_t[:], z_t[:], rstd_s[:], ALU.mult, [z_b, b_rstd], [z_b])
                        ACT(s_t[:], z_t[:], AF.Sigmoid, [z_b, b_lngc, b_lnbc], [s_b],
                            bias=lnbc[:, c:c + 1], scale=lngc[:, c:c + 1])
                        TS("dve", z_t[:], z_t[:], lngc[:, c:c + 1], lnbc[:, c:c + 1], ALU.mult, ALU.add, [z_b, b_lngc, b_lnbc], [z_b])
                        P.op("dve", lambda e, o=zb_t[:, c, :], a=z_t[:], b_=s_t[:]: e.tensor_tensor(out=o, in0=a, in1=b_, op=ALU.mult),
                             reads=[z_b, s_b], dwrites=[zb_b])
                    return [stats, lambda: (zchunk(0), zchunk(1)), lambda: (zchunk(2), zchunk(3)), lambda: zchunk(4),
                            lambda: zchunk(5), lambda: zchunk(6), lambda: zchunk(7), lambda: None]

                def Mo(t, o, o_t, o_b):
                    sl = slice(t * 512, (t + 1) * 512)
                    zb_t, zb_b = zcur[t]
                    (pa_t, pa_b), (pb_t, pb_b) = pa[o % 2], pb[o % 2]
                    for c in range(KC):
                        MM(pa_t[:, :], wcp[:, c, o * 128:(o + 1) * 128], zb_t[:, c, :], c == 0, c == KC - 1,
                           [b_wcp, zb_b], [pa_b])
                    for kc in range(KC):
                        MM(pb_t[:, :], wgb[:, kc, o * 128:(o + 1) * 128], hT[:, kc, sl], kc == 0, kc == KC - 1,
                           [b_wgb, b_hT[t]], [pb_b])
                    s_t, s_b = sgb.next()
                    ACT(s_t[:], pb_t[:, :], AF.Sigmoid, [pb_b], [s_b])
                    TT("dve", o_t[:, o, :], pa_t[:, :], s_t[:], ALU.mult, [pa_b, s_b], [o_b])

                S1(0)
                for p_ in S2_pieces(0):
                    p_()
                for t in range(8):
                    sl = slice(t * 512, (t + 1) * 512)
                    pieces = []
                    if t + 1 < 8:
                        S1(t + 1)
                        pieces = S2_pieces(t + 1)
                    o_t, o_b, o_s = bgt.next()
                    for o in range(KC):
                        Mo(t, o, o_t, o_b)
                        if pieces:
                            pieces.pop(0)()
                    DMA("sp", bgscr[:, :, sl], o_t[:], [o_b], [b_bg[t]], o_s)
                P.barrier()

        if upto >= 3:
            with ExitStack() as st:
                pS = Ring([(psum(st, "pS%d" % i), PBuf()) for i in range(3)])
                pO = Ring([(psum(st, "pO%d" % i), PBuf()) for i in range(2)])
                pBc = psum(st, "pBc"); b_pBc = PBuf()
                pT = pBc; b_pT = b_pBc
                pQ = Ring([(psum(st, "pQ%d" % i), PBuf()) for i in range(2)])
                qaug = [(sb(st, "qaug%d" % i, [70, S], BF16), Buf(), P.dsem("qaug%d" % i)) for i in range(2)]
                kaug = [(sb(st, "kaug%d" % i, [70, S], BF16), Buf(), P.dsem("kaug%d" % i)) for i in range(2)]
                vh = [(sb(st, "vh0", [128, S // 128, 65], BF16), Buf()), (sb(st, "vh1", [128, S // 128, 128], BF16), Buf())]
                wqkv = Ring([(sb(st, "wqkv%d" % i, [128, KC, 3, 128], BF16), Buf(), P.dsem("wqkv%d" % i)) for i in range(2)])
                qkraw = Ring([(sb(st, "qkraw%d" % i, [128, 8, 2, DH], F32), Buf()) for i in range(2)])
                sqs = sb(st, "sqs", [128, 8, 2, DH], F32); b_sqs = Buf()
                ssq = sb(st, "ssq", [128, 16], F32); b_ssq = Buf()
                rsq = sb(st, "rsq", [128, 16], F32); b_rsq = Buf()
                nhalf = sb(st, "nhalf", [128, 16], F32); b_nhalf = Buf()
                MSET("pool", nhalf[:], -0.5, [b_nhalf])
                PT = Ring([(sb(st, "PT%d" % i, [128, 512], BF16), Buf()) for i in range(4)])
                rden = Ring([(sb(st, "rden%d" % i, [128, 512], F32), Buf()) for i in range(1)])
                bcs = Ring([(sb(st, "bcs%d" % i, [128, 512], F32), Buf()) for i in range(1)])
                for i in range(2):
                    MSET("pool", qaug[i][0][64:70, :], 1.0, [qaug[i][1]])
                    MSET("pool", kaug[i][0][64:70, :], 1.0, [kaug[i][1]])
                MSET("pool", vh[0][0][:, :, 64:65], 1.0, [vh[0][1]])
                MSET("pool", vh[1][0][:, :, 0:64], 0.0, [vh[1][1]])
                MSET("pool", vh[1][0][:, :, 0:1], 1.0, [vh[1][1]])
                cur_w = [None]
                def inproj_gen(h):
                    e = h % 2
                    if e == 0:
                        w_t, w_b, w_s = wqkv.next()
                        c0 = (h // 2) * 128
                        for j, off in enumerate((OFF_Q, OFF_K, OFF_V)):
                            DMA("pool", w_t[:, :, j, :], win_d[:, off + c0:off + c0 + 128].rearrange("(kc p) n -> p kc n", p=128),
                                [], [w_b], w_s)
                        cur_w[0] = (w_t, w_b)
                    w_t, w_b = cur_w[0]
                    q_t, q_b, q_s = qaug[e]
                    k_t, k_b, k_s = kaug[e]
                    v_t, v_b = vh[e]
                    DMA("sp", q_t[64:67, :], fscr[3:6, h, :], [b_fscr], [q_b], q_s)
                    DMA("sp", k_t[67:70, :], fscr[0:3, h, :], [b_fscr], [k_b], k_s)
                    yield
                    voff = 0 if e == 0 else 64
                    pending_tr = []

                    def tr_steps(grp, qr_t, qr_b):
                        steps = []
                        for g2 in range(2):
                            g = grp * 2 + g2
                            for which in range(2):
                                def step(g=g, g2=g2, which=which):
                                    for ii in range(4):
                                        TR(pT[0:64, ii * 128:(ii + 1) * 128], qr_t[:, g2 * 4 + ii, which, :], ident[:],
                                           [qr_b, b_ident], [b_pT])
                                    if which == 0:
                                        CP("dve", q_t[0:64, g * 512:(g + 1) * 512], pT[0:64, :], [b_pT], [q_b])
                                    else:
                                        TS("dve", k_t[0:64, g * 512:(g + 1) * 512], pT[0:64, :], gqk[:], None, ALU.mult, None,
                                           [b_pT, b_gqk], [k_b])
                                steps.append(step)
                        return steps

                    for grp in range(4):
                        qr_t, qr_b = qkraw.next()
                        qk3 = qr_t[:].rearrange("p a b d -> p (a b) d")
                        for il in range(8):
                            i = grp * 8 + il
                            p_t, p_b = pQ.next()
                            for kc in range(KC):
                                MM(p_t[:, 0:192].rearrange("p (a b) -> p a b", a=3), hT[:, kc, i * 128:(i + 1) * 128],
                                   w_t[:, kc, :, e * 64:(e + 1) * 64], kc == 0, kc == KC - 1, [b_hT[i // 4], w_b], [p_b])
                            CP("dve", qr_t[:, il, :, :], p_t[:, 0:128].rearrange("p (a b) -> p a b", a=2), [p_b], [qr_b])
                            CP("dve", v_t[:, i, voff:voff + 64], p_t[:, 128:192], [p_b], [v_b])
                            if pending_tr and il >= 4:
                                pending_tr.pop(0)()
                            yield
                        TT("pool", sqs[:], qr_t[:], qr_t[:], ALU.mult, [qr_b], [b_sqs])
                        P.op("dve", lambda e_: e_.tensor_reduce(out=ssq[:], in_=sqs[:].rearrange("p a b d -> p (a b) d"), axis=AX.X, op=ALU.add),
                             reads=[b_sqs], writes=[b_ssq])
                        TS("pool", ssq[:], ssq[:], 1.0 / DH, EPS, ALU.mult, ALU.add, [b_ssq], [b_ssq])
                        TT("pool", rsq[:], ssq[:], nhalf[:], ALU.pow, [b_ssq, b_nhalf], [b_rsq])
                        TT("pool", qk3, qk3, rsq[:].unsqueeze(2).to_broadcast([128, 16, DH]), ALU.mult, [qr_b, b_rsq], [qr_b])
                        yield
                        pending_tr = tr_steps(grp, qr_t, qr_b)
                    yield
                    yield
                    while pending_tr:
                        pending_tr.pop(0)()
                        yield

                LAG = 3
                pend = []
                defer = []
                otile = {}

                def emit_S(h, j, i):
                    e = h % 2
                    q_t, q_b, _ = qaug[e]
                    k_t, k_b, _ = kaug[e]
                    s_t, s_b = pS.next()
                    kl = k_t[0:70, i * 128:(i + 1) * 128]
                    if i < 4 * j:
                        c0 = 0
                        MM(s_t[:, :], kl, q_t[0:70, j * 512:(j + 1) * 512], True, True, [k_b, q_b], [s_b])
                    else:
                        c0 = (i - 4 * j) * 128
                        MM(s_t[:, c0:c0 + 128], kl, q_t[0:70, j * 512 + c0:j * 512 + c0 + 128], True, False,
                           [k_b, q_b], [s_b], skip=True)
                        MM(s_t[:, c0:c0 + 128], identb[:], maskb[:], False, True, [b_identb, b_maskb], [s_b], skip=True)
                        if c0 + 128 < 512:
                            MM(s_t[:, c0 + 128:512], kl, q_t[0:70, j * 512 + c0 + 128:(j + 1) * 512], True, True,
                               [k_b, q_b], [s_b], skip=True)
                    p_t, p_b = PT.next()
                    ACT(p_t[:, c0:512], s_t[:, c0:512], AF.Exp, [s_b], [p_b])
                    pend.append((h, j, i, c0, p_t, p_b))

                def emit_PV():
                    h, j, i, c0, p_t, p_b = pend.pop(0)
                    e = h % 2
                    c = h // 2
                    v_t, v_b = vh[e]
                    M = 65 if e == 0 else 128
                    p0 = 64 if e == 0 else 0
                    o0 = 0 if e == 0 else 64
                    nblk = 4 * (j + 1)
                    if i == 0:
                        otile[(h, j)] = pO.next()
                    o_t, o_b = otile[(h, j)]
                    MM(o_t[0:M, c0:512], v_t[:, i, 0:M], p_t[:, c0:512], i == 0, i == nblk - 1, [v_b, p_b], [o_b], skip=True)
                    if i == nblk - 1:
                        del otile[(h, j)]
                        r_t, r_b = rden.next()
                        ACT(r_t[p0:p0 + 1, :], o_t[p0:p0 + 1, :], AF.Ln, [o_b], [r_b])
                        ACT(r_t[p0:p0 + 1, :], r_t[p0:p0 + 1, :], AF.Exp, [r_b], [r_b], scale=-1.0)

                        def tail():
                            MM(pBc[:, :], onesf[p0:p0 + 1, 0:128], r_t[p0:p0 + 1, :], True, True, [b_onesf, r_b], [b_pBc])
                            b_t, b_b = bcs.next()
                            CP("dve", b_t[o0:o0 + 64, :], pBc[o0:o0 + 64, :], [b_pBc], [b_b])
                            TT("dve", BIG[o0:o0 + 64, c, j * 512:(j + 1) * 512], o_t[o0:o0 + 64, :], b_t[o0:o0 + 64, :], ALU.mult,
                               [o_b, b_b], [b_big[j]])
                        defer.append([3, tail])

                def tick_defer(force=False):
                    for d in list(defer):
                        d[0] -= 1
                        if d[0] <= 0 or force:
                            d[1]()
                            defer.remove(d)

                g0 = inproj_gen(0)
                for _ in g0:
                    pass
                for h in range(H):
                    nxt = inproj_gen(h + 1) if h + 1 < H else None
                    cnt = 0
                    for j in range(8):
                        for i in range(4 * (j + 1)):
                            emit_S(h, j, i)
                            if len(pend) > LAG:
                                emit_PV()
                            tick_defer()
                            cnt += 1
                            if nxt is not None and cnt % 3 == 0:
                                next(nxt, None)
                    if nxt is not None:
                        for _ in nxt:
                            pass
                while pend:
                    emit_PV()
                    tick_defer()
                tick_defer(force=True)
                tick_defer(force=True)
                if debug:
                    DMA("sp", dbg_ao, BIG[:], b_big, [], d_out)
                P.barrier()

        if upto >= 4:
            with ExitStack() as st:
                pa = [(psum(st, "p4a%d" % i), PBuf()) for i in range(2)]
                pb = [(psum(st, "p4b%d" % i), PBuf()) for i in range(2)]
                wap = sb(st, "wap", [128, KC, D], BF16); b_wap = Buf(); d_wap = P.dsem("wap")
                wga = sb(st, "wga", [128, KC, D], BF16); b_wga = Buf(); d_wga = P.dsem("wga")
                if upto >= 5:
                    DMA("sp", wap[:], wps.rearrange("(kc p) n -> p kc n", p=128), [b_wscr], [b_wap], d_wap)
                    DMA("sp", wga[:], wgs.rearrange("(kc p) n -> p kc n", p=128), [b_wscr], [b_wga], d_wga)
                else:
                    for kc in range(KC):
                        DMA("pool", wap[:, kc, :], wap_d[kc * 128:(kc + 1) * 128, :], [], [b_wap], d_wap)
                        DMA("pool", wga[:, kc, :], win_d[kc * 128:(kc + 1) * 128, OFF_GATE:OFF_GATE + D], [], [b_wga], d_wga)
                bgl = Ring([(sb(st, "bgl%d" % i, [128, KC, 512], BF16), Buf(), P.dsem("bgl%d" % i)) for i in range(2)])
                mgt = Ring([(sb(st, "mgt%d" % i, [128, KC, 512], BF16), Buf(), P.dsem("mgt%d" % i)) for i in range(2)])
                gb2 = sb(st, "gb2", [128, D], F32); b_gb2 = Buf()
                DMA("sp", gb2[:], modscr[0:1, 5 * D:6 * D].broadcast_to([128, D]), [b_modscr], [b_gb2], d_misc)

                def scale_w2(t_):
                    for k_ in range(4):
                        w2c_ = BIG[:, 2 * k_:2 * k_ + 2, t_ * 512:(t_ + 1) * 512]
                        TT("pool", w2c_, w2c_, gb2[:].rearrange("p (h n) -> p h n", h=2), ALU.mult, [b_big[t_], b_gb2], [b_big[t_]])
                sga = Ring([(sb(st, "sga%d" % i, [128, 512], F32), Buf()) for i in range(2)])
                tmp = Ring([(sb(st, "tmp4%d" % i, [128, 512], F32), Buf()) for i in range(2)])
                d_w1 = P.dsem("w1"); d_w2 = P.dsem("w2")
                b_w2 = [Buf() for _ in range(4)]
                w1v = hT
                w2v = BIG[:].rearrange("p c t -> p (c t)").rearrange("p (f n) -> p f n", n=D)
                nxt_l = bgl.next()
                DMA("sp", nxt_l[0][:], bgscr[:, :, 0:512], [b_bg[0]], [nxt_l[1]], nxt_l[2])
                for t in range(8):
                    sl = slice(t * 512, (t + 1) * 512)
                    l_t, l_b, l_s = nxt_l
                    m_t, m_b, m_s = mgt.next()
                    for o in range(KC):
                        (pa_t, pa_b), (pb_t, pb_b) = pa[o % 2], pb[o % 2]
                        for c in range(KC):
                            MM(pa_t[:, :], wap[:, c, o * 128:(o + 1) * 128], BIG[:, c, sl], c == 0, c == KC - 1,
                               [b_wap, b_big[t]], [pa_b])
                        for kc in range(KC):
                            MM(pb_t[:, :], wga[:, kc, o * 128:(o + 1) * 128], hT[:, kc, sl], kc == 0, kc == KC - 1,
                               [b_wga, b_hT[t]], [pb_b])
                        s_t, s_b = sga.next()
                        ACT(s_t[:], pb_t[:, :], AF.Sigmoid, [pb_b], [s_b])
                        t_t, t_b = tmp.next()
                        TT("dve", t_t[:], pa_t[:, :], s_t[:], ALU.mult, [pa_b, s_b], [t_b])
                        TT("dve", m_t[:, o, :], t_t[:], l_t[:, o, :], ALU.add, [t_b, l_b], [m_b])
                    DMA("sp", mscr[:, :, sl], m_t[:], [m_b], [b_ms[t]], m_s)
                    if t + 1 < 8:
                        nxt_l = bgl.next()
                        DMA("sp", nxt_l[0][:], bgscr[:, :, (t + 1) * 512:(t + 2) * 512], [b_bg[t + 1]], [nxt_l[1]], nxt_l[2])
                    if upto >= 5:
                        DMA("sp", hT[:, :, sl], w1s[:, sl].rearrange("(kc p) n -> p kc n", p=128), [b_wscr], [b_hT[t]], d_w1)
                        for f in range(4 * t, 4 * t + 4):
                            DMA("sp", BIG[:, (f % 4) * 2:(f % 4) * 2 + 2, sl],
                                w2s[f * 128:(f + 1) * 128, :].rearrange("p (h n) -> p h n", h=2), [b_wscr], [b_big[t]], d_w2)
                    if upto >= 5 and t >= 1:
                        scale_w2(t - 1)
                if upto >= 5:
                    scale_w2(7)
                P.barrier(exclude=(d_w1,))

        if upto >= 5:
            with ExitStack() as st:
                TT5 = 256
                NT5 = S // TT5
                wo = sb(st, "wo", [128, KC, D], BF16); b_wo = Buf(); d_wo = P.dsem("wo")
                w1 = w1v
                w2 = w2v
                identf = ident
                pw = Ring([(psum(st, "pw%d" % i), PBuf()) for i in range(2)])
                ph = Ring([(psum(st, "ph%d" % i), PBuf()) for i in range(2)])
                po = [(psum(st, "po%d" % i), PBuf()) for i in range(4)]
                mgl = Ring([(sb(st, "mgl%d" % i, [128, KC, TT5], BF16), Buf(), P.dsem("mgl%d" % i)) for i in range(2)])
                xl = Ring([(sb(st, "xl%d" % i, [128, D], F32), Buf(), P.dsem("xl%d" % i)) for i in range(2)])
                x1 = Ring([(sb(st, "x1_%d" % i, [128, D], F32), Buf()) for i in range(2)])
                xn2 = Ring([(sb(st, "xn2_%d" % i, [128, D], F32), Buf()) for i in range(2)])
                junk5 = sb(st, "junk5", [128, D], BF16); b_junk5 = Buf()
                ss5 = Ring([(sb(st, "ss5_%d" % i, [128, 1], F32), Buf()) for i in range(4)])
                sd5 = Ring([(sb(st, "sd5_%d" % i, [128, 1], F32), Buf()) for i in range(4)])
                rs5 = Ring([(sb(st, "rs5_%d" % i, [128, 1], F32), Buf()) for i in range(4)])
                h2T = Ring([(sb(st, "h2T%d" % i, [128, KC, TT5], BF16), Buf()) for i in range(2)])
                rl = Ring([(sb(st, "rl%d" % i, [128, TT5], F32), Buf()) for i in range(2)])
                aT = Ring([(sb(st, "aT%d" % i, [128, TT5], BF16), Buf()) for i in range(3)])
                ot = Ring([(sb(st, "ot%d" % i, [128, D], F32), Buf(), P.dsem("ot%d" % i)) for i in range(2)])

                gb_t, gb_b, _ = ot.items[0]
                DMA("sp", wo[:], wos.rearrange("(kc p) n -> p kc n", p=128), [b_wscr], [b_wo], d_wo)
                DMA("sp", gb_t[:], modscr[0:1, 2 * D:3 * D].broadcast_to([128, D]), [b_modscr], [gb_b], d_misc)
                for kc in range(KC):
                    TT("dve", wo[:, kc, :], wo[:, kc, :], gb_t[:], ALU.mult, [b_wo, gb_b], [b_wo])
                stA5 = {}

                def A1(tt):
                    m_t, m_b, m_s = mgl.next()
                    DMA("sp", m_t[:], mscr[:, :, tt * TT5:(tt + 1) * TT5], [b_ms[tt // 2]], [m_b], m_s)
                    x1s, ns = [], []
                    for sub in range(2):
                        tok0 = tt * TT5 + sub * 128
                        x_t, x_b, x_s = xl.next()
                        DMA("sp", x_t[:], x_d[tok0:tok0 + 128, :], [], [x_b], x_s)
                        x1_t, x1_b = x1.next()
                        x1s.append((x1_t, x1_b))
                        for half in range(2):
                            hs = slice(half * 512, (half + 1) * 512)
                            p_t, p_b = pw.next()
                            for kc in range(KC):
                                MM(p_t[:, :], m_t[:, kc, sub * 128:(sub + 1) * 128], wo[:, kc, hs],
                                   kc == 0, kc == KC - 1, [m_b, b_wo], [p_b])
                            TT("dve", x1_t[:, hs], p_t[:, :], x_t[:, hs], ALU.add, [p_b, x_b], [], ) if False else \
                                P.op("dve", lambda e, o=x1_t[:, hs], a=p_t[:, :], b_=x_t[:, hs]: e.tensor_tensor(out=o, in0=a, in1=b_, op=ALU.add),
                                     reads=[p_b, x_b], dwrites=[x1_b])
                        ss_t, ss_b = ss5.next()
                        sd_t, sd_b = sd5.next()
                        rs_t, rs_b = rs5.next()
                        ACT(junk5[:], x1_t[:], AF.Square, [x1_b], [b_junk5, ss_b], accum=ss_t[:])
                        ACT(sd_t[:], ss_t[:], AF.Sqrt, [ss_b, b_epsc], [sd_b], bias=epsc[:], scale=1.0 / D)
                        P.op("dve", lambda e, o=rs_t, a=sd_t: e.reciprocal(out=o[:], in_=a[:]), reads=[sd_b], writes=[rs_b])
                        n_t, n_b = xn2.next()
                        TS("pool", n_t[:], x1_t[:], rs_t[:], 1.0, ALU.mult, ALU.mult, [x1_b, rs_b], [n_b])
                        ns.append((n_t, n_b))
                    stA5[tt] = (x1s, ns)

                def A2(tt):
                    x1s, ns = stA5[tt]
                    h_t, h_b = h2T.next()
                    for sub in range(2):
                        n_t, n_b = ns[sub]
                        for half in range(2):
                            p_t, p_b = pw.next()
                            for k4 in range(4):
                                kc = half * 4 + k4
                                TR(p_t[:, k4 * 128:(k4 + 1) * 128], n_t[:, kc * 128:(kc + 1) * 128], ident[:], [n_b, b_ident], [p_b])
                            for k4 in range(4):
                                kc = half * 4 + k4
                                src_ = p_t[:, k4 * 128:(k4 + 1) * 128]
                                dst = h_t[:, kc, sub * 128:(sub + 1) * 128]
                                if half == 0:
                                    ACT(dst, src_, AF.Identity, [p_b, b_gs2, b_modcol], [], dwrites=[h_b],
                                        bias=modcol[:, 24 + kc:25 + kc], scale=gs2[:, kc:kc + 1])
                                else:
                                    TS("dve", dst, src_, gs2[:, kc:kc + 1], modcol[:, 24 + kc:25 + kc], ALU.mult, ALU.add,
                                       [p_b, b_gs2, b_modcol], [], dwrites=[h_b])
                    stA5[tt] = (x1s, ns, h_t, h_b)

                def Bst(tt):
                    x1s, ns, h_t, h_b = stA5.pop(tt)
                    for sub in range(2):
                        x1_t, x1_b = x1s[sub]
                        for half in range(2):
                            o_t, o_b = po[sub * 2 + half]
                            MM(o_t[:, :], identf[:], x1_t[:, half * 512:(half + 1) * 512], True, False, [b_ident, x1_b], [o_b])
                    pend = None
                    for f in range(FC + 1):
                        if f < FC:
                            p_t, p_b = ph.next()
                            for kc in range(KC):
                                MM(p_t[:, 0:TT5], w1[:, kc, f * 128:(f + 1) * 128], h_t[:, kc, :], kc == 0, kc == KC - 1,
                                   [b_hT[f // 4], h_b], [p_b])
                            r_t, r_b = rl.next()
                            ACT(r_t[:], p_t[:, 0:TT5], AF.Relu, [p_b], [r_b])
                            a_t, a_b = aT.next()
                            TT("dve", a_t[:], r_t[:], r_t[:], ALU.mult, [r_b], [a_b])
                        if pend is not None:
                            pf_, pa_t, pa_b = pend
                            w2c = BIG[:, (pf_ % 4) * 2:(pf_ % 4) * 2 + 2, (pf_ // 4) * 512:(pf_ // 4 + 1) * 512]
                            for sub in range(2):
                                for half in range(2):
                                    o_t, o_b = po[sub * 2 + half]
                                    MM(o_t[:, :], pa_t[:, sub * 128:(sub + 1) * 128], w2c[:, half, :],
                                       False, pf_ == FC - 1, [pa_b, b_big[pf_ // 4]], [o_b])
                        pend = (f, a_t, a_b) if f < FC else None
                        if f == 3 and tt + 1 < NT5:
                            A1(tt + 1)
                        if f == 18 and tt + 1 < NT5:
                            A2(tt + 1)
                    for sub in range(2):
                        tok0 = tt * TT5 + sub * 128
                        out_t, out_b, out_s = ot.next()
                        for half in range(2):
                            hs = slice(half * 512, (half + 1) * 512)
                            o_t, o_b = po[sub * 2 + half]
                            if half == 0:
                                P.op("act", lambda e, o=out_t[:, hs], a=o_t[:, :]: e.activation(out=o, in_=a, func=AF.Identity),
                                     reads=[o_b], dwrites=[out_b])
                            else:
                                P.op("dve", lambda e, o=out_t[:, hs], a=o_t[:, :]: e.tensor_copy(out=o, in_=a),
                                     reads=[o_b], dwrites=[out_b])
                        DMA("sp", out_d[tok0:tok0 + 128, :], out_t[:], [out_b], [], out_s)
                        if out_s not in P.final_waits:
                            P.final_waits.append(out_s)

                A1(0)
                A2(0)
                for tt in range(NT5):
                    Bst(tt)
        P.emit(nc, top)
    return nc


def _col(v):
    return np.ascontiguousarray(np.asarray(v, np.float32).reshape(KC, 128).T)


def make_in_maps(inputs, cores):
    x = np.asarray(inputs["x"], np.float32)
    c = np.asarray(inputs["c"], np.float32)
    shared = {
        "w_ada": np.ascontiguousarray(inputs["w_ada"][0], dtype=np.float32),
        "b_ada": np.ascontiguousarray(inputs["b_ada"][0].reshape(1, -1), dtype=np.float32),
        "n1g": _col(inputs["norm1_g"][0]),
        "n2g": _col(inputs["norm2_g"][0]),
        "w_in": np.ascontiguousarray(inputs["w_in"][0], dtype=np.float32),
        "bfor": np.ascontiguousarray(np.asarray(inputs["b_forget"][0], np.float32).reshape(H, 1)),
        "qg": np.ascontiguousarray(np.asarray(inputs["q_norm_g"][0], np.float32).reshape(DH, 1)),
        "kg": np.ascontiguousarray(np.asarray(inputs["k_norm_g"][0], np.float32).reshape(DH, 1)),
        "w_attn_proj": np.ascontiguousarray(inputs["w_attn_proj"][0], dtype=np.float32),
        "cwT": np.ascontiguousarray(np.asarray(inputs["conv_w"][0], np.float32).T.reshape(KC, 128, CK).transpose(1, 0, 2)),
        "cb": _col(inputs["conv_b"][0]),
        "lng": _col(inputs["conv_ln_g"][0]),
        "lnb": _col(inputs["conv_ln_b"][0]),
        "w_conv_proj": np.ascontiguousarray(inputs["w_conv_proj"][0], dtype=np.float32),
        "w_out": np.ascontiguousarray(inputs["w_out"][0], dtype=np.float32),
        "w_mlp1": np.ascontiguousarray(inputs["w_mlp1"][0], dtype=np.float32),
        "w_mlp2": np.ascontiguousarray(inputs["w_mlp2"][0], dtype=np.float32),
        "ident": np.eye(128, dtype=np.float32),
        "maskb": np.where(np.arange(128)[None, :] >= np.arange(128)[:, None], 0.0, -30000.0).astype(np.float32),
    }
    maps = []
    for b in cores:
        m = dict(shared)
        m["x"] = np.ascontiguousarray(x[b])
        m["ccol"] = _col(c[b])
        maps.append(m)
    return maps


def kernel(**inputs):
    nc = build(upto=5, debug=False)
    cores = list(range(8))
    in_maps = make_in_maps(inputs, cores)
    res = run_bass_kernel_spmd(nc, in_maps, core_ids=cores)
    out = np.stack([np.asarray(r["out"], dtype=np.float32) for r in res.results], axis=0)
    return out
```

```python
import os
from contextlib import ExitStack
import numpy as np
import concourse.bass as bass
import concourse.mybir as mybir
from concourse.bass_utils import run_bass_kernel_spmd

F32 = mybir.dt.float32
BF16 = mybir.dt.bfloat16
AF = mybir.ActivationFunctionType
ALU = mybir.AluOpType
AX = mybir.AxisListType

S = 4096
D = 1024
H = 16
DH = 64
KC = 8
DFF = 4096
FC = 32
CK = 31
EPS = 1e-6
DIN = 7184
OFF_Q, OFF_K, OFF_V, OFF_F, OFF_GLU, OFF_GATE = 0, 1024, 2048, 3072, 3088, 5136

ENGS = ("pe", "act", "dve", "pool", "sp")
RAW_ONLY = False


class Buf:
    __slots__ = ("name", "w", "rs", "dw")
    excl = False

    def __init__(self, name=""):
        self.name = name
        self.w = None
        self.rs = []
        self.dw = []


class PBuf(Buf):
    __slots__ = ()
    excl = True


class Op:
    __slots__ = ("eng", "fn", "deps", "sig", "val", "idx", "dma", "dsem", "dval")

    def __init__(self, eng, fn, dma):
        self.eng = eng
        self.fn = fn
        self.dma = dma
        self.deps = ([], [])
        self.sig = False
        self.val = 0
        self.dsem = None
        self.dval = 0


class DSem:
    __slots__ = ("name", "issued", "h", "bg")

    def __init__(self, name, bg=False):
        self.name = name
        self.issued = 0
        self.h = None
        self.bg = bg


class Prog:
    def __init__(self, same_engine_sync=True):
        self.ops = {e: [] for e in ENGS}
        self.dsems = []
        self.same_engine_sync = same_engine_sync
        self.final_waits = []

    def dsem(self, name, bg=False):
        s = DSem(name, bg)
        self.dsems.append(s)
        return s

    def op(self, eng, fn, reads=(), writes=(), dsem=None, dwrites=()):
        o = Op(eng, fn, dsem is not None)
        o.idx = len(self.ops[eng])
        deps = []
        raw = set()
        for b in reads:
            if b.w is not None:
                deps.append(b.w)
                raw.add(id(b.w))
            for d in b.dw:
                deps.append(d)
                raw.add(id(d))
            if b.excl:
                deps.extend(r for r in b.rs if r.eng != eng)
        for b in writes:
            if b.w is not None:
                deps.append(b.w)
            deps.extend(b.dw)
            deps.extend(b.rs)
        for b in dwrites:
            if b.w is not None:
                deps.append(b.w)
            deps.extend(b.rs)
        dd = {}
        cd = {}
        for d in deps:
            if d.dma:
                dd[id(d.dsem)] = (d.dsem, d.dsem.issued)
            else:
                if d.eng == eng and not o.dma:
                    if eng == "pe" or not self.same_engine_sync or (RAW_ONLY and id(d) not in raw):
                        continue
                p = cd.get(d.eng)
                if p is None or d.idx > p.idx:
                    cd[d.eng] = d
        o.deps = (list(cd.values()), list(dd.values()))
        for d in cd.values():
            d.sig = True
        if o.dma:
            dsem.issued += 16
            o.dsem = dsem
            o.dval = dsem.issued
        for b in reads:
            if o.dma:
                b.rs = [r for r in b.rs if not (r.dma and r.dsem is o.dsem)]
            else:
                b.rs = [r for r in b.rs if r.dma or r.eng != eng]
            b.rs.append(o)
        for b in writes:
            b.w = o
            b.rs = []
            b.dw = []
        for b in dwrites:
            b.dw = [r for r in b.dw if r.dma or r.eng != eng]
            b.dw.append(o)
        self.ops[eng].append(o)
        return o

    def barrier(self, exclude=()):
        lasts = []
        for e in ENGS:
            for o in reversed(self.ops[e]):
                if not o.dma and o.fn is not None:
                    lasts.append(o)
                    o.sig = True
                    break
        dds = [(s, s.issued) for s in self.dsems if s.issued > 0 and s not in exclude and not s.bg]
        for e in ENGS:
            o = Op(e, None, False)
            o.idx = len(self.ops[e])
            o.deps = ([d for d in lasts if d.eng != e], list(dds))
            self.ops[e].append(o)

    def emit(self, nc, stack):
        esem = {e: stack.enter_context(nc.semaphore("s_" + e)) for e in ENGS}
        for s in self.dsems:
            s.h = stack.enter_context(nc.semaphore("d_" + s.name))
        for e in ENGS:
            c = 0
            for o in self.ops[e]:
                if o.dma or o.fn is None:
                    continue
                if o.sig:
                    c += 1
                    o.val = c
        block = stack.enter_context(nc.Block())
        secs = {"pe": block.tensor, "act": block.scalar, "dve": block.vector,
                "pool": block.gpsimd, "sp": block.sync}
        for e in ENGS:
            ops = self.ops[e]
            final = self.final_waits if e == "sp" else []

            def section(eng, ops=ops, e=e, final=final):
                known = {}
                for o in ops:
                    cds, dds = o.deps
                    for d in cds:
                        key = ("e", d.eng)
                        if known.get(key, 0) >= d.val:
                            continue
                        known[key] = d.val
                        eng.wait_ge(esem[d.eng], d.val)
                    for (s, v) in dds:
                        key = ("d", id(s))
                        if known.get(key, 0) >= v:
                            continue
                        known[key] = v
                        eng.wait_ge(s.h, v)
                    if o.fn is None:
                        continue
                    ins = o.fn(eng)
                    if o.dma:
                        ins.then_inc(o.dsem.h, 16)
                    elif o.sig:
                        ins.then_inc(esem[e], 1)
                for s in final:
                    eng.wait_ge(s.h, s.issued)

            secs[e](section)


class Ring:
    def __init__(self, items):
        self.items = items
        self.i = 0

    def next(self):
        it = self.items[self.i % len(self.items)]
        self.i += 1
        return it


def build(upto=5, debug=False):
    nc = bass.Bass("TRN2", target_bir_lowering=False)
    P = Prog()

    def din(name, shape, dt=F32):
        return nc.dram_tensor(name, shape, dt, kind="ExternalInput").ap()

    x_d = din("x", [S, D])
    ccol_d = din("ccol", [128, KC])
    wada_d = din("w_ada", [D, 6 * D])
    bada_d = din("b_ada", [1, 6 * D])
    n1g_d = din("n1g", [128, KC])
    n2g_d = din("n2g", [128, KC])
    win_d = din("w_in", [D, DIN])
    bfor_d = din("bfor", [H, 1])
    qg_d = din("qg", [DH, 1])
    kg_d = din("kg", [DH, 1])
    wap_d = din("w_attn_proj", [D, D])
    cwT_d = din("cwT", [128, KC, CK])
    cb_d = din("cb", [128, KC])
    lng_d = din("lng", [128, KC])
    lnb_d = din("lnb", [128, KC])
    wcp_d = din("w_conv_proj", [D, D])
    wout_d = din("w_out", [D, D])
    w1_d = din("w_mlp1", [D, DFF])
    w2_d = din("w_mlp2", [DFF, D])
    ident_d = din("ident", [128, 128])
    mask_d = din("maskb", [128, 128])

    out_d = nc.dram_tensor("out", [S, D], F32, kind="ExternalOutput").ap()
    skind = "ExternalOutput" if debug else "Internal"
    modscr = nc.dram_tensor("modscr", [1, 6 * D], F32, kind=skind).ap()
    fscr = nc.dram_tensor("fscr", [6, H, S], BF16, kind=skind).ap()
    bgscr = nc.dram_tensor("bgscr", [128, KC, S], BF16, kind=skind).ap()
    mscr = nc.dram_tensor("mscr", [128, KC, S], BF16, kind=skind).ap()
    w1s = nc.dram_tensor("w1s", [D, DFF], BF16, kind="Internal").ap()
    w2s = nc.dram_tensor("w2s", [DFF, D], BF16, kind="Internal").ap()
    wos = nc.dram_tensor("wos", [D, D], BF16, kind="Internal").ap()
    wps = nc.dram_tensor("wps", [D, D], BF16, kind="Internal").ap()
    wgs = nc.dram_tensor("wgs", [D, D], BF16, kind="Internal").ap()
    wcs = nc.dram_tensor("wcs", [D, D], BF16, kind="Internal").ap()
    wbs = nc.dram_tensor("wbs", [D, D], BF16, kind="Internal").ap()
    b_wscr = Buf()
    b_wscr2 = Buf()
    b_modscr, b_fscr = Buf(), Buf()
    b_bg = [Buf() for _ in range(8)]
    b_ms = [Buf() for _ in range(8)]
    if debug:
        dbg_hT = nc.dram_tensor("dbg_hT", [128, KC, S], BF16, kind="ExternalOutput").ap()
        dbg_ao = nc.dram_tensor("dbg_ao", [128, KC, S], BF16, kind="ExternalOutput").ap()

    d_out = P.dsem("out")
    P.final_waits.append(d_out)
    d_misc = P.dsem("misc")

    with ExitStack() as top:
        def sb(st, name, shape, dt):
            return st.enter_context(nc.sbuf_tensor("s_" + name, shape, dt))

        def psum(st, name):
            return st.enter_context(nc.psum_tensor("p_" + name, [128, 512], F32))

        def DMA(q, out, in_, reads, writes, dsem):
            return P.op(q, lambda e: e.dma_start(out=out, in_=in_), reads=reads, writes=writes, dsem=dsem)

        def MM(out, lhsT, rhs, start, stop, reads, writes, skip=False):
            return P.op("pe", lambda e: e.matmul(out, lhsT=lhsT, rhs=rhs, start=start, stop=stop,
                                                 skip_group_check=skip), reads=reads, writes=writes)

        def TR(out, in_, ident, reads, writes):
            return P.op("pe", lambda e: e.transpose(out=out, in_=in_, identity=ident), reads=reads, writes=writes)

        def ACT(out, in_, func, reads, writes, bias=None, scale=None, accum=None, dwrites=()):
            def fn(e):
                kw = {}
                if bias is not None:
                    kw["bias"] = bias
                if scale is not None:
                    kw["scale"] = scale
                if accum is not None:
                    kw["accum_out"] = accum
                return e.activation(out=out, in_=in_, func=func, **kw)
            return P.op("act", fn, reads=reads, writes=writes, dwrites=dwrites)

        def TS(eng, out, in0, s1, s2, op0, op1, reads, writes, dwrites=()):
            def fn(e):
                if op1 is None:
                    return e.tensor_scalar(out=out, in0=in0, scalar1=s1, scalar2=None, op0=op0)
                return e.tensor_scalar(out=out, in0=in0, scalar1=s1, scalar2=s2, op0=op0, op1=op1)
            return P.op(eng, fn, reads=reads, writes=writes, dwrites=dwrites)

        def TT(eng, out, in0, in1, op, reads, writes):
            return P.op(eng, lambda e: e.tensor_tensor(out=out, in0=in0, in1=in1, op=op), reads=reads, writes=writes)

        def STT(out, in0, scalar, in1, op0, op1, reads, writes):
            return P.op("dve", lambda e: e.scalar_tensor_tensor(out=out, in0=in0, scalar=scalar, in1=in1, op0=op0, op1=op1),
                        reads=reads, writes=writes)

        def CP(eng, out, in_, reads, writes):
            return P.op(eng, lambda e: e.tensor_copy(out=out, in_=in_), reads=reads, writes=writes)

        def MSET(eng, ap, val, writes):
            return P.op(eng, lambda e: e.memset(ap, val), writes=writes)

        ident = sb(top, "ident", [128, 128], F32); b_ident = Buf()
        identb = sb(top, "identb", [128, 128], BF16); b_identb = Buf()
        maskb = sb(top, "maskb_s", [128, 128], BF16); b_maskb = Buf()
        onesf = sb(top, "onesf", [128, 512], F32); b_onesf = Buf()
        modcol = sb(top, "modcol", [128, 48], F32); b_modcol = Buf()
        gs1 = sb(top, "gs1", [128, KC], F32); b_gs1 = Buf()
        gs2 = sb(top, "gs2", [128, KC], F32); b_gs2 = Buf()
        cbc = sb(top, "cbc", [128, KC], F32); b_cbc = Buf()
        lngc = sb(top, "lngc", [128, KC], F32); b_lngc = Buf()
        lnbc = sb(top, "lnbc", [128, KC], F32); b_lnbc = Buf()
        gqk = sb(top, "gqk", [DH, 1], F32); b_gqk = Buf()
        epsc = sb(top, "epsc", [128, 1], F32); b_epsc = Buf()
        onec = sb(top, "onec", [128, 1], F32); b_onec = Buf()

        DMA("sp", ident[:], ident_d, [], [b_ident], d_misc)
        DMA("pool", identb[:], ident_d, [], [b_identb], d_misc)
        DMA("pool", maskb[:], mask_d, [], [b_maskb], d_misc)
        DMA("sp", cbc[:], cb_d, [], [b_cbc], d_misc)
        DMA("sp", lngc[:], lng_d, [], [b_lngc], d_misc)
        DMA("sp", lnbc[:], lnb_d, [], [b_lnbc], d_misc)
        MSET("dve", onesf[:], 1.0, [b_onesf])
        MSET("dve", epsc[:], EPS, [b_epsc])
        MSET("dve", onec[:], 1.0, [b_onec])

        stA = top.enter_context(ExitStack())
        hT = sb(stA, "hT", [128, KC, S], BF16)
        b_hT = [Buf() for _ in range(8)]

        with ExitStack() as st:
            pm = psum(st, "pm"); b_pm = PBuf()
            pc = psum(st, "pc"); b_pc = PBuf()
            ptr = [(psum(st, "ptrA%d" % i), PBuf(), psum(st, "ptrB%d" % i), PBuf()) for i in range(2)]
            ccol = sb(st, "ccol", [128, KC], F32); b_ccol = Buf()
            cact = sb(st, "cact", [128, KC], F32); b_cact = Buf()
            badar = sb(st, "badar", [1, 6 * D], F32); b_badar = Buf()
            modrow = sb(st, "modrow", [1, 6 * D], F32); b_modrow = Buf()
            n1gc = sb(st, "n1gc", [128, KC], F32); b_n1gc = Buf()
            n2gc = sb(st, "n2gc", [128, KC], F32); b_n2gc = Buf()
            qgc = sb(st, "qgc", [DH, 1], F32); b_qgc = Buf()
            kgc = sb(st, "kgc", [DH, 1], F32); b_kgc = Buf()
            wst = Ring([(sb(st, "wst%d" % i, [128, KC, 512], F32), Buf(), P.dsem("wst%d" % i)) for i in range(2)])
            xt = Ring([(sb(st, "xt%d" % i, [128, D], F32), Buf(), P.dsem("xt%d" % i)) for i in range(4)])
            xn = Ring([(sb(st, "xn%d" % i, [128, D], F32), Buf()) for i in range(3)])
            junk = sb(st, "junk", [128, D], BF16); b_junk = Buf()
            ss = Ring([(sb(st, "ss%d" % i, [128, 1], F32), Buf()) for i in range(4)])
            sd = Ring([(sb(st, "sd%d" % i, [128, 1], F32), Buf()) for i in range(4)])
            rs = Ring([(sb(st, "rs%d" % i, [128, 1], F32), Buf()) for i in range(4)])

            DMA("sp", ccol[:], ccol_d, [], [b_ccol], d_misc)
            DMA("sp", badar[:], bada_d, [], [b_badar], d_misc)
            DMA("sp", n1gc[:], n1g_d, [], [b_n1gc], d_misc)
            DMA("sp", n2gc[:], n2g_d, [], [b_n2gc], d_misc)
            DMA("sp", qgc[:], qg_d, [], [b_qgc], d_misc)
            DMA("sp", kgc[:], kg_d, [], [b_kgc], d_misc)
            ACT(cact[:], ccol[:], AF.Silu, [b_ccol], [b_cact])
            STT(gqk[:], qgc[:], 0.125, kgc[:], ALU.mult, ALU.mult, [b_qgc, b_kgc], [b_gqk])

            wada_v = wada_d.rearrange("(kc p) n -> p kc n", p=128)

            def mod_tile(n):
                w_t, w_b, w_s = wst.next()
                DMA("sp", w_t[:], wada_v[:, :, n * 512:(n + 1) * 512], [], [w_b], w_s)
                for kc in range(KC):
                    MM(pm[0:1, :], cact[:, kc:kc + 1], w_t[:, kc, :], kc == 0, kc == KC - 1, [b_cact, w_b], [b_pm])
                TT("dve", modrow[0:1, n * 512:(n + 1) * 512], pm[0:1, :], badar[0:1, n * 512:(n + 1) * 512], ALU.add,
                   [b_pm, b_badar], [b_modrow])

            def mod_cols(j0, j1):
                for j in range(j0, j1):
                    MM(pc[:, j:j + 1], modrow[0:1, j * 128:(j + 1) * 128], onec[0:1, 0:1], True, True,
                       [b_modrow, b_onec], [b_pc], skip=True)
                CP("dve", modcol[:, j0:j1], pc[:, j0:j1], [b_pc], [b_modcol])

            for n in range(4):
                mod_tile(n)
            mod_cols(0, 16)
            STT(gs1[:], modcol[:, 8:16], 1.0, n1gc[:], ALU.add, ALU.mult, [b_modcol, b_n1gc], [b_gs1])

            st1 = {}

            def stage1(i):
                x_t, x_b, x_s = xt.next()
                DMA("sp", x_t[:], x_d[i * 128:(i + 1) * 128, :], [], [x_b], x_s)
                ss_t, ss_b = ss.next()
                sd_t, sd_b = sd.next()
                rs_t, rs_b = rs.next()
                ACT(junk[:], x_t[:], AF.Square, [x_b], [b_junk, ss_b], accum=ss_t[:])
                ACT(sd_t[:], ss_t[:], AF.Sqrt, [ss_b, b_epsc], [sd_b], bias=epsc[:], scale=1.0 / D)
                P.op("dve", lambda e, o=rs_t, a=sd_t: e.reciprocal(out=o[:], in_=a[:]), reads=[sd_b], writes=[rs_b])
                xn_t, xn_b = xn.next()
                TS("pool", xn_t[:], x_t[:], rs_t[:], 1.0, ALU.mult, ALU.mult, [x_b, rs_b], [xn_b])
                st1[i] = (xn_t, xn_b)

            def stage2(i):
                xn_t, xn_b = st1.pop(i)
                pA, bA, pB, bB = ptr[i % 2]
                for kc in range(KC):
                    pp, bp = (pA, bA) if kc < 4 else (pB, bB)
                    TR(pp[:, (kc % 4) * 128:(kc % 4 + 1) * 128], xn_t[:, kc * 128:(kc + 1) * 128], ident[:],
                       [xn_b, b_ident], [bp])
                for kc in range(KC):
                    pp, bp = (pA, bA) if kc < 4 else (pB, bB)
                    src = pp[:, (kc % 4) * 128:(kc % 4 + 1) * 128]
                    dst = hT[:, kc, i * 128:(i + 1) * 128]
                    if kc < 4:
                        ACT(dst, src, AF.Identity, [bp, b_gs1, b_modcol], [], dwrites=[b_hT[i // 4]],
                            bias=modcol[:, kc:kc + 1], scale=gs1[:, kc:kc + 1])
                    else:
                        TS("dve", dst, src, gs1[:, kc:kc + 1], modcol[:, kc:kc + 1], ALU.mult, ALU.add,
                           [bp, b_gs1, b_modcol], [], dwrites=[b_hT[i // 4]])

            NTI = S // 128
            stage1(0)
            stage1(1)
            for i in range(NTI):
                if i + 2 < NTI:
                    stage1(i + 2)
                stage2(i)
                if i % 3 == 2 and 4 + i // 3 < 12:
                    mod_tile(4 + i // 3)
            mod_cols(16, 48)
            STT(gs2[:], modcol[:, 32:40], 1.0, n2gc[:], ALU.add, ALU.mult, [b_modcol, b_n2gc], [b_gs2])
            DMA("sp", modscr, modrow[:], [b_modrow], [b_modscr], d_misc)

            P.barrier()
        with ExitStack() as st:
            pf = psum(st, "pf"); b_pf = PBuf()
            wf = sb(st, "wf", [128, KC, H], BF16); b_wf = Buf()
            nbf = sb(st, "nbf", [H, 1], F32); b_nbf = Buf()
            bfc = sb(st, "bfc", [H, 1], F32); b_bfc = Buf()
            ef = sb(st, "ef", [H, 512], F32); b_ef = Buf()
            spf = sb(st, "spf", [H, S], F32); b_spf = Buf()
            ncum = sb(st, "ncum", [H, S], F32); b_ncum = Buf()
            res = sb(st, "res", [H, S], F32); b_res = Buf()
            fpos = sb(st, "fpos", [H, 3, S], BF16); b_fpos = Buf()
            fneg = sb(st, "fneg", [H, 3, S], BF16); b_fneg = Buf()
            DMA("pool", wf[:], win_d[:, OFF_F:OFF_F + H].rearrange("(kc p) n -> p kc n", p=128), [], [b_wf], d_misc)
            DMA("sp", bfc[:], bfor_d, [], [b_bfc], d_misc)
            TS("dve", nbf[:], bfc[:], -1.0, None, ALU.mult, None, [b_bfc], [b_nbf])
            for t in range(8):
                for kc in range(KC):
                    MM(pf[0:H, :], wf[:, kc, :], hT[:, kc, t * 512:(t + 1) * 512], kc == 0, kc == KC - 1,
                       [b_wf, b_hT[t]], [b_pf])
                ACT(ef[:], pf[0:H, :], AF.Exp, [b_pf, b_nbf], [b_ef], bias=nbf[:], scale=-1.0)
                ACT(spf[:, t * 512:(t + 1) * 512], ef[:], AF.Ln, [b_ef, b_onec], [b_spf], bias=onec[0:H, :])
            for t in range(8):
                sl = slice(t * 512, (t + 1) * 512)
                init = 0.0 if t == 0 else ncum[:, t * 512 - 1:t * 512]
                P.op("dve", lambda e, sl=sl, init=init: e.tensor_tensor_scan(out=ncum[:, sl], data0=onesf[0:H, :], data1=spf[:, sl],
                                                                        initial=init, op0=ALU.mult, op1=ALU.add),
                     reads=[b_onesf, b_spf, b_ncum], writes=[b_ncum])
            CP("dve", fpos[:, 0, :], ncum[:], [b_ncum], [b_fpos])
            TT("dve", res[:], ncum[:], fpos[:, 0, :], ALU.subtract, [b_ncum, b_fpos], [b_res])
            CP("dve", fpos[:, 1, :], res[:], [b_res], [b_fpos])
            TT("dve", res[:], res[:], fpos[:, 1, :], ALU.subtract, [b_res, b_fpos], [b_res])
            CP("dve", fpos[:, 2, :], res[:], [b_res], [b_fpos])
            TS("pool", fneg[:], fpos[:], -1.0, 1.0, ALU.mult, ALU.mult, [b_fpos], [b_fneg])
            DMA("sp", fscr[0:3].rearrange("r h s -> h r s"), fpos[:], [b_fpos], [b_fscr], d_misc)
            DMA("sp", fscr[3:6].rearrange("r h s -> h r s"), fneg[:], [b_fneg], [b_fscr], d_misc)
            if debug:
                DMA("sp", dbg_hT, hT[:], b_hT, [], d_out)
            P.barrier()

        if upto >= 2:
            stB = stA.enter_context(ExitStack())
            BIG = sb(stB, "BIG", [128, KC, S], BF16)
            b_big = [Buf() for _ in range(8)]

        if upto >= 2:
            with ExitStack() as st:
                pa = [(psum(st, "pa%d" % i), PBuf()) for i in range(2)]
                pb = [(psum(st, "pb%d" % i), PBuf()) for i in range(2)]
                py = [(psum(st, "py%d" % i), PBuf()) for i in range(2)]
                cw = sb(st, "cw", [128, KC, CK], F32); b_cw = Buf()
                DMA("sp", cw[:], cwT_d, [], [b_cw], d_misc)
                wgl = Ring([(sb(st, "wgl%d" % i, [128, KC, 2, 128], BF16), Buf(), P.dsem("wgl%d" % i)) for i in range(2)])
                dg = Ring([(sb(st, "dg%d" % i, [128, CK, 128], BF16), Buf()) for i in range(2)])
                ub = Ring([(sb(st, "ub%d" % i, [128, 30 + S], BF16), Buf()) for i in range(2)])
                sg = Ring([(sb(st, "sg%d" % i, [128, 512], F32), Buf()) for i in range(2)])
                for (u_t, u_b) in ub.items:
                    MSET("pool", u_t[:, 0:30], 0.0, [u_b])
                for c in range(KC):
                    g_t, g_b, g_s = wgl.next()
                    DMA("pool", g_t[:, :, 0, :], win_d[:, OFF_GLU + c * 128:OFF_GLU + (c + 1) * 128].rearrange("(kc p) n -> p kc n", p=128),
                        [], [g_b], g_s)
                    DMA("pool", g_t[:, :, 1, :], win_d[:, OFF_GLU + D + c * 128:OFF_GLU + D + (c + 1) * 128].rearrange("(kc p) n -> p kc n", p=128),
                        [], [g_b], g_s)
                    if upto >= 5 and c == 1:
                        d_wscr = P.dsem("wscr", bg=True)
                        d_wscr2 = P.dsem("wscr2", bg=True)
                        DMA("pool", wcs.rearrange("r (h n) -> r h n", h=1), wcp_d.rearrange("r (h n) -> r h n", h=1), [], [b_wscr2], d_wscr2)
                        DMA("pool", wbs.rearrange("r (h n) -> r h n", h=1), win_d[:, OFF_GATE + D:OFF_GATE + 2 * D].rearrange("r (h n) -> r h n", h=1),
                            [], [b_wscr2], d_wscr2)
                        DMA("pool", wps.rearrange("r (h n) -> r h n", h=1), wap_d.rearrange("r (h n) -> r h n", h=1), [], [b_wscr], d_wscr)
                        DMA("pool", wgs.rearrange("r (h n) -> r h n", h=1), win_d[:, OFF_GATE:OFF_GATE + D].rearrange("r (h n) -> r h n", h=1),
                            [], [b_wscr], d_wscr)
                        DMA("pool", wos.rearrange("r (h n) -> r h n", h=1), wout_d.rearrange("r (h n) -> r h n", h=1), [], [b_wscr], d_wscr)
                        for q in range(4):
                            DMA("pool", w1s[q * 256:(q + 1) * 256, :].rearrange("r (h n) -> r h n", h=2),
                                w1_d[q * 256:(q + 1) * 256, :].rearrange("r (h n) -> r h n", h=2), [], [b_wscr], d_wscr)
                        for q in range(4):
                            DMA("pool", w2s[q * 1024:(q + 1) * 1024, :].rearrange("r (h n) -> r h n", h=1),
                                w2_d[q * 1024:(q + 1) * 1024, :].rearrange("r (h n) -> r h n", h=1), [], [b_wscr], d_wscr)

                    d_t, d_b = dg.next()
                    for k in range(CK):
                        TS("dve", d_t[:, k, :], ident[:], cw[:, c, k:k + 1], None, ALU.mult, None, [b_ident, b_cw], [d_b])
                    u_t, u_b = ub.next()
                    for t in range(8):
                        (pa_t, pa_b), (pb_t, pb_b) = pa[t % 2], pb[t % 2]
                        for kc in range(KC):
                            MM(pa_t[:, :], g_t[:, kc, 0, :], hT[:, kc, t * 512:(t + 1) * 512], kc == 0, kc == KC - 1,
                               [g_b, b_hT[t]], [pa_b])
                        for kc in range(KC):
                            MM(pb_t[:, :], g_t[:, kc, 1, :], hT[:, kc, t * 512:(t + 1) * 512], kc == 0, kc == KC - 1,
                               [g_b, b_hT[t]], [pb_b])
                        s_t, s_b = sg.next()
                        ACT(s_t[:], pb_t[:, :], AF.Sigmoid, [pb_b], [s_b])
                        TT("dve", u_t[:, 30 + t * 512:30 + (t + 1) * 512], pa_t[:, :], s_t[:], ALU.mult, [pa_b, s_b], [u_b])
                    for t in range(8):
                        y_t, y_b = py[t % 2]
                        for k in range(CK):
                            MM(y_t[:, :], d_t[:, k, :], u_t[:, t * 512 + k:t * 512 + k + 512], k == 0, k == CK - 1,
                               [d_b, u_b], [y_b])
                        ACT(BIG[:, c, t * 512:(t + 1) * 512], y_t[:, :], AF.Identity, [y_b, b_cbc], [b_big[t]],
                            bias=cbc[:, c:c + 1])
                P.barrier()
            with ExitStack() as st:
                pa = [(psum(st, "pa2%d" % i), PBuf()) for i in range(2)]
                pb = [(psum(st, "pb2%d" % i), PBuf()) for i in range(2)]
                wcp = sb(st, "wcp", [128, KC, D], BF16); b_wcp = Buf(); d_wcp = P.dsem("wcp")
                wgb = sb(st, "wgb", [128, KC, D], BF16); b_wgb = Buf(); d_wgb = P.dsem("wgb")
                if upto >= 5:
                    DMA("sp", wcp[:], wcs.rearrange("(kc p) n -> p kc n", p=128), [b_wscr2], [b_wcp], d_wcp)
                    DMA("sp", wgb[:], wbs.rearrange("(kc p) n -> p kc n", p=128), [b_wscr2], [b_wgb], d_wgb)
                else:
                    for kc in range(KC):
                        DMA("pool", wcp[:, kc, :], wcp_d[kc * 128:(kc + 1) * 128, :], [], [b_wcp], d_wcp)
                        DMA("pool", wgb[:, kc, :], win_d[kc * 128:(kc + 1) * 128, OFF_GATE + D:OFF_GATE + 2 * D], [], [b_wgb], d_wgb)
                pmean = psum(st, "pmean"); b_pmean = PBuf()
                pmsq = psum(st, "pmsq"); b_pmsq = PBuf()
                onesb = sb(st, "onesb", [128, 128], BF16); b_onesb = Buf()
                MSET("dve", onesb[:], 1.0 / D, [b_onesb])
                ysq = sb(st, "ysq", [128, KC, 512], BF16); b_ysq = Buf()
                mean_s = sb(st, "mean_s", [128, 512], F32); b_mean = Buf()
                var_s = sb(st, "var_s", [128, 512], F32); b_var = Buf()
                rstd_s = var_s; b_rstd = b_var
                zc = Ring([(sb(st, "zc%d" % i, [128, 512], F32), Buf()) for i in range(2)])
                zbr = Ring([(sb(st, "zb%d" % i, [128, KC, 512], BF16), Buf()) for i in range(2)])
                sgb = Ring([(sb(st, "sgb%d" % i, [128, 512], F32), Buf()) for i in range(1)])
                bgt = Ring([(sb(st, "bgt%d" % i, [128, KC, 512], BF16), Buf(), P.dsem("bgt%d" % i)) for i in range(1)])
                zs = Ring([(sb(st, "zs%d" % i, [128, 512], BF16), Buf()) for i in range(2)])
                zcur = {}

                def S1(t):
                    sl = slice(t * 512, (t + 1) * 512)
                    TT("pool", ysq[:], BIG[:, :, sl], BIG[:, :, sl], ALU.mult, [b_big[t]], [b_ysq])
                    for c in range(KC):
                        MM(pmean[:, :], onesb[:], BIG[:, c, sl], c == 0, c == KC - 1, [b_onesb, b_big[t]], [b_pmean])
                    for c in range(KC):
                        MM(pmsq[:, :], onesb[:], ysq[:, c, :], c == 0, c == KC - 1, [b_onesb, b_ysq], [b_pmsq])

                def S2_pieces(t):
                    sl = slice(t * 512, (t + 1) * 512)
                    zb_t, zb_b = zbr.next()
                    zcur[t] = (zb_t, zb_b)

                    def stats():
                        CP("dve", mean_s[:], pmean[:, :], [b_pmean], [b_mean])
                        TT("dve", var_s[:], mean_s[:], mean_s[:], ALU.mult, [b_mean], [b_var])
                        TT("dve", var_s[:], pmsq[:, :], var_s[:], ALU.subtract, [b_pmsq, b_var], [b_var])
                        ACT(var_s[:], var_s[:], AF.Ln, [b_var, b_epsc], [b_var], bias=epsc[:])
                        ACT(rstd_s[:], var_s[:], AF.Exp, [b_var], [b_rstd], scale=-0.5)

                    def zchunk(c):
                        z_t, z_b = zc.next()
                        s_t, s_b = zs.next()
                        TT("pool", z_t[:], BIG[:, c, sl], mean_s[:], ALU.subtract, [b_big[t], b_mean], [z_b])
                        TT("pool", z_t[:], z_t[:], rstd_s[:], ALU.mult, [z_b, b_rstd], [z_b])
                        ACT(s_t[:], z_t[:], AF.Sigmoid, [z_b, b_lngc, b_lnbc], [s_b],
                            bias=lnbc[:, c:c + 1], scale=lngc[:, c:c + 1])
                        TS("dve", z_t[:], z_t[:], lngc[:, c:c + 1], lnbc[:, c:c + 1], ALU.mult, ALU.add, [z_b, b_lngc, b_lnbc], [z_b])
                        P.op("dve", lambda e, o=zb_t[:, c, :], a=z_t[:], b_=s_t[:]: e.tensor_tensor(out=o, in0=a, in1=b_, op=ALU.mult),
                             reads=[z_b, s_b], dwrites=[zb_b])
                    return [stats, lambda: (zchunk(0), zchunk(1)), lambda: (zchunk(2), zchunk(3)), lambda: zchunk(4),
                            lambda: zchunk(5), lambda: zchunk(6), lambda: zchunk(7), lambda: None]

                def Mo(t, o, o_t, o_b):
                    sl = slice(t * 512, (t + 1) * 512)
                    zb_t, zb_b = zcur[t]
                    (pa_t, pa_b), (pb_t, pb_b) = pa[o % 2], pb[o % 2]
                    for c in range(KC):
                        MM(pa_t[:, :], wcp[:, c, o * 128:(o + 1) * 128], zb_t[:, c, :], c == 0, c == KC - 1,
                           [b_wcp, zb_b], [pa_b])
                    for kc in range(KC):
                        MM(pb_t[:, :], wgb[:, kc, o * 128:(o + 1) * 128], hT[:, kc, sl], kc == 0, kc == KC - 1,
                           [b_wgb, b_hT[t]], [pb_b])
                    s_t, s_b = sgb.next()
                    ACT(s_t[:], pb_t[:, :], AF.Sigmoid, [pb_b], [s_b])
                    TT("dve", o_t[:, o, :], pa_t[:, :], s_t[:], ALU.mult, [pa_b, s_b], [o_b])

                S1(0)
                for p_ in S2_pieces(0):
                    p_()
                for t in range(8):
                    sl = slice(t * 512, (t + 1) * 512)
                    pieces = []
                    if t + 1 < 8:
                        S1(t + 1)
                        pieces = S2_pieces(t + 1)
                    o_t, o_b, o_s = bgt.next()
                    for o in range(KC):
                        Mo(t, o, o_t, o_b)
                        if pieces:
                            pieces.pop(0)()
                    DMA("sp", bgscr[:, :, sl], o_t[:], [o_b], [b_bg[t]], o_s)
                P.barrier()

        if upto >= 3:
            with ExitStack() as st:
                pS = Ring([(psum(st, "pS%d" % i), PBuf()) for i in range(3)])
                pO = Ring([(psum(st, "pO%d" % i), PBuf()) for i in range(2)])
                pBc = psum(st, "pBc"); b_pBc = PBuf()
                pT = pBc; b_pT = b_pBc
                pQ = Ring([(psum(st, "pQ%d" % i), PBuf()) for i in range(2)])
                qaug = [(sb(st, "qaug%d" % i, [70, S], BF16), Buf(), P.dsem("qaug%d" % i)) for i in range(2)]
                kaug = [(sb(st, "kaug%d" % i, [70, S], BF16), Buf(), P.dsem("kaug%d" % i)) for i in range(2)]
                vh = [(sb(st, "vh0", [128, S // 128, 65], BF16), Buf()), (sb(st, "vh1", [128, S // 128, 128], BF16), Buf())]
                wqkv = Ring([(sb(st, "wqkv%d" % i, [128, KC, 3, 128], BF16), Buf(), P.dsem("wqkv%d" % i)) for i in range(2)])
                qkraw = Ring([(sb(st, "qkraw%d" % i, [128, 8, 2, DH], F32), Buf()) for i in range(2)])
                sqs = sb(st, "sqs", [128, 8, 2, DH], F32); b_sqs = Buf()
                ssq = sb(st, "ssq", [128, 16], F32); b_ssq = Buf()
                rsq = sb(st, "rsq", [128, 16], F32); b_rsq = Buf()
                nhalf = sb(st, "nhalf", [128, 16], F32); b_nhalf = Buf()
                MSET("pool", nhalf[:], -0.5, [b_nhalf])
                PT = Ring([(sb(st, "PT%d" % i, [128, 512], BF16), Buf()) for i in range(4)])
                rden = Ring([(sb(st, "rden%d" % i, [128, 512], F32), Buf()) for i in range(1)])
                bcs = Ring([(sb(st, "bcs%d" % i, [128, 512], F32), Buf()) for i in range(1)])
                for i in range(2):
                    MSET("pool", qaug[i][0][64:70, :], 1.0, [qaug[i][1]])
                    MSET("pool", kaug[i][0][64:70, :], 1.0, [kaug[i][1]])
                MSET("pool", vh[0][0][:, :, 64:65], 1.0, [vh[0][1]])
                MSET("pool", vh[1][0][:, :, 0:64], 0.0, [vh[1][1]])
                MSET("pool", vh[1][0][:, :, 0:1], 1.0, [vh[1][1]])
                cur_w = [None]
                def inproj_gen(h):
                    e = h % 2
                    if e == 0:
                        w_t, w_b, w_s = wqkv.next()
                        c0 = (h // 2) * 128
                        for j, off in enumerate((OFF_Q, OFF_K, OFF_V)):
                            DMA("pool", w_t[:, :, j, :], win_d[:, off + c0:off + c0 + 128].rearrange("(kc p) n -> p kc n", p=128),
                                [], [w_b], w_s)
                        cur_w[0] = (w_t, w_b)
                    w_t, w_b = cur_w[0]
                    q_t, q_b, q_s = qaug[e]
                    k_t, k_b, k_s = kaug[e]
                    v_t, v_b = vh[e]
                    DMA("sp", q_t[64:67, :], fscr[3:6, h, :], [b_fscr], [q_b], q_s)
                    DMA("sp", k_t[67:70, :], fscr[0:3, h, :], [b_fscr], [k_b], k_s)
                    yield
                    voff = 0 if e == 0 else 64
                    pending_tr = []

                    def tr_steps(grp, qr_t, qr_b):
                        steps = []
                        for g2 in range(2):
                            g = grp * 2 + g2
                            for which in range(2):
                                def step(g=g, g2=g2, which=which):
                                    for ii in range(4):
                                        TR(pT[0:64, ii * 128:(ii + 1) * 128], qr_t[:, g2 * 4 + ii, which, :], ident[:],
                                           [qr_b, b_ident], [b_pT])
                                    if which == 0:
                                        CP("dve", q_t[0:64, g * 512:(g + 1) * 512], pT[0:64, :], [b_pT], [q_b])
                                    else:
                                        TS("dve", k_t[0:64, g * 512:(g + 1) * 512], pT[0:64, :], gqk[:], None, ALU.mult, None,
                                           [b_pT, b_gqk], [k_b])
                                steps.append(step)
                        return steps

                    for grp in range(4):
                        qr_t, qr_b = qkraw.next()
                        qk3 = qr_t[:].rearrange("p a b d -> p (a b) d")
                        for il in range(8):
                            i = grp * 8 + il
                            p_t, p_b = pQ.next()
                            for kc in range(KC):
                                MM(p_t[:, 0:192].rearrange("p (a b) -> p a b", a=3), hT[:, kc, i * 128:(i + 1) * 128],
                                   w_t[:, kc, :, e * 64:(e + 1) * 64], kc == 0, kc == KC - 1, [b_hT[i // 4], w_b], [p_b])
                            CP("dve", qr_t[:, il, :, :], p_t[:, 0:128].rearrange("p (a b) -> p a b", a=2), [p_b], [qr_b])
                            CP("dve", v_t[:, i, voff:voff + 64], p_t[:, 128:192], [p_b], [v_b])
                            if pending_tr and il >= 4:
                                pending_tr.pop(0)()
                            yield
                        TT("pool", sqs[:], qr_t[:], qr_t[:], ALU.mult, [qr_b], [b_sqs])
                        P.op("dve", lambda e_: e_.tensor_reduce(out=ssq[:], in_=sqs[:].rearrange("p a b d -> p (a b) d"), axis=AX.X, op=ALU.add),
                             reads=[b_sqs], writes=[b_ssq])
                        TS("pool", ssq[:], ssq[:], 1.0 / DH, EPS, ALU.mult, ALU.add, [b_ssq], [b_ssq])
                        TT("pool", rsq[:], ssq[:], nhalf[:], ALU.pow, [b_ssq, b_nhalf], [b_rsq])
                        TT("pool", qk3, qk3, rsq[:].unsqueeze(2).to_broadcast([128, 16, DH]), ALU.mult, [qr_b, b_rsq], [qr_b])
                        yield
                        pending_tr = tr_steps(grp, qr_t, qr_b)
                    yield
                    yield
                    while pending_tr:
                        pending_tr.pop(0)()
                        yield

                LAG = 3
                pend = []
                defer = []
                otile = {}

                def emit_S(h, j, i):
                    e = h % 2
                    q_t, q_b, _ = qaug[e]
                    k_t, k_b, _ = kaug[e]
                    s_t, s_b = pS.next()
                    kl = k_t[0:70, i * 128:(i + 1) * 128]
                    if i < 4 * j:
                        c0 = 0
                        MM(s_t[:, :], kl, q_t[0:70, j * 512:(j + 1) * 512], True, True, [k_b, q_b], [s_b])
                    else:
                        c0 = (i - 4 * j) * 128
                        MM(s_t[:, c0:c0 + 128], kl, q_t[0:70, j * 512 + c0:j * 512 + c0 + 128], True, False,
                           [k_b, q_b], [s_b], skip=True)
                        MM(s_t[:, c0:c0 + 128], identb[:], maskb[:], False, True, [b_identb, b_maskb], [s_b], skip=True)
                        if c0 + 128 < 512:
                            MM(s_t[:, c0 + 128:512], kl, q_t[0:70, j * 512 + c0 + 128:(j + 1) * 512], True, True,
                               [k_b, q_b], [s_b], skip=True)
                    p_t, p_b = PT.next()
                    ACT(p_t[:, c0:512], s_t[:, c0:512], AF.Exp, [s_b], [p_b])
                    pend.append((h, j, i, c0, p_t, p_b))

                def emit_PV():
                    h, j, i, c0, p_t, p_b = pend.pop(0)
                    e = h % 2
                    c = h // 2
                    v_t, v_b = vh[e]
                    M = 65 if e == 0 else 128
                    p0 = 64 if e == 0 else 0
                    o0 = 0 if e == 0 else 64
                    nblk = 4 * (j + 1)
                    if i == 0:
                        otile[(h, j)] = pO.next()
                    o_t, o_b = otile[(h, j)]
                    MM(o_t[0:M, c0:512], v_t[:, i, 0:M], p_t[:, c0:512], i == 0, i == nblk - 1, [v_b, p_b], [o_b], skip=True)
                    if i == nblk - 1:
                        del otile[(h, j)]
                        r_t, r_b = rden.next()
                        ACT(r_t[p0:p0 + 1, :], o_t[p0:p0 + 1, :], AF.Ln, [o_b], [r_b])
                        ACT(r_t[p0:p0 + 1, :], r_t[p0:p0 + 1, :], AF.Exp, [r_b], [r_b], scale=-1.0)

                        def tail():
                            MM(pBc[:, :], onesf[p0:p0 + 1, 0:128], r_t[p0:p0 + 1, :], True, True, [b_onesf, r_b], [b_pBc])
                            b_t, b_b = bcs.next()
                            CP("dve", b_t[o0:o0 + 64, :], pBc[o0:o0 + 64, :], [b_pBc], [b_b])
                            TT("dve", BIG[o0:o0 + 64, c, j * 512:(j + 1) * 512], o_t[o0:o0 + 64, :], b_t[o0:o0 + 64, :], ALU.mult,
                               [o_b, b_b], [b_big[j]])
                        defer.append([3, tail])

                def tick_defer(force=False):
                    for d in list(defer):
                        d[0] -= 1
                        if d[0] <= 0 or force:
                            d[1]()
                            defer.remove(d)

                g0 = inproj_gen(0)
                for _ in g0:
                    pass
                for h in range(H):
                    nxt = inproj_gen(h + 1) if h + 1 < H else None
                    cnt = 0
                    for j in range(8):
                        for i in range(4 * (j + 1)):
                            emit_S(h, j, i)
                            if len(pend) > LAG:
                                emit_PV()
                            tick_defer()
                            cnt += 1
                            if nxt is not None and cnt % 3 == 0:
                                next(nxt, None)
                    if nxt is not None:
                        for _ in nxt:
                            pass
                while pend:
                    emit_PV()
                    tick_defer()
                tick_defer(force=True)
                tick_defer(force=True)
                if debug:
                    DMA("sp", dbg_ao, BIG[:], b_big, [], d_out)
                P.barrier()

        if upto >= 4:
            with ExitStack() as st:
                pa = [(psum(st, "p4a%d" % i), PBuf()) for i in range(2)]
                pb = [(psum(st, "p4b%d" % i), PBuf()) for i in range(2)]
                wap = sb(st, "wap", [128, KC, D], BF16); b_wap = Buf(); d_wap = P.dsem("wap")
                wga = sb(st, "wga", [128, KC, D], BF16); b_wga = Buf(); d_wga = P.dsem("wga")
                if upto >= 5:
                    DMA("sp", wap[:], wps.rearrange("(kc p) n -> p kc n", p=128), [b_wscr], [b_wap], d_wap)
                    DMA("sp", wga[:], wgs.rearrange("(kc p) n -> p kc n", p=128), [b_wscr], [b_wga], d_wga)
                else:
                    for kc in range(KC):
                        DMA("pool", wap[:, kc, :], wap_d[kc * 128:(kc + 1) * 128, :], [], [b_wap], d_wap)
                        DMA("pool", wga[:, kc, :], win_d[kc * 128:(kc + 1) * 128, OFF_GATE:OFF_GATE + D], [], [b_wga], d_wga)
                bgl = Ring([(sb(st, "bgl%d" % i, [128, KC, 512], BF16), Buf(), P.dsem("bgl%d" % i)) for i in range(2)])
                mgt = Ring([(sb(st, "mgt%d" % i, [128, KC, 512], BF16), Buf(), P.dsem("mgt%d" % i)) for i in range(2)])
                gb2 = sb(st, "gb2", [128, D], F32); b_gb2 = Buf()
                DMA("sp", gb2[:], modscr[0:1, 5 * D:6 * D].broadcast_to([128, D]), [b_modscr], [b_gb2], d_misc)

                def scale_w2(t_):
                    for k_ in range(4):
                        w2c_ = BIG[:, 2 * k_:2 * k_ + 2, t_ * 512:(t_ + 1) * 512]
                        TT("pool", w2c_, w2c_, gb2[:].rearrange("p (h n) -> p h n", h=2), ALU.mult, [b_big[t_], b_gb2], [b_big[t_]])
                sga = Ring([(sb(st, "sga%d" % i, [128, 512], F32), Buf()) for i in range(2)])
                tmp = Ring([(sb(st, "tmp4%d" % i, [128, 512], F32), Buf()) for i in range(2)])
                d_w1 = P.dsem("w1"); d_w2 = P.dsem("w2")
                b_w2 = [Buf() for _ in range(4)]
                w1v = hT
                w2v = BIG[:].rearrange("p c t -> p (c t)").rearrange("p (f n) -> p f n", n=D)
                nxt_l = bgl.next()
                DMA("sp", nxt_l[0][:], bgscr[:, :, 0:512], [b_bg[0]], [nxt_l[1]], nxt_l[2])
                for t in range(8):
                    sl = slice(t * 512, (t + 1) * 512)
                    l_t, l_b, l_s = nxt_l
                    m_t, m_b, m_s = mgt.next()
                    for o in range(KC):
                        (pa_t, pa_b), (pb_t, pb_b) = pa[o % 2], pb[o % 2]
                        for c in range(KC):
                            MM(pa_t[:, :], wap[:, c, o * 128:(o + 1) * 128], BIG[:, c, sl], c == 0, c == KC - 1,
                               [b_wap, b_big[t]], [pa_b])
                        for kc in range(KC):
                            MM(pb_t[:, :], wga[:, kc, o * 128:(o + 1) * 128], hT[:, kc, sl], kc == 0, kc == KC - 1,
                               [b_wga, b_hT[t]], [pb_b])
                        s_t, s_b = sga.next()
                        ACT(s_t[:], pb_t[:, :], AF.Sigmoid, [pb_b], [s_b])
                        t_t, t_b = tmp.next()
                        TT("dve", t_t[:], pa_t[:, :], s_t[:], ALU.mult, [pa_b, s_b], [t_b])
                        TT("dve", m_t[:, o, :], t_t[:], l_t[:, o, :], ALU.add, [t_b, l_b], [m_b])
                    DMA("sp", mscr[:, :, sl], m_t[:], [m_b], [b_ms[t]], m_s)
                    if t + 1 < 8:
                        nxt_l = bgl.next()
                        DMA("sp", nxt_l[0][:], bgscr[:, :, (t + 1) * 512:(t + 2) * 512], [b_bg[t + 1]], [nxt_l[1]], nxt_l[2])
                    if upto >= 5:
                        DMA("sp", hT[:, :, sl], w1s[:, sl].rearrange("(kc p) n -> p kc n", p=128), [b_wscr], [b_hT[t]], d_w1)
                        for f in range(4 * t, 4 * t + 4):
                            DMA("sp", BIG[:, (f % 4) * 2:(f % 4) * 2 + 2, sl],
                                w2s[f * 128:(f + 1) * 128, :].rearrange("p (h n) -> p h n", h=2), [b_wscr], [b_big[t]], d_w2)
                    if upto >= 5 and t >= 1:
                        scale_w2(t - 1)
                if upto >= 5:
                    scale_w2(7)
                P.barrier(exclude=(d_w1,))

        if upto >= 5:
            with ExitStack() as st:
                TT5 = 256
                NT5 = S // TT5
                wo = sb(st, "wo", [128, KC, D], BF16); b_wo = Buf(); d_wo = P.dsem("wo")
                w1 = w1v
                w2 = w2v
                identf = ident
                pw = Ring([(psum(st, "pw%d" % i), PBuf()) for i in range(2)])
                ph = Ring([(psum(st, "ph%d" % i), PBuf()) for i in range(2)])
                po = [(psum(st, "po%d" % i), PBuf()) for i in range(4)]
                mgl = Ring([(sb(st, "mgl%d" % i, [128, KC, TT5], BF16), Buf(), P.dsem("mgl%d" % i)) for i in range(2)])
                xl = Ring([(sb(st, "xl%d" % i, [128, D], F32), Buf(), P.dsem("xl%d" % i)) for i in range(2)])
                x1 = Ring([(sb(st, "x1_%d" % i, [128, D], F32), Buf()) for i in range(2)])
                xn2 = Ring([(sb(st, "xn2_%d" % i, [128, D], F32), Buf()) for i in range(2)])
                junk5 = sb(st, "junk5", [128, D], BF16); b_junk5 = Buf()
                ss5 = Ring([(sb(st, "ss5_%d" % i, [128, 1], F32), Buf()) for i in range(4)])
                sd5 = Ring([(sb(st, "sd5_%d" % i, [128, 1], F32), Buf()) for i in range(4)])
                rs5 = Ring([(sb(st, "rs5_%d" % i, [128, 1], F32), Buf()) for i in range(4)])
                h2T = Ring([(sb(st, "h2T%d" % i, [128, KC, TT5], BF16), Buf()) for i in range(2)])
                rl = Ring([(sb(st, "rl%d" % i, [128, TT5], F32), Buf()) for i in range(2)])
                aT = Ring([(sb(st, "aT%d" % i, [128, TT5], BF16), Buf()) for i in range(3)])
                ot = Ring([(sb(st, "ot%d" % i, [128, D], F32), Buf(), P.dsem("ot%d" % i)) for i in range(2)])

                gb_t, gb_b, _ = ot.items[0]
                DMA("sp", wo[:], wos.rearrange("(kc p) n -> p kc n", p=128), [b_wscr], [b_wo], d_wo)
                DMA("sp", gb_t[:], modscr[0:1, 2 * D:3 * D].broadcast_to([128, D]), [b_modscr], [gb_b], d_misc)
                for kc in range(KC):
                    TT("dve", wo[:, kc, :], wo[:, kc, :], gb_t[:], ALU.mult, [b_wo, gb_b], [b_wo])
                stA5 = {}

                def A1(tt):
                    m_t, m_b, m_s = mgl.next()
                    DMA("sp", m_t[:], mscr[:, :, tt * TT5:(tt + 1) * TT5], [b_ms[tt // 2]], [m_b], m_s)
                    x1s, ns = [], []
                    for sub in range(2):
                        tok0 = tt * TT5 + sub * 128
                        x_t, x_b, x_s = xl.next()
                        DMA("sp", x_t[:], x_d[tok0:tok0 + 128, :], [], [x_b], x_s)
                        x1_t, x1_b = x1.next()
                        x1s.append((x1_t, x1_b))
                        for half in range(2):
                            hs = slice(half * 512, (half + 1) * 512)
                            p_t, p_b = pw.next()
                            for kc in range(KC):
                                MM(p_t[:, :], m_t[:, kc, sub * 128:(sub + 1) * 128], wo[:, kc, hs],
                                   kc == 0, kc == KC - 1, [m_b, b_wo], [p_b])
                            TT("dve", x1_t[:, hs], p_t[:, :], x_t[:, hs], ALU.add, [p_b, x_b], [], ) if False else \
                                P.op("dve", lambda e, o=x1_t[:, hs], a=p_t[:, :], b_=x_t[:, hs]: e.tensor_tensor(out=o, in0=a, in1=b_, op=ALU.add),
                                     reads=[p_b, x_b], dwrites=[x1_b])
                        ss_t, ss_b = ss5.next()
                        sd_t, sd_b = sd5.next()
                        rs_t, rs_b = rs5.next()
                        ACT(junk5[:], x1_t[:], AF.Square, [x1_b], [b_junk5, ss_b], accum=ss_t[:])
                        ACT(sd_t[:], ss_t[:], AF.Sqrt, [ss_b, b_epsc], [sd_b], bias=epsc[:], scale=1.0 / D)
                        P.op("dve", lambda e, o=rs_t, a=sd_t: e.reciprocal(out=o[:], in_=a[:]), reads=[sd_b], writes=[rs_b])
                        n_t, n_b = xn2.next()
                        TS("pool", n_t[:], x1_t[:], rs_t[:], 1.0, ALU.mult, ALU.mult, [x1_b, rs_b], [n_b])
                        ns.append((n_t, n_b))
                    stA5[tt] = (x1s, ns)

                def A2(tt):
                    x1s, ns = stA5[tt]
                    h_t, h_b = h2T.next()
                    for sub in range(2):
                        n_t, n_b = ns[sub]
                        for half in range(2):
                            p_t, p_b = pw.next()
                            for k4 in range(4):
                                kc = half * 4 + k4
                                TR(p_t[:, k4 * 128:(k4 + 1) * 128], n_t[:, kc * 128:(kc + 1) * 128], ident[:], [n_b, b_ident], [p_b])
                            for k4 in range(4):
                                kc = half * 4 + k4
                                src_ = p_t[:, k4 * 128:(k4 + 1) * 128]
                                dst = h_t[:, kc, sub * 128:(sub + 1) * 128]
                                if half == 0:
                                    ACT(dst, src_, AF.Identity, [p_b, b_gs2, b_modcol], [], dwrites=[h_b],
                                        bias=modcol[:, 24 + kc:25 + kc], scale=gs2[:, kc:kc + 1])
                                else:
                                    TS("dve", dst, src_, gs2[:, kc:kc + 1], modcol[:, 24 + kc:25 + kc], ALU.mult, ALU.add,
                                       [p_b, b_gs2, b_modcol], [], dwrites=[h_b])
                    stA5[tt] = (x1s, ns, h_t, h_b)

                def Bst(tt):
                    x1s, ns, h_t, h_b = stA5.pop(tt)
                    for sub in range(2):
                        x1_t, x1_b = x1s[sub]
                        for half in range(2):
                            o_t, o_b = po[sub * 2 + half]
                            MM(o_t[:, :], identf[:], x1_t[:, half * 512:(half + 1) * 512], True, False, [b_ident, x1_b], [o_b])
                    pend = None
                    for f in range(FC + 1):
                        if f < FC:
                            p_t, p_b = ph.next()
                            for kc in range(KC):
                                MM(p_t[:, 0:TT5], w1[:, kc, f * 128:(f + 1) * 128], h_t[:, kc, :], kc == 0, kc == KC - 1,
                                   [b_hT[f // 4], h_b], [p_b])
                            r_t, r_b = rl.next()
                            ACT(r_t[:], p_t[:, 0:TT5], AF.Relu, [p_b], [r_b])
                            a_t, a_b = aT.next()
                            TT("dve", a_t[:], r_t[:], r_t[:], ALU.mult, [r_b], [a_b])
                        if pend is not None:
                            pf_, pa_t, pa_b = pend
                            w2c = BIG[:, (pf_ % 4) * 2:(pf_ % 4) * 2 + 2, (pf_ // 4) * 512:(pf_ // 4 + 1) * 512]
                            for sub in range(2):
                                for half in range(2):
                                    o_t, o_b = po[sub * 2 + half]
                                    MM(o_t[:, :], pa_t[:, sub * 128:(sub + 1) * 128], w2c[:, half, :],
                                       False, pf_ == FC - 1, [pa_b, b_big[pf_ // 4]], [o_b])
                        pend = (f, a_t, a_b) if f < FC else None
                        if f == 3 and tt + 1 < NT5:
                            A1(tt + 1)
                        if f == 18 and tt + 1 < NT5:
                            A2(tt + 1)
                    for sub in range(2):
                        tok0 = tt * TT5 + sub * 128
                        out_t, out_b, out_s = ot.next()
                        for half in range(2):
                            hs = slice(half * 512, (half + 1) * 512)
                            o_t, o_b = po[sub * 2 + half]
                            if half == 0:
                                P.op("act", lambda e, o=out_t[:, hs], a=o_t[:, :]: e.activation(out=o, in_=a, func=AF.Identity),
                                     reads=[o_b], dwrites=[out_b])
                            else:
                                P.op("dve", lambda e, o=out_t[:, hs], a=o_t[:, :]: e.tensor_copy(out=o, in_=a),
                                     reads=[o_b], dwrites=[out_b])
                        DMA("sp", out_d[tok0:tok0 + 128, :], out_t[:], [out_b], [], out_s)
                        if out_s not in P.final_waits:
                            P.final_waits.append(out_s)

                A1(0)
                A2(0)
                for tt in range(NT5):
                    Bst(tt)
        P.emit(nc, top)
    return nc


def _col(v):
    return np.ascontiguousarray(np.asarray(v, np.float32).reshape(KC, 128).T)


def make_in_maps(inputs, cores):
    x = np.asarray(inputs["x"], np.float32)
    c = np.asarray(inputs["c"], np.float32)
    shared = {
        "w_ada": np.ascontiguousarray(inputs["w_ada"][0], dtype=np.float32),
        "b_ada": np.ascontiguousarray(inputs["b_ada"][0].reshape(1, -1), dtype=np.float32),
        "n1g": _col(inputs["norm1_g"][0]),
        "n2g": _col(inputs["norm2_g"][0]),
        "w_in": np.ascontiguousarray(inputs["w_in"][0], dtype=np.float32),
        "bfor": np.ascontiguousarray(np.asarray(inputs["b_forget"][0], np.float32).reshape(H, 1)),
        "qg": np.ascontiguousarray(np.asarray(inputs["q_norm_g"][0], np.float32).reshape(DH, 1)),
        "kg": np.ascontiguousarray(np.asarray(inputs["k_norm_g"][0], np.float32).reshape(DH, 1)),
        "w_attn_proj": np.ascontiguousarray(inputs["w_attn_proj"][0], dtype=np.float32),
        "cwT": np.ascontiguousarray(np.asarray(inputs["conv_w"][0], np.float32).T.reshape(KC, 128, CK).transpose(1, 0, 2)),
        "cb": _col(inputs["conv_b"][0]),
        "lng": _col(inputs["conv_ln_g"][0]),
        "lnb": _col(inputs["conv_ln_b"][0]),
        "w_conv_proj": np.ascontiguousarray(inputs["w_conv_proj"][0], dtype=np.float32),
        "w_out": np.ascontiguousarray(inputs["w_out"][0], dtype=np.float32),
        "w_mlp1": np.ascontiguousarray(inputs["w_mlp1"][0], dtype=np.float32),
        "w_mlp2": np.ascontiguousarray(inputs["w_mlp2"][0], dtype=np.float32),
        "ident": np.eye(128, dtype=np.float32),
        "maskb": np.where(np.arange(128)[None, :] >= np.arange(128)[:, None], 0.0, -30000.0).astype(np.float32),
    }
    maps = []
    for b in cores:
        m = dict(shared)
        m["x"] = np.ascontiguousarray(x[b])
        m["ccol"] = _col(c[b])
        maps.append(m)
    return maps


def kernel(**inputs):
    nc = build(upto=5, debug=False)
    cores = list(range(8))
    in_maps = make_in_maps(inputs, cores)
    res = run_bass_kernel_spmd(nc, in_maps, core_ids=cores)
    out = np.stack([np.asarray(r["out"], dtype=np.float32) for r in res.results], axis=0)
    return out
```

```python
import os
from contextlib import ExitStack
import numpy as np
import concourse.bass as bass
import concourse.mybir as mybir
from concourse.bass_utils import run_bass_kernel_spmd

F32 = mybir.dt.float32
BF16 = mybir.dt.bfloat16
AF = mybir.ActivationFunctionType
ALU = mybir.AluOpType
AX = mybir.AxisListType

S = 4096
D = 1024
H = 16
DH = 64
KC = 8
DFF = 4096
FC = 32
CK = 31
EPS = 1e-6
DIN = 7184
OFF_Q, OFF_K, OFF_V, OFF_F, OFF_GLU, OFF_GATE = 0, 1024, 2048, 3072, 3088, 5136

ENGS = ("pe", "act", "dve", "pool", "sp")
RAW_ONLY = False


class Buf:
    __slots__ = ("name", "w", "rs", "dw")
    excl = False

    def __init__(self, name=""):
        self.name = name
        self.w = None
        self.rs = []
        self.dw = []


class PBuf(Buf):
    __slots__ = ()
    excl = True


class Op:
    __slots__ = ("eng", "fn", "deps", "sig", "val", "idx", "dma", "dsem", "dval")

    def __init__(self, eng, fn, dma):
        self.eng = eng
        self.fn = fn
        self.dma = dma
        self.deps = ([], [])
        self.sig = False
        self.val = 0
        self.dsem = None
        self.dval = 0


class DSem:
    __slots__ = ("name", "issued", "h", "bg")

    def __init__(self, name, bg=False):
        self.name = name
        self.issued = 0
        self.h = None
        self.bg = bg


class Prog:
    def __init__(self, same_engine_sync=True):
        self.ops = {e: [] for e in ENGS}
        self.dsems = []
        self.same_engine_sync = same_engine_sync
        self.final_waits = []

    def dsem(self, name, bg=False):
        s = DSem(name, bg)
        self.dsems.append(s)
        return s

    def op(self, eng, fn, reads=(), writes=(), dsem=None, dwrites=()):
        o = Op(eng, fn, dsem is not None)
        o.idx = len(self.ops[eng])
        deps = []
        raw = set()
        for b in reads:
            if b.w is not None:
                deps.append(b.w)
                raw.add(id(b.w))
            for d in b.dw:
                deps.append(d)
                raw.add(id(d))
            if b.excl:
                deps.extend(r for r in b.rs if r.eng != eng)
        for b in writes:
            if b.w is not None:
                deps.append(b.w)
            deps.extend(b.dw)
            deps.extend(b.rs)
        for b in dwrites:
            if b.w is not None:
                deps.append(b.w)
            deps.extend(b.rs)
        dd = {}
        cd = {}
        for d in deps:
            if d.dma:
                dd[id(d.dsem)] = (d.dsem, d.dsem.issued)
            else:
                if d.eng == eng and not o.dma:
                    if eng == "pe" or not self.same_engine_sync or (RAW_ONLY and id(d) not in raw):
                        continue
                p = cd.get(d.eng)
                if p is None or d.idx > p.idx:
                    cd[d.eng] = d
        o.deps = (list(cd.values()), list(dd.values()))
        for d in cd.values():
            d.sig = True
        if o.dma:
            dsem.issued += 16
            o.dsem = dsem
            o.dval = dsem.issued
        for b in reads:
            if o.dma:
                b.rs = [r for r in b.rs if not (r.dma and r.dsem is o.dsem)]
            else:
                b.rs = [r for r in b.rs if r.dma or r.eng != eng]
            b.rs.append(o)
        for b in writes:
            b.w = o
            b.rs = []
            b.dw = []
        for b in dwrites:
            b.dw = [r for r in b.dw if r.dma or r.eng != eng]
            b.dw.append(o)
        self.ops[eng].append(o)
        return o

    def barrier(self, exclude=()):
        lasts = []
        for e in ENGS:
            for o in reversed(self.ops[e]):
                if not o.dma and o.fn is not None:
                    lasts.append(o)
                    o.sig = True
                    break
        dds = [(s, s.issued) for s in self.dsems if s.issued > 0 and s not in exclude and not s.bg]
        for e in ENGS:
            o = Op(e, None, False)
            o.idx = len(self.ops[e])
            o.deps = ([d for d in lasts if d.eng != e], list(dds))
            self.ops[e].append(o)

    def emit(self, nc, stack):
        esem = {e: stack.enter_context(nc.semaphore("s_" + e)) for e in ENGS}
        for s in self.dsems:
            s.h = stack.enter_context(nc.semaphore("d_" + s.name))
        for e in ENGS:
            c = 0
            for o in self.ops[e]:
                if o.dma or o.fn is None:
                    continue
                if o.sig:
                    c += 1
                    o.val = c
        block = stack.enter_context(nc.Block())
        secs = {"pe": block.tensor, "act": block.scalar, "dve": block.vector,
                "pool": block.gpsimd, "sp": block.sync}
        for e in ENGS:
            ops = self.ops[e]
            final = self.final_waits if e == "sp" else []

            def section(eng, ops=ops, e=e, final=final):
                known = {}
                for o in ops:
                    cds, dds = o.deps
                    for d in cds:
                        key = ("e", d.eng)
                        if known.get(key, 0) >= d.val:
                            continue
                        known[key] = d.val
                        eng.wait_ge(esem[d.eng], d.val)
                    for (s, v) in dds:
                        key = ("d", id(s))
                        if known.get(key, 0) >= v:
                            continue
                        known[key] = v
                        eng.wait_ge(s.h, v)
                    if o.fn is None:
                        continue
                    ins = o.fn(eng)
                    if o.dma:
                        ins.then_inc(o.dsem.h, 16)
                    elif o.sig:
                        ins.then_inc(esem[e], 1)
                for s in final:
                    eng.wait_ge(s.h, s.issued)

            secs[e](section)


class Ring:
    def __init__(self, items):
        self.items = items
        self.i = 0

    def next(self):
        it = self.items[self.i % len(self.items)]
        self.i += 1
        return it


def build(upto=5, debug=False):
    nc = bass.Bass("TRN2", target_bir_lowering=False)
    P = Prog()

    def din(name, shape, dt=F32):
        return nc.dram_tensor(name, shape, dt, kind="ExternalInput").ap()

    x_d = din("x", [S, D])
    ccol_d = din("ccol", [128, KC])
    wada_d = din("w_ada", [D, 6 * D])
    bada_d = din("b_ada", [1, 6 * D])
    n1g_d = din("n1g", [128, KC])
    n2g_d = din("n2g", [128, KC])
    win_d = din("w_in", [D, DIN])
    bfor_d = din("bfor", [H, 1])
    qg_d = din("qg", [DH, 1])
    kg_d = din("kg", [DH, 1])
    wap_d = din("w_attn_proj", [D, D])
    cwT_d = din("cwT", [128, KC, CK])
    cb_d = din("cb", [128, KC])
    lng_d = din("lng", [128, KC])
    lnb_d = din("lnb", [128, KC])
    wcp_d = din("w_conv_proj", [D, D])
    wout_d = din("w_out", [D, D])
    w1_d = din("w_mlp1", [D, DFF])
    w2_d = din("w_mlp2", [DFF, D])
    ident_d = din("ident", [128, 128])
    mask_d = din("maskb", [128, 128])

    out_d = nc.dram_tensor("out", [S, D], F32, kind="ExternalOutput").ap()
    skind = "ExternalOutput" if debug else "Internal"
    modscr = nc.dram_tensor("modscr", [1, 6 * D], F32, kind=skind).ap()
    fscr = nc.dram_tensor("fscr", [6, H, S], BF16, kind=skind).ap()
    bgscr = nc.dram_tensor("bgscr", [128, KC, S], BF16, kind=skind).ap()
    mscr = nc.dram_tensor("mscr", [128, KC, S], BF16, kind=skind).ap()
    w1s = nc.dram_tensor("w1s", [D, DFF], BF16, kind="Internal").ap()
    w2s = nc.dram_tensor("w2s", [DFF, D], BF16, kind="Internal").ap()
    wos = nc.dram_tensor("wos", [D, D], BF16, kind="Internal").ap()
    wps = nc.dram_tensor("wps", [D, D], BF16, kind="Internal").ap()
    wgs = nc.dram_tensor("wgs", [D, D], BF16, kind="Internal").ap()
    wcs = nc.dram_tensor("wcs", [D, D], BF16, kind="Internal").ap()
    wbs = nc.dram_tensor("wbs", [D, D], BF16, kind="Internal").ap()
    b_wscr = Buf()
    b_wscr2 = Buf()
    b_modscr, b_fscr = Buf(), Buf()
    b_bg = [Buf() for _ in range(8)]
    b_ms = [Buf() for _ in range(8)]
    if debug:
        dbg_hT = nc.dram_tensor("dbg_hT", [128, KC, S], BF16, kind="ExternalOutput").ap()
        dbg_ao = nc.dram_tensor("dbg_ao", [128, KC, S], BF16, kind="ExternalOutput").ap()

    d_out = P.dsem("out")
    P.final_waits.append(d_out)
    d_misc = P.dsem("misc")

    with ExitStack() as top:
        def sb(st, name, shape, dt):
            return st.enter_context(nc.sbuf_tensor("s_" + name, shape, dt))

        def psum(st, name):
            return st.enter_context(nc.psum_tensor("p_" + name, [128, 512], F32))

        def DMA(q, out, in_, reads, writes, dsem):
            return P.op(q, lambda e: e.dma_start(out=out, in_=in_), reads=reads, writes=writes, dsem=dsem)

        def MM(out, lhsT, rhs, start, stop, reads, writes, skip=False):
            return P.op("pe", lambda e: e.matmul(out, lhsT=lhsT, rhs=rhs, start=start, stop=stop,
                                                 skip_group_check=skip), reads=reads, writes=writes)

        def TR(out, in_, ident, reads, writes):
            return P.op("pe", lambda e: e.transpose(out=out, in_=in_, identity=ident), reads=reads, writes=writes)

        def ACT(out, in_, func, reads, writes, bias=None, scale=None, accum=None, dwrites=()):
            def fn(e):
                kw = {}
                if bias is not None:
                    kw["bias"] = bias
                if scale is not None:
                    kw["scale"] = scale
                if accum is not None:
                    kw["accum_out"] = accum
                return e.activation(out=out, in_=in_, func=func, **kw)
            return P.op("act", fn, reads=reads, writes=writes, dwrites=dwrites)

        def TS(eng, out, in0, s1, s2, op0, op1, reads, writes, dwrites=()):
            def fn(e):
                if op1 is None:
                    return e.tensor_scalar(out=out, in0=in0, scalar1=s1, scalar2=None, op0=op0)
                return e.tensor_scalar(out=out, in0=in0, scalar1=s1, scalar2=s2, op0=op0, op1=op1)
            return P.op(eng, fn, reads=reads, writes=writes, dwrites=dwrites)

        def TT(eng, out, in0, in1, op, reads, writes):
            return P.op(eng, lambda e: e.tensor_tensor(out=out, in0=in0, in1=in1, op=op), reads=reads, writes=writes)

        def STT(out, in0, scalar, in1, op0, op1, reads, writes):
            return P.op("dve", lambda e: e.scalar_tensor_tensor(out=out, in0=in0, scalar=scalar, in1=in1, op0=op0, op1=op1),
                        reads=reads, writes=writes)

        def CP(eng, out, in_, reads, writes):
            return P.op(eng, lambda e: e.tensor_copy(out=out, in_=in_), reads=reads, writes=writes)

        def MSET(eng, ap, val, writes):
            return P.op(eng, lambda e: e.memset(ap, val), writes=writes)

        ident = sb(top, "ident", [128, 128], F32); b_ident = Buf()
        identb = sb(top, "identb", [128, 128], BF16); b_identb = Buf()
        maskb = sb(top, "maskb_s", [128, 128], BF16); b_maskb = Buf()
        onesf = sb(top, "onesf", [128, 512], F32); b_onesf = Buf()
        modcol = sb(top, "modcol", [128, 48], F32); b_modcol = Buf()
        gs1 = sb(top, "gs1", [128, KC], F32); b_gs1 = Buf()
        gs2 = sb(top, "gs2", [128, KC], F32); b_gs2 = Buf()
        cbc = sb(top, "cbc", [128, KC], F32); b_cbc = Buf()
        lngc = sb(top, "lngc", [128, KC], F32); b_lngc = Buf()
        lnbc = sb(top, "lnbc", [128, KC], F32); b_lnbc = Buf()
        gqk = sb(top, "gqk", [DH, 1], F32); b_gqk = Buf()
        epsc = sb(top, "epsc", [128, 1], F32); b_epsc = Buf()
        onec = sb(top, "onec", [128, 1], F32); b_onec = Buf()

        DMA("sp", ident[:], ident_d, [], [b_ident], d_misc)
        DMA("pool", identb[:], ident_d, [], [b_identb], d_misc)
        DMA("pool", maskb[:], mask_d, [], [b_maskb], d_misc)
        DMA("sp", cbc[:], cb_d, [], [b_cbc], d_misc)
        DMA("sp", lngc[:], lng_d, [], [b_lngc], d_misc)
        DMA("sp", lnbc[:], lnb_d, [], [b_lnbc], d_misc)
        MSET("dve", onesf[:], 1.0, [b_onesf])
        MSET("dve", epsc[:], EPS, [b_epsc])
        MSET("dve", onec[:], 1.0, [b_onec])

        stA = top.enter_context(ExitStack())
        hT = sb(stA, "hT", [128, KC, S], BF16)
        b_hT = [Buf() for _ in range(8)]

        with ExitStack() as st:
            pm = psum(st, "pm"); b_pm = PBuf()
            pc = psum(st, "pc"); b_pc = PBuf()
            ptr = [(psum(st, "ptrA%d" % i), PBuf(), psum(st, "ptrB%d" % i), PBuf()) for i in range(2)]
            ccol = sb(st, "ccol", [128, KC], F32); b_ccol = Buf()
            cact = sb(st, "cact", [128, KC], F32); b_cact = Buf()
            badar = sb(st, "badar", [1, 6 * D], F32); b_badar = Buf()
            modrow = sb(st, "modrow", [1, 6 * D], F32); b_modrow = Buf()
            n1gc = sb(st, "n1gc", [128, KC], F32); b_n1gc = Buf()
            n2gc = sb(st, "n2gc", [128, KC], F32); b_n2gc = Buf()
            qgc = sb(st, "qgc", [DH, 1], F32); b_qgc = Buf()
            kgc = sb(st, "kgc", [DH, 1], F32); b_kgc = Buf()
            wst = Ring([(sb(st, "wst%d" % i, [128, KC, 512], F32), Buf(), P.dsem("wst%d" % i)) for i in range(2)])
            xt = Ring([(sb(st, "xt%d" % i, [128, D], F32), Buf(), P.dsem("xt%d" % i)) for i in range(4)])
            xn = Ring([(sb(st, "xn%d" % i, [128, D], F32), Buf()) for i in range(3)])
            junk = sb(st, "junk", [128, D], BF16); b_junk = Buf()
            ss = Ring([(sb(st, "ss%d" % i, [128, 1], F32), Buf()) for i in range(4)])
            sd = Ring([(sb(st, "sd%d" % i, [128, 1], F32), Buf()) for i in range(4)])
            rs = Ring([(sb(st, "rs%d" % i, [128, 1], F32), Buf()) for i in range(4)])

            DMA("sp", ccol[:], ccol_d, [], [b_ccol], d_misc)
            DMA("sp", badar[:], bada_d, [], [b_badar], d_misc)
            DMA("sp", n1gc[:], n1g_d, [], [b_n1gc], d_misc)
            DMA("sp", n2gc[:], n2g_d, [], [b_n2gc], d_misc)
            DMA("sp", qgc[:], qg_d, [], [b_qgc], d_misc)
            DMA("sp", kgc[:], kg_d, [], [b_kgc], d_misc)
            ACT(cact[:], ccol[:], AF.Silu, [b_ccol], [b_cact])
            STT(gqk[:], qgc[:], 0.125, kgc[:], ALU.mult, ALU.mult, [b_qgc, b_kgc], [b_gqk])

            wada_v = wada_d.rearrange("(kc p) n -> p kc n", p=128)

            def mod_tile(n):
                w_t, w_b, w_s = wst.next()
                DMA("sp", w_t[:], wada_v[:, :, n * 512:(n + 1) * 512], [], [w_b], w_s)
                for kc in range(KC):
                    MM(pm[0:1, :], cact[:, kc:kc + 1], w_t[:, kc, :], kc == 0, kc == KC - 1, [b_cact, w_b], [b_pm])
                TT("dve", modrow[0:1, n * 512:(n + 1) * 512], pm[0:1, :], badar[0:1, n * 512:(n + 1) * 512], ALU.add,
                   [b_pm, b_badar], [b_modrow])

            def mod_cols(j0, j1):
                for j in range(j0, j1):
                    MM(pc[:, j:j + 1], modrow[0:1, j * 128:(j + 1) * 128], onec[0:1, 0:1], True, True,
                       [b_modrow, b_onec], [b_pc], skip=True)
                CP("dve", modcol[:, j0:j1], pc[:, j0:j1], [b_pc], [b_modcol])

            for n in range(4):
                mod_tile(n)
            mod_cols(0, 16)
            STT(gs1[:], modcol[:, 8:16], 1.0, n1gc[:], ALU.add, ALU.mult, [b_modcol, b_n1gc], [b_gs1])

            st1 = {}

            def stage1(i):
                x_t, x_b, x_s = xt.next()
                DMA("sp", x_t[:], x_d[i * 128:(i + 1) * 128, :], [], [x_b], x_s)
                ss_t, ss_b = ss.next()
                sd_t, sd_b = sd.next()
                rs_t, rs_b = rs.next()
                ACT(junk[:], x_t[:], AF.Square, [x_b], [b_junk, ss_b], accum=ss_t[:])
                ACT(sd_t[:], ss_t[:], AF.Sqrt, [ss_b, b_epsc], [sd_b], bias=epsc[:], scale=1.0 / D)
                P.op("dve", lambda e, o=rs_t, a=sd_t: e.reciprocal(out=o[:], in_=a[:]), reads=[sd_b], writes=[rs_b])
                xn_t, xn_b = xn.next()
                TS("pool", xn_t[:], x_t[:], rs_t[:], 1.0, ALU.mult, ALU.mult, [x_b, rs_b], [xn_b])
                st1[i] = (xn_t, xn_b)

            def stage2(i):
                xn_t, xn_b = st1.pop(i)
                pA, bA, pB, bB = ptr[i % 2]
                for kc in range(KC):
                    pp, bp = (pA, bA) if kc < 4 else (pB, bB)
                    TR(pp[:, (kc % 4) * 128:(kc % 4 + 1) * 128], xn_t[:, kc * 128:(kc + 1) * 128], ident[:],
                       [xn_b, b_ident], [bp])
                for kc in range(KC):
                    pp, bp = (pA, bA) if kc < 4 else (pB, bB)
                    src = pp[:, (kc % 4) * 128:(kc % 4 + 1) * 128]
                    dst = hT[:, kc, i * 128:(i + 1) * 128]
                    if kc < 4:
                        ACT(dst, src, AF.Identity, [bp, b_gs1, b_modcol], [], dwrites=[b_hT[i // 4]],
                            bias=modcol[:, kc:kc + 1], scale=gs1[:, kc:kc + 1])
                    else:
                        TS("dve", dst, src, gs1[:, kc:kc + 1], modcol[:, kc:kc + 1], ALU.mult, ALU.add,
                           [bp, b_gs1, b_modcol], [], dwrites=[b_hT[i // 4]])

            NTI = S // 128
            stage1(0)
            stage1(1)
            for i in range(NTI):
                if i + 2 < NTI:
                    stage1(i + 2)
                stage2(i)
                if i % 3 == 2 and 4 + i // 3 < 12:
                    mod_tile(4 + i // 3)
            mod_cols(16, 48)
            STT(gs2[:], modcol[:, 32:40], 1.0, n2gc[:], ALU.add, ALU.mult, [b_modcol, b_n2gc], [b_gs2])
            DMA("sp", modscr, modrow[:], [b_modrow], [b_modscr], d_misc)

            P.barrier()
        with ExitStack() as st:
            pf = psum(st, "pf"); b_pf = PBuf()
            wf = sb(st, "wf", [128, KC, H], BF16); b_wf = Buf()
            nbf = sb(st, "nbf", [H, 1], F32); b_nbf = Buf()
            bfc = sb(st, "bfc", [H, 1], F32); b_bfc = Buf()
            ef = sb(st, "ef", [H, 512], F32); b_ef = Buf()
            spf = sb(st, "spf", [H, S], F32); b_spf = Buf()
            ncum = sb(st, "ncum", [H, S], F32); b_ncum = Buf()
            res = sb(st, "res", [H, S], F32); b_res = Buf()
            fpos = sb(st, "fpos", [H, 3, S], BF16); b_fpos = Buf()
            fneg = sb(st, "fneg", [H, 3, S], BF16); b_fneg = Buf()
            DMA("pool", wf[:], win_d[:, OFF_F:OFF_F + H].rearrange("(kc p) n -> p kc n", p=128), [], [b_wf], d_misc)
            DMA("sp", bfc[:], bfor_d, [], [b_bfc], d_misc)
            TS("dve", nbf[:], bfc[:], -1.0, None, ALU.mult, None, [b_bfc], [b_nbf])
            for t in range(8):
                for kc in range(KC):
                    MM(pf[0:H, :], wf[:, kc, :], hT[:, kc, t * 512:(t + 1) * 512], kc == 0, kc == KC - 1,
                       [b_wf, b_hT[t]], [b_pf])
                ACT(ef[:], pf[0:H, :], AF.Exp, [b_pf, b_nbf], [b_ef], bias=nbf[:], scale=-1.0)
                ACT(spf[:, t * 512:(t + 1) * 512], ef[:], AF.Ln, [b_ef, b_onec], [b_spf], bias=onec[0:H, :])
            for t in range(8):
                sl = slice(t * 512, (t + 1) * 512)
                init = 0.0 if t == 0 else ncum[:, t * 512 - 1:t * 512]
                P.op("dve", lambda e, sl=sl, init=init: e.tensor_tensor_scan(out=ncum[:, sl], data0=onesf[0:H, :], data1=spf[:, sl],
                                                                        initial=init, op0=ALU.mult, op1=ALU.add),
                     reads=[b_onesf, b_spf, b_ncum], writes=[b_ncum])
            CP("dve", fpos[:, 0, :], ncum[:], [b_ncum], [b_fpos])
            TT("dve", res[:], ncum[:], fpos[:, 0, :], ALU.subtract, [b_ncum, b_fpos], [b_res])
            CP("dve", fpos[:, 1, :], res[:], [b_res], [b_fpos])
            TT("dve", res[:], res[:], fpos[:, 1, :], ALU.subtract, [b_res, b_fpos], [b_res])
            CP("dve", fpos[:, 2, :], res[:], [b_res], [b_fpos])
            TS("pool", fneg[:], fpos[:], -1.0, 1.0, ALU.mult, ALU.mult, [b_fpos], [b_fneg])
            DMA("sp", fscr[0:3].rearrange("r h s -> h r s"), fpos[:], [b_fpos], [b_fscr], d_misc)
            DMA("sp", fscr[3:6].rearrange("r h s -> h r s"), fneg[:], [b_fneg], [b_fscr], d_misc)
            if debug:
                DMA("sp", dbg_hT, hT[:], b_hT, [], d_out)
            P.barrier()

        if upto >= 2:
            stB = stA.enter_context(ExitStack())
            BIG = sb(stB, "BIG", [128, KC, S], BF16)
            b_big = [Buf() for _ in range(8)]

        if upto >= 2:
            with ExitStack() as st:
                pa = [(psum(st, "pa%d" % i), PBuf()) for i in range(2)]
                pb = [(psum(st, "pb%d" % i), PBuf()) for i in range(2)]
                py = [(psum(st, "py%d" % i), PBuf()) for i in range(2)]
                cw = sb(st, "cw", [128, KC, CK], F32); b_cw = Buf()
                DMA("sp", cw[:], cwT_d, [], [b_cw], d_misc)
                wgl = Ring([(sb(st, "wgl%d" % i, [128, KC, 2, 128], BF16), Buf(), P.dsem("wgl%d" % i)) for i in range(2)])
                dg = Ring([(sb(st, "dg%d" % i, [128, CK, 128], BF16), Buf()) for i in range(2)])
                ub = Ring([(sb(st, "ub%d" % i, [128, 30 + S], BF16), Buf()) for i in range(2)])
                sg = Ring([(sb(st, "sg%d" % i, [128, 512], F32), Buf()) for i in range(2)])
                for (u_t, u_b) in ub.items:
                    MSET("pool", u_t[:, 0:30], 0.0, [u_b])
                for c in range(KC):
                    g_t, g_b, g_s = wgl.next()
                    DMA("pool", g_t[:, :, 0, :], win_d[:, OFF_GLU + c * 128:OFF_GLU + (c + 1) * 128].rearrange("(kc p) n -> p kc n", p=128),
                        [], [g_b], g_s)
                    DMA("pool", g_t[:, :, 1, :], win_d[:, OFF_GLU + D + c * 128:OFF_GLU + D + (c + 1) * 128].rearrange("(kc p) n -> p kc n", p=128),
                        [], [g_b], g_s)
                    if upto >= 5 and c == 1:
                        d_wscr = P.dsem("wscr", bg=True)
                        d_wscr2 = P.dsem("wscr2", bg=True)
                        DMA("pool", wcs.rearrange("r (h n) -> r h n", h=1), wcp_d.rearrange("r (h n) -> r h n", h=1), [], [b_wscr2], d_wscr2)
                        DMA("pool", wbs.rearrange("r (h n) -> r h n", h=1), win_d[:, OFF_GATE + D:OFF_GATE + 2 * D].rearrange("r (h n) -> r h n", h=1),
                            [], [b_wscr2], d_wscr2)
                        DMA("pool", wps.rearrange("r (h n) -> r h n", h=1), wap_d.rearrange("r (h n) -> r h n", h=1), [], [b_wscr], d_wscr)
                        DMA("pool", wgs.rearrange("r (h n) -> r h n", h=1), win_d[:, OFF_GATE:OFF_GATE + D].rearrange("r (h n) -> r h n", h=1),
                            [], [b_wscr], d_wscr)
                        DMA("pool", wos.rearrange("r (h n) -> r h n", h=1), wout_d.rearrange("r (h n) -> r h n", h=1), [], [b_wscr], d_wscr)
                        for q in range(4):
                            DMA("pool", w1s[q * 256:(q + 1) * 256, :].rearrange("r (h n) -> r h n", h=2),
                                w1_d[q * 256:(q + 1) * 256, :].rearrange("r (h n) -> r h n", h=2), [], [b_wscr], d_wscr)
                        for q in range(4):
                            DMA("pool", w2s[q * 1024:(q + 1) * 1024, :].rearrange("r (h n) -> r h n", h=1),
                                w2_d[q * 1024:(q + 1) * 1024, :].rearrange("r (h n) -> r h n", h=1), [], [b_wscr], d_wscr)

                    d_t, d_b = dg.next()
                    for k in range(CK):
                        TS("dve", d_t[:, k, :], ident[:], cw[:, c, k:k + 1], None, ALU.mult, None, [b_ident, b_cw], [d_b])
                    u_t, u_b = ub.next()
                    for t in range(8):
                        (pa_t, pa_b), (pb_t, pb_b) = pa[t % 2], pb[t % 2]
                        for kc in range(KC):
                            MM(pa_t[:, :], g_t[:, kc, 0, :], hT[:, kc, t * 512:(t + 1) * 512], kc == 0, kc == KC - 1,
                               [g_b, b_hT[t]], [pa_b])
                        for kc in range(KC):
                            MM(pb_t[:, :], g_t[:, kc, 1, :], hT[:, kc, t * 512:(t + 1) * 512], kc == 0, kc == KC - 1,
                               [g_b, b_hT[t]], [pb_b])
                        s_t, s_b = sg.next()
                        ACT(s_t[:], pb_t[:, :], AF.Sigmoid, [pb_b], [s_b])
                        TT("dve", u_t[:, 30 + t * 512:30 + (t + 1) * 512], pa_t[:, :], s_t[:], ALU.mult, [pa_b, s_b], [u_b])
                    for t in range(8):
                        y_t, y_b = py[t % 2]
                        for k in range(CK):
                            MM(y_t[:, :], d_t[:, k, :], u_t[:, t * 512 + k:t * 512 + k + 512], k == 0, k == CK - 1,
                               [d_b, u_b], [y_b])
                        ACT(BIG[:, c, t * 512:(t + 1) * 512], y_t[:, :], AF.Identity, [y_b, b_cbc], [b_big[t]],
                            bias=cbc[:, c:c + 1])
                P.barrier()
            with ExitStack() as st:
                pa = [(psum(st, "pa2%d" % i), PBuf()) for i in range(2)]
                pb = [(psum(st, "pb2%d" % i), PBuf()) for i in range(2)]
                wcp = sb(st, "wcp", [128, KC, D], BF16); b_wcp = Buf(); d_wcp = P.dsem("wcp")
                wgb = sb(st, "wgb", [128, KC, D], BF16); b_wgb = Buf(); d_wgb = P.dsem("wgb")
                if upto >= 5:
                    DMA("sp", wcp[:], wcs.rearrange("(kc p) n -> p kc n", p=128), [b_wscr2], [b_wcp], d_wcp)
                    DMA("sp", wgb[:], wbs.rearrange("(kc p) n -> p kc n", p=128), [b_wscr2], [b_wgb], d_wgb)
                else:
                    for kc in range(KC):
                        DMA("pool", wcp[:, kc, :], wcp_d[kc * 128:(kc + 1) * 128, :], [], [b_wcp], d_wcp)
                        DMA("pool", wgb[:, kc, :], win_d[kc * 128:(kc + 1) * 128, OFF_GATE + D:OFF_GATE + 2 * D], [], [b_wgb], d_wgb)
                pmean = psum(st, "pmean"); b_pmean = PBuf()
                pmsq = psum(st, "pmsq"); b_pmsq = PBuf()
                onesb = sb(st, "onesb", [128, 128], BF16); b_onesb = Buf()
                MSET("dve", onesb[:], 1.0 / D, [b_onesb])
                ysq = sb(st, "ysq", [128, KC, 512], BF16); b_ysq = Buf()
                mean_s = sb(st, "mean_s", [128, 512], F32); b_mean = Buf()
                var_s = sb(st, "var_s", [128, 512], F32); b_var = Buf()
                rstd_s = var_s; b_rstd = b_var
                zc = Ring([(sb(st, "zc%d" % i, [128, 512], F32), Buf()) for i in range(2)])
                zbr = Ring([(sb(st, "zb%d" % i, [128, KC, 512], BF16), Buf()) for i in range(2)])
                sgb = Ring([(sb(st, "sgb%d" % i, [128, 512], F32), Buf()) for i in range(1)])
                bgt = Ring([(sb(st, "bgt%d" % i, [128, KC, 512], BF16), Buf(), P.dsem("bgt%d" % i)) for i in range(1)])
                zs = Ring([(sb(st, "zs%d" % i, [128, 512], BF16), Buf()) for i in range(2)])
                zcur = {}

                def S1(t):
                    sl = slice(t * 512, (t + 1) * 512)
                    TT("pool", ysq[:], BIG[:, :, sl], BIG[:, :, sl], ALU.mult, [b_big[t]], [b_ysq])
                    for c in range(KC):
                        MM(pmean[:, :], onesb[:], BIG[:, c, sl], c == 0, c == KC - 1, [b_onesb, b_big[t]], [b_pmean])
                    for c in range(KC):
                        MM(pmsq[:, :], onesb[:], ysq[:, c, :], c == 0, c == KC - 1, [b_onesb, b_ysq], [b_pmsq])

                def S2_pieces(t):
                    sl = slice(t * 512, (t + 1) * 512)
                    zb_t, zb_b = zbr.next()
                    zcur[t] = (zb_t, zb_b)

                    def stats():
                        CP("dve", mean_s[:], pmean[:, :], [b_pmean], [b_mean])
                        TT("dve", var_s[:], mean_s[:], mean_s[:], ALU.mult, [b_mean], [b_var])
                        TT("dve", var_s[:], pmsq[:, :], var_s[:], ALU.subtract, [b_pmsq, b_var], [b_var])
                        ACT(var_s[:], var_s[:], AF.Ln, [b_var, b_epsc], [b_var], bias=epsc[:])
                        ACT(rstd_s[:], var_s[:], AF.Exp, [b_var], [b_rstd], scale=-0.5)

                    def zchunk(c):
                        z_t, z_b = zc.next()
                        s_t, s_b = zs.next()
                        TT("pool", z_t[:], BIG[:, c, sl], mean_s[:], ALU.subtract, [b_big[t], b_mean], [z_b])
                        TT("pool", z_t[:], z_t[:], rstd_s[:], ALU.mult, [z_b, b_rstd], [z_b])
                        ACT(s_t[:], z_t[:], AF.Sigmoid, [z_b, b_lngc, b_lnbc], [s_b],
                            bias=lnbc[:, c:c + 1], scale=lngc[:, c:c + 1])
                        ACT(z_t[:], z_t[:], AF.Identity, [z_b, b_lngc, b_lnbc], [z_b],
                            bias=lnbc[:, c:c + 1], scale=lngc[:, c:c + 1])
                        P.op("dve", lambda e, o=zb_t[:, c, :], a=z_t[:], b_=s_t[:]: e.tensor_tensor(out=o, in0=a, in1=b_, op=ALU.mult),
                             reads=[z_b, s_b], dwrites=[zb_b])
                    return [stats, lambda: (zchunk(0), zchunk(1)), lambda: (zchunk(2), zchunk(3)), lambda: zchunk(4),
                            lambda: zchunk(5), lambda: zchunk(6), lambda: zchunk(7), lambda: None]

                def Mo(t, o, o_t, o_b):
                    sl = slice(t * 512, (t + 1) * 512)
                    zb_t, zb_b = zcur[t]
                    (pa_t, pa_b), (pb_t, pb_b) = pa[o % 2], pb[o % 2]
                    for c in range(KC):
                        MM(pa_t[:, :], wcp[:, c, o * 128:(o + 1) * 128], zb_t[:, c, :], c == 0, c == KC - 1,
                           [b_wcp, zb_b], [pa_b])
                    for kc in range(KC):
                        MM(pb_t[:, :], wgb[:, kc, o * 128:(o + 1) * 128], hT[:, kc, sl], kc == 0, kc == KC - 1,
                           [b_wgb, b_hT[t]], [pb_b])
                    s_t, s_b = sgb.next()
                    ACT(s_t[:], pb_t[:, :], AF.Sigmoid, [pb_b], [s_b])
                    TT("dve", o_t[:, o, :], pa_t[:, :], s_t[:], ALU.mult, [pa_b, s_b], [o_b])

                S1(0)
                for p_ in S2_pieces(0):
                    p_()
                for t in range(8):
                    sl = slice(t * 512, (t + 1) * 512)
                    pieces = []
                    if t + 1 < 8:
                        S1(t + 1)
                        pieces = S2_pieces(t + 1)
                    o_t, o_b, o_s = bgt.next()
                    for o in range(KC):
                        Mo(t, o, o_t, o_b)
                        if pieces:
                            pieces.pop(0)()
                    DMA("sp", bgscr[:, :, sl], o_t[:], [o_b], [b_bg[t]], o_s)
                P.barrier()

        if upto >= 3:
            with ExitStack() as st:
                pS = Ring([(psum(st, "pS%d" % i), PBuf()) for i in range(3)])
                pO = Ring([(psum(st, "pO%d" % i), PBuf()) for i in range(2)])
                pBc = psum(st, "pBc"); b_pBc = PBuf()
                pT = pBc; b_pT = b_pBc
                pQ = Ring([(psum(st, "pQ%d" % i), PBuf()) for i in range(2)])
                qaug = [(sb(st, "qaug%d" % i, [70, S], BF16), Buf(), P.dsem("qaug%d" % i)) for i in range(2)]
                kaug = [(sb(st, "kaug%d" % i, [70, S], BF16), Buf(), P.dsem("kaug%d" % i)) for i in range(2)]
                vh = [(sb(st, "vh0", [128, S // 128, 65], BF16), Buf()), (sb(st, "vh1", [128, S // 128, 128], BF16), Buf())]
                wqkv = Ring([(sb(st, "wqkv%d" % i, [128, KC, 3, 128], BF16), Buf(), P.dsem("wqkv%d" % i)) for i in range(2)])
                qkraw = Ring([(sb(st, "qkraw%d" % i, [128, 8, 2, DH], F32), Buf()) for i in range(2)])
                sqs = sb(st, "sqs", [128, 8, 2, DH], F32); b_sqs = Buf()
                ssq = sb(st, "ssq", [128, 16], F32); b_ssq = Buf()
                rsq = sb(st, "rsq", [128, 16], F32); b_rsq = Buf()
                nhalf = sb(st, "nhalf", [128, 16], F32); b_nhalf = Buf()
                MSET("pool", nhalf[:], -0.5, [b_nhalf])
                PT = Ring([(sb(st, "PT%d" % i, [128, 512], BF16), Buf()) for i in range(4)])
                rden = Ring([(sb(st, "rden%d" % i, [128, 512], F32), Buf()) for i in range(1)])
                bcs = Ring([(sb(st, "bcs%d" % i, [128, 512], F32), Buf()) for i in range(1)])
                for i in range(2):
                    MSET("pool", qaug[i][0][64:70, :], 1.0, [qaug[i][1]])
                    MSET("pool", kaug[i][0][64:70, :], 1.0, [kaug[i][1]])
                MSET("pool", vh[0][0][:, :, 64:65], 1.0, [vh[0][1]])
                MSET("pool", vh[1][0][:, :, 0:64], 0.0, [vh[1][1]])
                MSET("pool", vh[1][0][:, :, 0:1], 1.0, [vh[1][1]])
                cur_w = [None]
                def inproj_gen(h):
                    e = h % 2
                    if e == 0:
                        w_t, w_b, w_s = wqkv.next()
                        c0 = (h // 2) * 128
                        for j, off in enumerate((OFF_Q, OFF_K, OFF_V)):
                            DMA("pool", w_t[:, :, j, :], win_d[:, off + c0:off + c0 + 128].rearrange("(kc p) n -> p kc n", p=128),
                                [], [w_b], w_s)
                        cur_w[0] = (w_t, w_b)
                    w_t, w_b = cur_w[0]
                    q_t, q_b, q_s = qaug[e]
                    k_t, k_b, k_s = kaug[e]
                    v_t, v_b = vh[e]
                    DMA("sp", q_t[64:67, :], fscr[3:6, h, :], [b_fscr], [q_b], q_s)
                    DMA("sp", k_t[67:70, :], fscr[0:3, h, :], [b_fscr], [k_b], k_s)
                    yield
                    voff = 0 if e == 0 else 64
                    pending_tr = []

                    def tr_steps(grp, qr_t, qr_b):
                        steps = []
                        for g2 in range(2):
                            g = grp * 2 + g2
                            for which in range(2):
                                def step(g=g, g2=g2, which=which):
                                    for ii in range(4):
                                        TR(pT[0:64, ii * 128:(ii + 1) * 128], qr_t[:, g2 * 4 + ii, which, :], ident[:],
                                           [qr_b, b_ident], [b_pT])
                                    if which == 0:
                                        CP("dve", q_t[0:64, g * 512:(g + 1) * 512], pT[0:64, :], [b_pT], [q_b])
                                    else:
                                        TS("dve", k_t[0:64, g * 512:(g + 1) * 512], pT[0:64, :], gqk[:], None, ALU.mult, None,
                                           [b_pT, b_gqk], [k_b])
                                steps.append(step)
                        return steps

                    for grp in range(4):
                        qr_t, qr_b = qkraw.next()
                        qk3 = qr_t[:].rearrange("p a b d -> p (a b) d")
                        for il in range(8):
                            i = grp * 8 + il
                            p_t, p_b = pQ.next()
                            for kc in range(KC):
                                MM(p_t[:, 0:192].rearrange("p (a b) -> p a b", a=3), hT[:, kc, i * 128:(i + 1) * 128],
                                   w_t[:, kc, :, e * 64:(e + 1) * 64], kc == 0, kc == KC - 1, [b_hT[i // 4], w_b], [p_b])
                            CP("dve", qr_t[:, il, :, :], p_t[:, 0:128].rearrange("p (a b) -> p a b", a=2), [p_b], [qr_b])
                            CP("dve", v_t[:, i, voff:voff + 64], p_t[:, 128:192], [p_b], [v_b])
                            if pending_tr and il >= 4:
                                pending_tr.pop(0)()
                            yield
                        TT("pool", sqs[:], qr_t[:], qr_t[:], ALU.mult, [qr_b], [b_sqs])
                        P.op("dve", lambda e_: e_.tensor_reduce(out=ssq[:], in_=sqs[:].rearrange("p a b d -> p (a b) d"), axis=AX.X, op=ALU.add),
                             reads=[b_sqs], writes=[b_ssq])
                        TS("pool", ssq[:], ssq[:], 1.0 / DH, EPS, ALU.mult, ALU.add, [b_ssq], [b_ssq])
                        TT("pool", rsq[:], ssq[:], nhalf[:], ALU.pow, [b_ssq, b_nhalf], [b_rsq])
                        TT("pool", qk3, qk3, rsq[:].unsqueeze(2).to_broadcast([128, 16, DH]), ALU.mult, [qr_b, b_rsq], [qr_b])
                        yield
                        pending_tr = tr_steps(grp, qr_t, qr_b)
                    yield
                    yield
                    while pending_tr:
                        pending_tr.pop(0)()
                        yield

                LAG = 3
                pend = []
                defer = []
                otile = {}

                def emit_S(h, j, i):
                    e = h % 2
                    q_t, q_b, _ = qaug[e]
                    k_t, k_b, _ = kaug[e]
                    s_t, s_b = pS.next()
                    kl = k_t[0:70, i * 128:(i + 1) * 128]
                    if i < 4 * j:
                        c0 = 0
                        MM(s_t[:, :], kl, q_t[0:70, j * 512:(j + 1) * 512], True, True, [k_b, q_b], [s_b])
                    else:
                        c0 = (i - 4 * j) * 128
                        MM(s_t[:, c0:c0 + 128], kl, q_t[0:70, j * 512 + c0:j * 512 + c0 + 128], True, False,
                           [k_b, q_b], [s_b], skip=True)
                        MM(s_t[:, c0:c0 + 128], identb[:], maskb[:], False, True, [b_identb, b_maskb], [s_b], skip=True)
                        if c0 + 128 < 512:
                            MM(s_t[:, c0 + 128:512], kl, q_t[0:70, j * 512 + c0 + 128:(j + 1) * 512], True, True,
                               [k_b, q_b], [s_b], skip=True)
                    p_t, p_b = PT.next()
                    ACT(p_t[:, c0:512], s_t[:, c0:512], AF.Exp, [s_b], [p_b])
                    pend.append((h, j, i, c0, p_t, p_b))

                def emit_PV():
                    h, j, i, c0, p_t, p_b = pend.pop(0)
                    e = h % 2
                    c = h // 2
                    v_t, v_b = vh[e]
                    M = 65 if e == 0 else 128
                    p0 = 64 if e == 0 else 0
                    o0 = 0 if e == 0 else 64
                    nblk = 4 * (j + 1)
                    if i == 0:
                        otile[(h, j)] = pO.next()
                    o_t, o_b = otile[(h, j)]
                    MM(o_t[0:M, c0:512], v_t[:, i, 0:M], p_t[:, c0:512], i == 0, i == nblk - 1, [v_b, p_b], [o_b], skip=True)
                    if i == nblk - 1:
                        del otile[(h, j)]
                        r_t, r_b = rden.next()
                        ACT(r_t[p0:p0 + 1, :], o_t[p0:p0 + 1, :], AF.Ln, [o_b], [r_b])
                        ACT(r_t[p0:p0 + 1, :], r_t[p0:p0 + 1, :], AF.Exp, [r_b], [r_b], scale=-1.0)

                        def tail():
                            MM(pBc[:, :], onesf[p0:p0 + 1, 0:128], r_t[p0:p0 + 1, :], True, True, [b_onesf, r_b], [b_pBc])
                            b_t, b_b = bcs.next()
                            CP("dve", b_t[o0:o0 + 64, :], pBc[o0:o0 + 64, :], [b_pBc], [b_b])
                            TT("dve", BIG[o0:o0 + 64, c, j * 512:(j + 1) * 512], o_t[o0:o0 + 64, :], b_t[o0:o0 + 64, :], ALU.mult,
                               [o_b, b_b], [b_big[j]])
                        defer.append([3, tail])

                def tick_defer(force=False):
                    for d in list(defer):
                        d[0] -= 1
                        if d[0] <= 0 or force:
                            d[1]()
                            defer.remove(d)

                g0 = inproj_gen(0)
                for _ in g0:
                    pass
                for h in range(H):
                    nxt = inproj_gen(h + 1) if h + 1 < H else None
                    cnt = 0
                    for j in range(8):
                        for i in range(4 * (j + 1)):
                            emit_S(h, j, i)
                            if len(pend) > LAG:
                                emit_PV()
                            tick_defer()
                            cnt += 1
                            if nxt is not None and cnt % 3 == 0:
                                next(nxt, None)
                    if nxt is not None:
                        for _ in nxt:
                            pass
                while pend:
                    emit_PV()
                    tick_defer()
                tick_defer(force=True)
                tick_defer(force=True)
                if debug:
                    DMA("sp", dbg_ao, BIG[:], b_big, [], d_out)
                P.barrier()

        if upto >= 4:
            with ExitStack() as st:
                pa = [(psum(st, "p4a%d" % i), PBuf()) for i in range(2)]
                pb = [(psum(st, "p4b%d" % i), PBuf()) for i in range(2)]
                wap = sb(st, "wap", [128, KC, D], BF16); b_wap = Buf(); d_wap = P.dsem("wap")
                wga = sb(st, "wga", [128, KC, D], BF16); b_wga = Buf(); d_wga = P.dsem("wga")
                if upto >= 5:
                    DMA("sp", wap[:], wps.rearrange("(kc p) n -> p kc n", p=128), [b_wscr], [b_wap], d_wap)
                    DMA("sp", wga[:], wgs.rearrange("(kc p) n -> p kc n", p=128), [b_wscr], [b_wga], d_wga)
                else:
                    for kc in range(KC):
                        DMA("pool", wap[:, kc, :], wap_d[kc * 128:(kc + 1) * 128, :], [], [b_wap], d_wap)
                        DMA("pool", wga[:, kc, :], win_d[kc * 128:(kc + 1) * 128, OFF_GATE:OFF_GATE + D], [], [b_wga], d_wga)
                bgl = Ring([(sb(st, "bgl%d" % i, [128, KC, 512], BF16), Buf(), P.dsem("bgl%d" % i)) for i in range(2)])
                mgt = Ring([(sb(st, "mgt%d" % i, [128, KC, 512], BF16), Buf(), P.dsem("mgt%d" % i)) for i in range(2)])
                gb2 = sb(st, "gb2", [128, D], F32); b_gb2 = Buf()
                DMA("sp", gb2[:], modscr[0:1, 5 * D:6 * D].broadcast_to([128, D]), [b_modscr], [b_gb2], d_misc)

                def scale_w2(t_):
                    for k_ in range(4):
                        w2c_ = BIG[:, 2 * k_:2 * k_ + 2, t_ * 512:(t_ + 1) * 512]
                        TT("pool", w2c_, w2c_, gb2[:].rearrange("p (h n) -> p h n", h=2), ALU.mult, [b_big[t_], b_gb2], [b_big[t_]])
                sga = Ring([(sb(st, "sga%d" % i, [128, 512], F32), Buf()) for i in range(2)])
                tmp = Ring([(sb(st, "tmp4%d" % i, [128, 512], F32), Buf()) for i in range(2)])
                d_w1 = P.dsem("w1"); d_w2 = P.dsem("w2")
                b_w2 = [Buf() for _ in range(4)]
                w1v = hT
                w2v = BIG[:].rearrange("p c t -> p (c t)").rearrange("p (f n) -> p f n", n=D)
                nxt_l = bgl.next()
                DMA("sp", nxt_l[0][:], bgscr[:, :, 0:512], [b_bg[0]], [nxt_l[1]], nxt_l[2])
                for t in range(8):
                    sl = slice(t * 512, (t + 1) * 512)
                    l_t, l_b, l_s = nxt_l
                    m_t, m_b, m_s = mgt.next()
                    for o in range(KC):
                        (pa_t, pa_b), (pb_t, pb_b) = pa[o % 2], pb[o % 2]
                        for c in range(KC):
                            MM(pa_t[:, :], wap[:, c, o * 128:(o + 1) * 128], BIG[:, c, sl], c == 0, c == KC - 1,
                               [b_wap, b_big[t]], [pa_b])
                        for kc in range(KC):
                            MM(pb_t[:, :], wga[:, kc, o * 128:(o + 1) * 128], hT[:, kc, sl], kc == 0, kc == KC - 1,
                               [b_wga, b_hT[t]], [pb_b])
                        s_t, s_b = sga.next()
                        ACT(s_t[:], pb_t[:, :], AF.Sigmoid, [pb_b], [s_b])
                        t_t, t_b = tmp.next()
                        TT("dve", t_t[:], pa_t[:, :], s_t[:], ALU.mult, [pa_b, s_b], [t_b])
                        TT("dve", m_t[:, o, :], t_t[:], l_t[:, o, :], ALU.add, [t_b, l_b], [m_b])
                    DMA("sp", mscr[:, :, sl], m_t[:], [m_b], [b_ms[t]], m_s)
                    if t + 1 < 8:
                        nxt_l = bgl.next()
                        DMA("sp", nxt_l[0][:], bgscr[:, :, (t + 1) * 512:(t + 2) * 512], [b_bg[t + 1]], [nxt_l[1]], nxt_l[2])
                    if upto >= 5:
                        DMA("sp", hT[:, :, sl], w1s[:, sl].rearrange("(kc p) n -> p kc n", p=128), [b_wscr], [b_hT[t]], d_w1)
                        for f in range(4 * t, 4 * t + 4):
                            DMA("sp", BIG[:, (f % 4) * 2:(f % 4) * 2 + 2, sl],
                                w2s[f * 128:(f + 1) * 128, :].rearrange("p (h n) -> p h n", h=2), [b_wscr], [b_big[t]], d_w2)
                    if upto >= 5 and t >= 1:
                        scale_w2(t - 1)
                if upto >= 5:
                    scale_w2(7)
                P.barrier(exclude=(d_w1,))

        if upto >= 5:
            with ExitStack() as st:
                TT5 = 256
                NT5 = S // TT5
                wo = sb(st, "wo", [128, KC, D], BF16); b_wo = Buf(); d_wo = P.dsem("wo")
                w1 = w1v
                w2 = w2v
                identf = ident
                pw = Ring([(psum(st, "pw%d" % i), PBuf()) for i in range(2)])
                ph = Ring([(psum(st, "ph%d" % i), PBuf()) for i in range(2)])
                po = [(psum(st, "po%d" % i), PBuf()) for i in range(4)]
                mgl = Ring([(sb(st, "mgl%d" % i, [128, KC, TT5], BF16), Buf(), P.dsem("mgl%d" % i)) for i in range(2)])
                xl = Ring([(sb(st, "xl%d" % i, [128, D], F32), Buf(), P.dsem("xl%d" % i)) for i in range(2)])
                x1 = Ring([(sb(st, "x1_%d" % i, [128, D], F32), Buf()) for i in range(2)])
                xn2 = Ring([(sb(st, "xn2_%d" % i, [128, D], F32), Buf()) for i in range(2)])
                junk5 = sb(st, "junk5", [128, D], BF16); b_junk5 = Buf()
                ss5 = Ring([(sb(st, "ss5_%d" % i, [128, 1], F32), Buf()) for i in range(4)])
                sd5 = Ring([(sb(st, "sd5_%d" % i, [128, 1], F32), Buf()) for i in range(4)])
                rs5 = Ring([(sb(st, "rs5_%d" % i, [128, 1], F32), Buf()) for i in range(4)])
                h2T = Ring([(sb(st, "h2T%d" % i, [128, KC, TT5], BF16), Buf()) for i in range(2)])
                rl = Ring([(sb(st, "rl%d" % i, [128, TT5], F32), Buf()) for i in range(2)])
                aT = Ring([(sb(st, "aT%d" % i, [128, TT5], BF16), Buf()) for i in range(3)])
                ot = Ring([(sb(st, "ot%d" % i, [128, D], F32), Buf(), P.dsem("ot%d" % i)) for i in range(2)])

                gb_t, gb_b, _ = ot.items[0]
                DMA("sp", wo[:], wos.rearrange("(kc p) n -> p kc n", p=128), [b_wscr], [b_wo], d_wo)
                DMA("sp", gb_t[:], modscr[0:1, 2 * D:3 * D].broadcast_to([128, D]), [b_modscr], [gb_b], d_misc)
                for kc in range(KC):
                    TT("dve", wo[:, kc, :], wo[:, kc, :], gb_t[:], ALU.mult, [b_wo, gb_b], [b_wo])
                stA5 = {}

                def A1(tt):
                    m_t, m_b, m_s = mgl.next()
                    DMA("sp", m_t[:], mscr[:, :, tt * TT5:(tt + 1) * TT5], [b_ms[tt // 2]], [m_b], m_s)
                    x1s, ns = [], []
                    for sub in range(2):
                        tok0 = tt * TT5 + sub * 128
                        x_t, x_b, x_s = xl.next()
                        DMA("sp", x_t[:], x_d[tok0:tok0 + 128, :], [], [x_b], x_s)
                        x1_t, x1_b = x1.next()
                        x1s.append((x1_t, x1_b))
                        for half in range(2):
                            hs = slice(half * 512, (half + 1) * 512)
                            p_t, p_b = pw.next()
                            for kc in range(KC):
                                MM(p_t[:, :], m_t[:, kc, sub * 128:(sub + 1) * 128], wo[:, kc, hs],
                                   kc == 0, kc == KC - 1, [m_b, b_wo], [p_b])
                            TT("dve", x1_t[:, hs], p_t[:, :], x_t[:, hs], ALU.add, [p_b, x_b], [], ) if False else \
                                P.op("dve", lambda e, o=x1_t[:, hs], a=p_t[:, :], b_=x_t[:, hs]: e.tensor_tensor(out=o, in0=a, in1=b_, op=ALU.add),
                                     reads=[p_b, x_b], dwrites=[x1_b])
                        ss_t, ss_b = ss5.next()
                        sd_t, sd_b = sd5.next()
                        rs_t, rs_b = rs5.next()
                        ACT(junk5[:], x1_t[:], AF.Square, [x1_b], [b_junk5, ss_b], accum=ss_t[:])
                        ACT(sd_t[:], ss_t[:], AF.Sqrt, [ss_b, b_epsc], [sd_b], bias=epsc[:], scale=1.0 / D)
                        P.op("dve", lambda e, o=rs_t, a=sd_t: e.reciprocal(out=o[:], in_=a[:]), reads=[sd_b], writes=[rs_b])
                        n_t, n_b = xn2.next()
                        TS("pool", n_t[:], x1_t[:], rs_t[:], 1.0, ALU.mult, ALU.mult, [x1_b, rs_b], [n_b])
                        ns.append((n_t, n_b))
                    stA5[tt] = (x1s, ns)

                def A2(tt):
                    x1s, ns = stA5[tt]
                    h_t, h_b = h2T.next()
                    for sub in range(2):
                        n_t, n_b = ns[sub]
                        for half in range(2):
                            p_t, p_b = pw.next()
                            for k4 in range(4):
                                kc = half * 4 + k4
                                TR(p_t[:, k4 * 128:(k4 + 1) * 128], n_t[:, kc * 128:(kc + 1) * 128], ident[:], [n_b, b_ident], [p_b])
                            for k4 in range(4):
                                kc = half * 4 + k4
                                src_ = p_t[:, k4 * 128:(k4 + 1) * 128]
                                dst = h_t[:, kc, sub * 128:(sub + 1) * 128]
                                if half == 0:
                                    ACT(dst, src_, AF.Identity, [p_b, b_gs2, b_modcol], [], dwrites=[h_b],
                                        bias=modcol[:, 24 + kc:25 + kc], scale=gs2[:, kc:kc + 1])
                                else:
                                    TS("dve", dst, src_, gs2[:, kc:kc + 1], modcol[:, 24 + kc:25 + kc], ALU.mult, ALU.add,
                                       [p_b, b_gs2, b_modcol], [], dwrites=[h_b])
                    stA5[tt] = (x1s, ns, h_t, h_b)

                def Bst(tt):
                    x1s, ns, h_t, h_b = stA5.pop(tt)
                    for sub in range(2):
                        x1_t, x1_b = x1s[sub]
                        for half in range(2):
                            o_t, o_b = po[sub * 2 + half]
                            MM(o_t[:, :], identf[:], x1_t[:, half * 512:(half + 1) * 512], True, False, [b_ident, x1_b], [o_b])
                    pend = None
                    for f in range(FC + 1):
                        if f < FC:
                            p_t, p_b = ph.next()
                            for kc in range(KC):
                                MM(p_t[:, 0:TT5], w1[:, kc, f * 128:(f + 1) * 128], h_t[:, kc, :], kc == 0, kc == KC - 1,
                                   [b_hT[f // 4], h_b], [p_b])
                            r_t, r_b = rl.next()
                            ACT(r_t[:], p_t[:, 0:TT5], AF.Relu, [p_b], [r_b])
                            a_t, a_b = aT.next()
                            TT("dve", a_t[:], r_t[:], r_t[:], ALU.mult, [r_b], [a_b])
                        if pend is not None:
                            pf_, pa_t, pa_b = pend
                            w2c = BIG[:, (pf_ % 4) * 2:(pf_ % 4) * 2 + 2, (pf_ // 4) * 512:(pf_ // 4 + 1) * 512]
                            for sub in range(2):
                                for half in range(2):
                                    o_t, o_b = po[sub * 2 + half]
                                    MM(o_t[:, :], pa_t[:, sub * 128:(sub + 1) * 128], w2c[:, half, :],
                                       False, pf_ == FC - 1, [pa_b, b_big[pf_ // 4]], [o_b])
                        pend = (f, a_t, a_b) if f < FC else None
                        if f == 3 and tt + 1 < NT5:
                            A1(tt + 1)
                        if f == 18 and tt + 1 < NT5:
                            A2(tt + 1)
                    for sub in range(2):
                        tok0 = tt * TT5 + sub * 128
                        out_t, out_b, out_s = ot.next()
                        for half in range(2):
                            hs = slice(half * 512, (half + 1) * 512)
                            o_t, o_b = po[sub * 2 + half]
                            if half == 0:
                                P.op("act", lambda e, o=out_t[:, hs], a=o_t[:, :]: e.activation(out=o, in_=a, func=AF.Identity),
                                     reads=[o_b], dwrites=[out_b])
                            else:
                                P.op("dve", lambda e, o=out_t[:, hs], a=o_t[:, :]: e.tensor_copy(out=o, in_=a),
                                     reads=[o_b], dwrites=[out_b])
                        DMA("sp", out_d[tok0:tok0 + 128, :], out_t[:], [out_b], [], out_s)
                        if out_s not in P.final_waits:
                            P.final_waits.append(out_s)

                A1(0)
                A2(0)
                for tt in range(NT5):
                    Bst(tt)
        P.emit(nc, top)
    return nc


def _col(v):
    return np.ascontiguousarray(np.asarray(v, np.float32).reshape(KC, 128).T)


def make_in_maps(inputs, cores):
    x = np.asarray(inputs["x"], np.float32)
    c = np.asarray(inputs["c"], np.float32)
    shared = {
        "w_ada": np.ascontiguousarray(inputs["w_ada"][0], dtype=np.float32),
        "b_ada": np.ascontiguousarray(inputs["b_ada"][0].reshape(1, -1), dtype=np.float32),
        "n1g": _col(inputs["norm1_g"][0]),
        "n2g": _col(inputs["norm2_g"][0]),
        "w_in": np.ascontiguousarray(inputs["w_in"][0], dtype=np.float32),
        "bfor": np.ascontiguousarray(np.asarray(inputs["b_forget"][0], np.float32).reshape(H, 1)),
        "qg": np.ascontiguousarray(np.asarray(inputs["q_norm_g"][0], np.float32).reshape(DH, 1)),
        "kg": np.ascontiguousarray(np.asarray(inputs["k_norm_g"][0], np.float32).reshape(DH, 1)),
        "w_attn_proj": np.ascontiguousarray(inputs["w_attn_proj"][0], dtype=np.float32),
        "cwT": np.ascontiguousarray(np.asarray(inputs["conv_w"][0], np.float32).T.reshape(KC, 128, CK).transpose(1, 0, 2)),
        "cb": _col(inputs["conv_b"][0]),
        "lng": _col(inputs["conv_ln_g"][0]),
        "lnb": _col(inputs["conv_ln_b"][0]),
        "w_conv_proj": np.ascontiguousarray(inputs["w_conv_proj"][0], dtype=np.float32),
        "w_out": np.ascontiguousarray(inputs["w_out"][0], dtype=np.float32),
        "w_mlp1": np.ascontiguousarray(inputs["w_mlp1"][0], dtype=np.float32),
        "w_mlp2": np.ascontiguousarray(inputs["w_mlp2"][0], dtype=np.float32),
        "ident": np.eye(128, dtype=np.float32),
        "maskb": np.where(np.arange(128)[None, :] >= np.arange(128)[:, None], 0.0, -30000.0).astype(np.float32),
    }
    maps = []
    for b in cores:
        m = dict(shared)
        m["x"] = np.ascontiguousarray(x[b])
        m["ccol"] = _col(c[b])
        maps.append(m)
    return maps


def kernel(**inputs):
    nc = build(upto=5, debug=False)
    cores = list(range(8))
    in_maps = make_in_maps(inputs, cores)
    res = run_bass_kernel_spmd(nc, in_maps, core_ids=cores)
    out = np.stack([np.asarray(r["out"], dtype=np.float32) for r in res.results], axis=0)
    return out
```
